# Optimizing a Trainium2 kernel written in Bass

```python
import math
import jax
import jax.numpy as jnp
from jax import lax
import numpy as np

D_MODEL = 1024
BATCH = 32
SEQ = 256
DEPTH = 4
DEC_BATCH = 4
DEC_SEQ = 1024
PAST_LEN = 256

GRID_W = 64
N_GROUPS = 4
GROUP_W = D_MODEL // N_GROUPS
MIX_W = N_GROUPS * GROUP_W
H_A = 4
HD_A = GROUP_W // H_A
CONV_K = 5
DN_CHUNK = 64
H_B = 4
HD_B = GROUP_W // H_B
RW_W_LORA = 64
RW_A_LORA = 64
RW_G_LORA = 128
RW_GN_EPS = 64e-5
H_C = 4
HD_C_V = GROUP_W // H_C
HD_C_QK = HD_C_V // 2
ROPE_BASE = 10000.0
H_D = 4
HD_D = GROUP_W // H_D
NA_ROWS = 8
NA_COLS = 16
D_FF = 2816
N_MOD = 9
Q_BLOCK = 128
NORM_EPS = 1e-6

A_QKV = 3 * GROUP_W
A_COLS = A_QKV + GROUP_W + 4 * H_A
B_COLS = 3 * GROUP_W + 2 * RW_W_LORA + RW_A_LORA + RW_G_LORA
C_COLS = 3 * GROUP_W
D_COLS = 3 * GROUP_W
IN_COLS = A_COLS + B_COLS + C_COLS + D_COLS
IN_SPLITS = (A_COLS, A_COLS + B_COLS, A_COLS + B_COLS + C_COLS)
RW_SPLITS = (GROUP_W, 2 * GROUP_W, 3 * GROUP_W, 3 * GROUP_W + 2 * RW_W_LORA,
             3 * GROUP_W + 2 * RW_W_LORA + RW_A_LORA)

kernel_name = "hybrid_diffusion_trunk_step"


def rms_norm(x, g, eps=NORM_EPS):
    xf = x.astype(jnp.float32)
    y = xf * lax.rsqrt(jnp.mean(xf * xf, axis=-1, keepdims=True) + eps)
    return (y * g.astype(jnp.float32)).astype(x.dtype)


def l2_normalize(x, eps=1e-6):
    xf = x.astype(jnp.float32)
    return xf * lax.rsqrt(jnp.sum(xf * xf, axis=-1, keepdims=True) + eps)


def swiglu(h, w_gate_up, w_down):
    gate, up = jnp.split(h @ w_gate_up, 2, axis=-1)
    return (jax.nn.silu(gate) * up) @ w_down


def modulation(cond, w_mod, b_mod):
    m = jax.nn.silu(cond) @ w_mod + b_mod
    return m.reshape(cond.shape[0], N_MOD, D_MODEL)


def modulate(x, g_pre, shift, scale):
    return rms_norm(x, g_pre) * (1.0 + scale[:, None, :]) + shift[:, None, :]


def residual_add(x, y, g_post, gate, weight):
    return x + weight * gate[:, None, :] * rms_norm(y, g_post)


def depthwise_conv_centred(x, w):
    pad = CONV_K // 2
    return lax.conv_general_dilated(x, w[:, None, :].astype(x.dtype), window_strides=(1,),
                                    padding=[(pad, pad)], dimension_numbers=("NWC", "WIO", "NWC"),
                                    feature_group_count=x.shape[-1])


def token_shift_centred(p, mu_prev, mu_next):
    prev = jnp.pad(p[:, :-1], ((0, 0), (1, 0), (0, 0)))
    nxt = jnp.pad(p[:, 1:], ((0, 0), (0, 1), (0, 0)))
    return p + (prev - p) * mu_prev + (nxt - p) * mu_next


def axial_rope_tables(n_tokens, dim):
    t = jnp.arange(n_tokens)
    pos = jnp.stack([t // GRID_W, t % GRID_W], axis=-1).astype(jnp.float32)
    n_freq = dim // 4
    inv = ROPE_BASE ** (-jnp.arange(n_freq, dtype=jnp.float32) / n_freq)
    ang = pos[:, :, None] * inv
    return jnp.cos(ang), jnp.sin(ang)


def apply_rope_2d(x, cos, sin):
    xs = x.astype(jnp.float32).reshape(x.shape[:-1] + (2, 2, x.shape[-1] // 4))
    x1, x2 = xs[..., 0, :], xs[..., 1, :]
    c, s = cos[:, None], sin[:, None]
    out = jnp.stack([x1 * c - x2 * s, x2 * c + x1 * s], axis=-2)
    return out.reshape(x.shape).astype(x.dtype)


def blocked_softmax_attn(q, k, v):
    b, h, lq, d = q.shape
    scale = d ** -0.5
    qb = jnp.moveaxis(q.reshape(b, h, lq // Q_BLOCK, Q_BLOCK, d), 2, 0)

    def one_block(qi):
        s = jnp.einsum("bhqd,bhkd->bhqk", qi, k).astype(jnp.float32) * scale
        p = jax.nn.softmax(s, axis=-1)
        return jnp.einsum("bhqk,bhkd->bhqd", p.astype(v.dtype), v)

    o = lax.map(one_block, qb)
    return jnp.moveaxis(o, 0, 2).reshape(b, h, lq, v.shape[-1])


def blocked_diff_attn(q, k, v, lam):
    b, h, lq, _, d = q.shape
    scale = d ** -0.5
    qb = jnp.moveaxis(q.reshape(b, h, lq // Q_BLOCK, Q_BLOCK, 2, d), 2, 0)

    def one_block(qi):
        s = jnp.einsum("bhqmd,bhkmd->bhmqk", qi, k).astype(jnp.float32) * scale
        p = jax.nn.softmax(s, axis=-1)
        w = p[:, :, 0] - lam * p[:, :, 1]
        return jnp.einsum("bhqk,bhkv->bhqv", w.astype(v.dtype), v)

    o = lax.map(one_block, qb)
    return jnp.moveaxis(o, 0, 2).reshape(b, h, lq, v.shape[-1])


def neighborhood_attn(q, k, v, ctx_k, ctx_v, rel_bias):
    b, h, n, d = q.shape
    rows = n // GRID_W
    kr = min(NA_ROWS, rows)
    kc = NA_COLS
    scale = d ** -0.5
    r_idx = jnp.arange(rows)
    c_idx = jnp.arange(GRID_W)
    r_start = jnp.clip(r_idx - kr // 2, 0, rows - kr)
    key_rows = r_start[:, None] + jnp.arange(kr)[None, :]
    c_start = jnp.clip(c_idx - kc // 2, 0, GRID_W - kc)
    in_win = (c_idx[None, :] >= c_start[:, None]) & (c_idx[None, :] < c_start[:, None] + kc)
    dr = key_rows - r_idx[:, None] + NA_ROWS - 1
    dc = jnp.clip(c_idx[None, :] - c_idx[:, None] + NA_COLS - 1, 0, 2 * NA_COLS - 2)
    bias = rel_bias.astype(jnp.float32)[:, dr][:, :, :, dc]
    bias = jnp.where(in_win[None, None, None], bias, -jnp.inf).transpose(0, 1, 3, 2, 4)
    qg = q.reshape(b, h, rows, GRID_W, d)
    kg = k.reshape(b, h, rows, GRID_W, d)[:, :, key_rows]
    vg = v.reshape(b, h, rows, GRID_W, d)[:, :, key_rows]
    s_loc = jnp.einsum("bhrqd,bhrjkd->bhrqjk", qg, kg).astype(jnp.float32) * scale + bias
    s_ctx = jnp.einsum("bhrqd,bhcd->bhrqc", qg, ctx_k).astype(jnp.float32) * scale
    n_loc = kr * GRID_W
    s = jnp.concatenate([s_loc.reshape(b, h, rows, GRID_W, n_loc), s_ctx], axis=-1)
    pr = jax.nn.softmax(s, axis=-1).astype(v.dtype)
    p_loc = pr[..., :n_loc].reshape(b, h, rows, GRID_W, kr, GRID_W)
    o = (jnp.einsum("bhrqjk,bhrjkd->bhrqd", p_loc, vg)
         + jnp.einsum("bhrqc,bhcd->bhrqd", pr[..., n_loc:], ctx_v))
    return o.reshape(b, h, n, d)


def gated_delta_chunked(q, k, v, log_a, beta, s0):
    lead = q.shape[:3]
    L = q.shape[3]
    dv = v.shape[-1]
    C = DN_CHUNK
    n = L // C
    q, k, v = (t.reshape(lead + (n, C, t.shape[-1])) for t in (q, k, v))
    g = jnp.cumsum(log_a.reshape(lead + (n, C)), axis=-1)
    beta = beta.reshape(lead + (n, C))
    tri = jnp.tril(jnp.ones((C, C), bool))
    strict = jnp.tril(jnp.ones((C, C), bool), -1)
    decay = jnp.exp(jnp.where(tri, g[..., :, None] - g[..., None, :], -jnp.inf))
    kb = k * beta[..., None]
    a_mat = jnp.where(strict, jnp.einsum("...id,...jd->...ij", kb, k) * decay, 0.0)
    rhs = jnp.concatenate([v * beta[..., None], kb * jnp.exp(g)[..., None]], axis=-1)
    sol = lax.linalg.triangular_solve(jnp.eye(C, dtype=q.dtype) + a_mat, rhs, left_side=True,
                                      lower=True, unit_diagonal=True)
    u, w = sol[..., :dv], sol[..., dv:]
    qk = jnp.einsum("...id,...jd->...ij", q, k) * decay

    def step(S, inp):
        q_c, k_c, u_c, w_c, g_c, qk_c = inp
        v_new = u_c - jnp.einsum("...cd,...dv->...cv", w_c, S)
        o_c = (jnp.einsum("...cd,...dv->...cv", q_c * jnp.exp(g_c)[..., None], S)
               + jnp.einsum("...ij,...jv->...iv", qk_c, v_new))
        g_last = g_c[..., -1]
        k_dec = k_c * jnp.exp(g_last[..., None] - g_c)[..., None]
        S = S * jnp.exp(g_last)[..., None, None] + jnp.einsum("...cd,...cv->...dv", k_dec, v_new)
        return S, o_c

    xs = tuple(jnp.moveaxis(t, 3, 0) for t in (q, k, u, w, g, qk))
    s_final, o = lax.scan(step, s0, xs)
    o = jnp.moveaxis(o, 0, 3).reshape(lead + (L, dv))
    return o, s_final


def deltanet_mixer(p, conv_w, a_log, dt_bias, norm_w, s0_fwd, s0_bwd):
    b, L, _ = p.shape
    f32 = jnp.float32
    qkv, gate, ab = jnp.split(p, [A_QKV, A_QKV + GROUP_W], axis=-1)
    qkv = jax.nn.silu(depthwise_conv_centred(qkv, conv_w)).astype(f32)
    q, k, v = (t.reshape(b, L, H_A, HD_A).transpose(0, 2, 1, 3) for t in jnp.split(qkv, 3, axis=-1))
    q = l2_normalize(q) * HD_A ** -0.5
    k = l2_normalize(k)
    ab = ab.astype(f32).reshape(b, L, 2, 2, H_A)
    log_a = -jnp.exp(a_log.astype(f32)) * jax.nn.softplus(ab[:, :, 0] + dt_bias.astype(f32))
    beta = jax.nn.sigmoid(ab[:, :, 1])
    log_a = log_a.transpose(2, 0, 3, 1)
    beta = beta.transpose(2, 0, 3, 1)

    def both_dirs(t):
        return jnp.stack([t, t[..., ::-1, :]])

    log_a = jnp.stack([log_a[0], log_a[1][..., ::-1]])
    beta = jnp.stack([beta[0], beta[1][..., ::-1]])
    s0 = jnp.stack([s0_fwd, s0_bwd]).astype(f32)
    o, s_fin = gated_delta_chunked(both_dirs(q), both_dirs(k), both_dirs(v), log_a, beta, s0)
    o = (o[0] + o[1][..., ::-1, :]).transpose(0, 2, 1, 3)
    o = rms_norm(o, norm_w) * jax.nn.silu(gate.astype(f32).reshape(b, L, H_A, HD_A))
    return o.reshape(b, L, GROUP_W).astype(p.dtype), s_fin[0].astype(p.dtype), s_fin[1].astype(p.dtype)


def rwkv7_scan(r, w, k, v, kk, a, s0):
    def step(S, inp):
        r_t, w_t, k_t, v_t, kk_t, a_t = inp
        sa = jnp.einsum("zbhvk,zbhk->zbhv", S, -kk_t)
        S = (S * w_t[..., None, :] + sa[..., :, None] * (kk_t * a_t)[..., None, :]
             + v_t[..., :, None] * k_t[..., None, :])
        return S, jnp.einsum("zbhvk,zbhk->zbhv", S, r_t)

    s_fin, y = lax.scan(step, s0, (r, w, k, v, kk, a))
    return y, s_fin


def rwkv7_mixer(p, mu, w0, w2, a0, a2, g2, k_k, k_a, r_k, ln, s0_fwd, s0_bwd):
    b, L, _ = p.shape
    f32 = jnp.float32
    out_dtype = p.dtype
    p = token_shift_centred(p, mu[0], mu[1]).astype(f32)
    r, k, v, wl, al, gl = jnp.split(p, RW_SPLITS, axis=-1)
    z = w0.astype(f32) + jnp.einsum("blzr,zrc->blzc", jnp.tanh(wl).reshape(b, L, 2, RW_W_LORA), w2.astype(f32))
    decay = jnp.exp(-jnp.exp(-jax.nn.softplus(-z) - 0.5))
    a = jax.nn.sigmoid(a0.astype(f32) + al @ a2.astype(f32))
    g = jax.nn.sigmoid(gl) @ g2.astype(f32)

    def heads(t):
        return t.reshape(t.shape[:-1] + (H_B, HD_B))

    kk = l2_normalize(heads(k * k_k.astype(f32)))
    k = k * (1.0 + (a - 1.0) * k_a.astype(f32))
    r_h, k_h, v_h, a_h = heads(r), heads(k), heads(v), heads(a)

    def both_dirs(t):
        t = jnp.moveaxis(t, 1, 0)
        return jnp.stack([t, t[::-1]], axis=1)

    w_t = jnp.moveaxis(heads(decay), 1, 0)
    w_t = jnp.stack([w_t[:, :, 0], w_t[::-1, :, 1]], axis=1)
    s0 = jnp.stack([s0_fwd, s0_bwd]).astype(f32)
    y, s_fin = rwkv7_scan(both_dirs(r_h), w_t, both_dirs(k_h), both_dirs(v_h), both_dirs(kk),
                          both_dirs(a_h), s0)
    y = jnp.moveaxis(y[:, 0] + y[::-1, 1], 0, 1)
    mean = jnp.mean(y, axis=-1, keepdims=True)
    var = jnp.mean(jnp.square(y - mean), axis=-1, keepdims=True)
    y = ((y - mean) * lax.rsqrt(var + RW_GN_EPS)).reshape(b, L, GROUP_W) * ln[0] + ln[1]
    bonus = jnp.sum(r_h * k_h * r_k.astype(f32), axis=-1, keepdims=True) * v_h
    out = (y + bonus.reshape(b, L, GROUP_W)) * g
    return out.astype(out_dtype), s_fin[0].astype(out_dtype), s_fin[1].astype(out_dtype)


def diff_attn_mixer(p, lam_vecs, norm_w, lam_init, rope, ctx_k, ctx_v):
    b, L, _ = p.shape
    q, k, v = jnp.split(p, 3, axis=-1)
    q = q.reshape(b, L, H_C, 2, HD_C_QK).transpose(0, 2, 1, 3, 4)
    k = k.reshape(b, L, H_C, 2, HD_C_QK).transpose(0, 2, 1, 3, 4)
    v = v.reshape(b, L, H_C, HD_C_V).transpose(0, 2, 1, 3)
    if rope is not None:
        q = apply_rope_2d(q, rope[0], rope[1])
        k = apply_rope_2d(k, rope[0], rope[1])
    lv = lam_vecs.astype(jnp.float32)
    lam = jnp.exp(jnp.sum(lv[0] * lv[1])) - jnp.exp(jnp.sum(lv[2] * lv[3])) + lam_init
    if ctx_k is None:
        keys, vals = k, v
    else:
        keys = jnp.concatenate([ctx_k, k], axis=2)
        vals = jnp.concatenate([ctx_v, v], axis=2)
    o = rms_norm(blocked_diff_attn(q, keys, vals, lam), norm_w) * (1.0 - lam_init)
    return o.transpose(0, 2, 1, 3).reshape(b, L, GROUP_W).astype(p.dtype), k, v


def na_mixer(p, rel_bias, ctx_k, ctx_v):
    b, L, _ = p.shape
    q, k, v = (t.reshape(b, L, H_D, HD_D).transpose(0, 2, 1, 3) for t in jnp.split(p, 3, axis=-1))
    if ctx_k is None:
        o = blocked_softmax_attn(q, k, v)
    else:
        o = neighborhood_attn(q, k, v, ctx_k, ctx_v, rel_bias)
    return o.transpose(0, 2, 1, 3).reshape(b, L, GROUP_W).astype(p.dtype), k, v


def token_mixing(h, lp, layer_idx, cache):
    b, L, _ = h.shape
    pa, pb, pc, pd = jnp.split(h @ lp["w_in"], IN_SPLITS, axis=-1)
    lam_init = 0.8 - 0.6 * math.exp(-0.3 * layer_idx)
    if cache is None:
        zero_a = jnp.zeros((b, H_A, HD_A, HD_A), h.dtype)
        zero_b = jnp.zeros((b, H_B, HD_B, HD_B), h.dtype)
        dn_f, dn_b, rw_f, rw_b = zero_a, zero_a, zero_b, zero_b
        diff_k = diff_v = na_k = na_v = None
        rope = None
    else:
        diff_k, diff_v, na_k, na_v, dn_f, dn_b, rw_f, rw_b = cache
        rope = axial_rope_tables(L, HD_C_QK)
    oa, dn_f1, dn_b1 = deltanet_mixer(pa, lp["dn_conv"], lp["dn_a_log"], lp["dn_dt_bias"], lp["dn_norm"], dn_f, dn_b)
    ob, rw_f1, rw_b1 = rwkv7_mixer(pb, lp["rw_mu"], lp["rw_w0"], lp["rw_w2"], lp["rw_a0"], lp["rw_a2"],
                                   lp["rw_g2"], lp["rw_k_k"], lp["rw_k_a"], lp["rw_r_k"], lp["rw_ln"], rw_f, rw_b)
    oc, kc, vc = diff_attn_mixer(pc, lp["da_lambda"], lp["da_norm"], lam_init, rope, diff_k, diff_v)
    od, kd, vd = na_mixer(pd, lp["na_bias"], na_k, na_v)
    y = jnp.concatenate([oa, ob, oc, od], axis=-1) @ lp["w_out"]
    if cache is None:
        return y, (kc, vc, kd, vd, dn_f1, dn_b1, rw_f1, rw_b1)
    return y, None


def trunk_layer(x, mod, lp, layer_idx, cache):
    h = modulate(x, lp["norm_pre"][0], mod[:, 0], mod[:, 1])
    x = residual_add(x, swiglu(h, lp["ffn_in"][0], lp["ffn_out"][0]), lp["norm_post"][0], mod[:, 2], 0.5)
    h = modulate(x, lp["norm_pre"][1], mod[:, 3], mod[:, 4])
    y, new_cache = token_mixing(h, lp, layer_idx, cache)
    x = residual_add(x, y, lp["norm_post"][1], mod[:, 5], 1.0)
    h = modulate(x, lp["norm_pre"][2], mod[:, 6], mod[:, 7])
    x = residual_add(x, swiglu(h, lp["ffn_in"][1], lp["ffn_out"][1]), lp["norm_post"][2], mod[:, 8], 0.5)
    return x, new_cache


def setup_inputs(seed: int = 0) -> dict:
    key = jax.random.key(seed)
    keys = iter(jax.random.split(key, 48))

    def nrm(shape, std):
        return std * jax.random.normal(next(keys), shape, jnp.float32)

    def unif(shape, lo, hi):
        return jax.random.uniform(next(keys), shape, jnp.float32, lo, hi)

    dt = jnp.exp(unif((DEPTH, 2, H_A), math.log(1e-3), math.log(1e-1)))
    return {
        "x_prompt": nrm((BATCH, SEQ, D_MODEL), 1.0),
        "x_sample": nrm((DEC_BATCH, DEC_SEQ, D_MODEL), 1.0),
        "cache_diff_k": nrm((DEC_BATCH, DEPTH, H_C, PAST_LEN, 2, HD_C_QK), 1.0),
        "cache_diff_v": nrm((DEC_BATCH, DEPTH, H_C, PAST_LEN, HD_C_V), 1.0),
        "cache_na_k": nrm((DEC_BATCH, DEPTH, H_D, PAST_LEN, HD_D), 1.0),
        "cache_na_v": nrm((DEC_BATCH, DEPTH, H_D, PAST_LEN, HD_D), 1.0),
        "state_dn_fwd": nrm((DEC_BATCH, DEPTH, H_A, HD_A, HD_A), 0.3),
        "state_dn_bwd": nrm((DEC_BATCH, DEPTH, H_A, HD_A, HD_A), 0.3),
        "state_rwkv_fwd": nrm((DEC_BATCH, DEPTH, H_B, HD_B, HD_B), 1.0),
        "state_rwkv_bwd": nrm((DEC_BATCH, DEPTH, H_B, HD_B, HD_B), 1.0),
        "c": nrm((DEC_BATCH, D_MODEL), 1.0),
        "c_ctx": nrm((D_MODEL,), 1.0),
        "w_mod": nrm((DEPTH, D_MODEL, N_MOD * D_MODEL), 0.5 * D_MODEL ** -0.5),
        "b_mod": nrm((DEPTH, N_MOD * D_MODEL), 0.02),
        "norm_pre": 1.0 + nrm((DEPTH, 3, D_MODEL), 0.02),
        "norm_post": 1.0 + nrm((DEPTH, 3, D_MODEL), 0.02),
        "ffn_in": nrm((DEPTH, 2, D_MODEL, 2 * D_FF), D_MODEL ** -0.5),
        "ffn_out": nrm((DEPTH, 2, D_FF, D_MODEL), D_FF ** -0.5),
        "w_in": nrm((DEPTH, D_MODEL, IN_COLS), D_MODEL ** -0.5),
        "w_out": nrm((DEPTH, MIX_W, D_MODEL), MIX_W ** -0.5),
        "dn_conv": nrm((DEPTH, CONV_K, A_QKV), CONV_K ** -0.5),
        "dn_a_log": jnp.log(unif((DEPTH, 2, H_A), 1.0, 16.0)),
        "dn_dt_bias": dt + jnp.log(-jnp.expm1(-dt)),
        "dn_norm": 1.0 + nrm((DEPTH, HD_A), 0.02),
        "rw_mu": unif((DEPTH, 2, B_COLS), 0.0, 0.5),
        "rw_w0": unif((DEPTH, 2, GROUP_W), -6.0, 1.0),
        "rw_w2": nrm((DEPTH, 2, RW_W_LORA, GROUP_W), 0.1),
        "rw_a0": nrm((DEPTH, GROUP_W), 0.5),
        "rw_a2": nrm((DEPTH, RW_A_LORA, GROUP_W), 0.5 * RW_A_LORA ** -0.5),
        "rw_g2": nrm((DEPTH, RW_G_LORA, GROUP_W), RW_G_LORA ** -0.5),
        "rw_k_k": 0.85 + nrm((DEPTH, GROUP_W), 0.05),
        "rw_k_a": 1.0 + nrm((DEPTH, GROUP_W), 0.05),
        "rw_r_k": nrm((DEPTH, H_B, HD_B), 0.1),
        "rw_ln": jnp.stack([1.0 + nrm((DEPTH, GROUP_W), 0.02), nrm((DEPTH, GROUP_W), 0.02)], axis=1),
        "da_lambda": nrm((DEPTH, 4, HD_C_QK), 0.1),
        "da_norm": 1.0 + nrm((DEPTH, HD_C_V), 0.02),
        "na_bias": nrm((DEPTH, H_D, 2 * NA_ROWS - 1, 2 * NA_COLS - 1), 0.1),
    }


def reference(x_prompt, x_sample, cache_diff_k, cache_diff_v, cache_na_k, cache_na_v,
              state_dn_fwd, state_dn_bwd, state_rwkv_fwd, state_rwkv_bwd, c, c_ctx,
              w_mod, b_mod, norm_pre, norm_post, ffn_in, ffn_out, w_in, w_out,
              dn_conv, dn_a_log, dn_dt_bias, dn_norm,
              rw_mu, rw_w0, rw_w2, rw_a0, rw_a2, rw_g2, rw_k_k, rw_k_a, rw_r_k, rw_ln,
              da_lambda, da_norm, na_bias):
    xp, xs = x_prompt, x_sample
    layer_states = []
    for l in range(DEPTH):
        lp = {
            "norm_pre": norm_pre[l], "norm_post": norm_post[l],
            "ffn_in": ffn_in[l], "ffn_out": ffn_out[l],
            "w_in": w_in[l], "w_out": w_out[l],
            "dn_conv": dn_conv[l], "dn_a_log": dn_a_log[l], "dn_dt_bias": dn_dt_bias[l], "dn_norm": dn_norm[l],
            "rw_mu": rw_mu[l], "rw_w0": rw_w0[l], "rw_w2": rw_w2[l], "rw_a0": rw_a0[l], "rw_a2": rw_a2[l],
            "rw_g2": rw_g2[l], "rw_k_k": rw_k_k[l], "rw_k_a": rw_k_a[l], "rw_r_k": rw_r_k[l], "rw_ln": rw_ln[l],
            "da_lambda": da_lambda[l], "da_norm": da_norm[l], "na_bias": na_bias[l],
        }
        mod_ctx = modulation(c_ctx[None, :], w_mod[l], b_mod[l])
        mod_lat = modulation(c, w_mod[l], b_mod[l])
        xp, st = trunk_layer(xp, mod_ctx, lp, l, None)
        layer_states.append(st)
        cache_l = (cache_diff_k[:, l], cache_diff_v[:, l], cache_na_k[:, l], cache_na_v[:, l],
                   state_dn_fwd[:, l], state_dn_bwd[:, l], state_rwkv_fwd[:, l], state_rwkv_bwd[:, l])
        xs, _ = trunk_layer(xs, mod_lat, lp, l, cache_l)
    new_diff_k = jnp.stack([s[0] for s in layer_states], axis=1)
    new_diff_v = jnp.stack([s[1] for s in layer_states], axis=1)
    new_na_k = jnp.stack([s[2] for s in layer_states], axis=1)
    new_na_v = jnp.stack([s[3] for s in layer_states], axis=1)
    new_dn_fwd = jnp.stack([s[4] for s in layer_states], axis=1)
    new_dn_bwd = jnp.stack([s[5] for s in layer_states], axis=1)
    new_rwkv_fwd = jnp.stack([s[6] for s in layer_states], axis=1)
    new_rwkv_bwd = jnp.stack([s[7] for s in layer_states], axis=1)
    return (xp, xs, new_diff_k, new_diff_v, new_na_k, new_na_v, new_dn_fwd, new_dn_bwd, new_rwkv_fwd, new_rwkv_bwd)
```

```python
import math
from contextlib import ExitStack
import numpy as np
import concourse.bass as bass
import concourse.mybir as mybir
from concourse.bass_utils import run_bass_kernel_spmd

F32 = mybir.dt.float32
BF16 = mybir.dt.bfloat16
AF = mybir.ActivationFunctionType
ALU = mybir.AluOpType
AX = mybir.AxisListType

EPOCH = 30000
NDMASEM = 20
NSLOT = 7
PE_SKIP_OWN = False
MMQ = True

L_ = 4
A0, B0, C0, D0 = 0, 1040, 2128, 2896
INC = 3664
NV = 114
NEGBIG = -60000.0
WC = 0.6065306597126334


class Tile:
    def __init__(self, h, name, track=True):
        self.h = h
        self.name = name
        self.wl = []
        self.rl = []
        self.track = track
        self.scoped = False
        self.psum = False
        self.born = 0

    def __getitem__(self, idx):
        return AV(self.h[idx], self)

    def v(self, fn):
        return AV(fn(self.h), self)


class AV:
    def __init__(self, ap, t):
        self.ap = ap
        self.t = t


class Eng:
    def __init__(self, K, name, e):
        self.K = K
        self.name = name
        self.e = e
        self.sems = []
        self.count = 0
        self.nidx = 0
        self.cnt_at = {}
        self.seen = {}
        self.pending = []
        self.wq = []

    def next_inc(self):
        k = self.count // EPOCH
        while len(self.sems) <= k:
            self.sems.append(self.K.nc.alloc_semaphore(f"s_{self.name}_{len(self.sems)}"))
        return self.sems[k], self.count % EPOCH + 1

    def issue(self, ins):
        idx = self.nidx
        self.nidx += 1
        need = self.K.needed
        if need is None or idx in need[self.name]:
            sem, val = self.next_inc()
            ins.then_inc(sem, 1)
            self.count += 1
            self.cnt_at[idx] = (sem, val)
        return ("e", self, idx)

    def last_tok(self):
        if self.nidx == 0:
            return None
        return ("e", self, self.nidx - 1)

    def wait(self, tok):
        if tok is None:
            return
        if tok[0] == "e":
            _, T, idx = tok
            if self.seen.get(T.name, -1) >= idx:
                return
            self.seen[T.name] = idx
            self.K.needed_rec[T.name].add(idx)
            self.wq.append(T.cnt_at[idx])
            return
        sem, val = tok
        key = id(sem)
        if self.seen.get(key, 0) >= val:
            return
        self.wq.append((sem, val))
        self.seen[key] = val

    def flush(self, keep_last=False):
        wq, self.wq = self.wq, []
        last = None
        if keep_last and wq:
            last = wq.pop()
        for sem, val in wq:
            self.e.wait_ge(sem, val)
            self.K.nwait += 1
        return last


WHOLE = (0, 1 << 30, 0, 1 << 30)
DT_BYTES = {F32: 4, BF16: 2}


def ovl(a, b):
    return a[0] < b[1] and b[0] < a[1] and a[2] < b[3] and b[2] < a[3]


def inside(a, b):
    return a[0] >= b[0] and a[1] <= b[1] and a[2] >= b[2] and a[3] <= b[3]


def merge_entries(ents):
    best = {}
    for bx, tk in ents:
        kk = tok_key(tk)
        if kk not in best:
            best[kk] = (bx, tk)
        else:
            b0, t0 = best[kk]
            ub = (min(b0[0], bx[0]), max(b0[1], bx[1]), min(b0[2], bx[2]), max(b0[3], bx[3]))
            best[kk] = (ub, tk if tok_ord(tk) > tok_ord(t0) else t0)
    return list(best.values())


def tok_key(tk):
    return (tk[1].name,) if tk[0] == "e" else (id(tk[0]),)


def tok_ord(tk):
    return tk[2] if tk[0] == "e" else tk[1]


def compress(toks):
    best = {}
    for tk in toks:
        kk = tok_key(tk)
        if kk not in best or tok_ord(best[kk]) < tok_ord(tk):
            best[kk] = tk
    return list(best.values())


class K:
    def __init__(self, needed=None):
        self.nc = bass.Bass("TRN2", target_bir_lowering=False)
        nc = self.nc
        self.needed = needed
        self.nwait = 0
        self.pe = Eng(self, "pe", nc.tensor)
        self.dve = Eng(self, "dve", nc.vector)
        self.act = Eng(self, "act", nc.scalar)
        self.pool = Eng(self, "pool", nc.gpsimd)
        self.sp = Eng(self, "sp", nc.sync)
        self.engs = [self.pe, self.dve, self.act, self.pool, self.sp]
        self.needed_rec = {e.name: set() for e in self.engs}
        self.dsems = {}
        for e in (self.pool, self.sp):
            self.dsems[e.name] = dict(sems=[nc.alloc_semaphore(f"dma_{e.name}{i}") for i in range(NDMASEM)],
                                      use=[0] * NDMASEM, nxt=0)
        self.out_toks = []
        self.ntile = 0
        self.ninstr = 0
        self.stamp = 0

    def sb(self, shape, dt=F32, name=None):
        self.ntile += 1
        name = f"{name or 't'}_{self.ntile}"
        return Tile(self.nc.alloc_sbuf_tensor(name, list(shape), dt), name)

    def ps(self, shape, dt=F32, name=None):
        self.ntile += 1
        name = f"{name or 'p'}_{self.ntile}"
        t = Tile(self.nc.alloc_psum_tensor(name, list(shape), dt), name)
        t.psum = True
        return t

    def dram(self, name, shape, dt, kind):
        return Tile(self.nc.dram_tensor(name, list(shape), dt, kind=kind), name, track=False)

    @staticmethod
    def box(a):
        if a.t.psum:
            return WHOLE
        ap = a.ap
        pairs = ap.ap
        esz = DT_BYTES[ap.dtype]
        row = pairs[0][0] * esz
        off = int(ap.offset) * esz
        if row <= 0:
            return WHOLE
        p0, f0 = off // row, off % row
        ext = esz
        for st, cnt in pairs[1:]:
            if st < 0:
                return WHOLE
            ext += (cnt - 1) * st * esz
        return (p0, p0 + pairs[0][1], f0, f0 + ext)

    def _deps(self, eng, reads, writes):
        skip = (eng is self.pe and PE_SKIP_OWN)

        def w(tk):
            if skip and tk[0] == "e" and tk[1] is eng:
                return
            eng.wait(tk)
        if eng.pending:
            born = 0
            for a in list(reads) + list(writes):
                if a is not None and a.t.scoped and a.t.born > born:
                    born = a.t.born
            if born:
                keep = []
                for st_, toks in eng.pending:
                    if st_ <= born:
                        for tk in toks:
                            eng.wait(tk)
                    else:
                        keep.append((st_, toks))
                eng.pending = keep
        for a in reads:
            if a is None or not a.t.track:
                continue
            bx = self.box(a)
            for b, tk in a.t.wl:
                if ovl(b, bx):
                    w(tk)
            if a.t.psum:
                for b, tk in a.t.rl:
                    if tk[0] == "e" and tk[1] is not eng:
                        w(tk)
        for a in writes:
            if not a.t.track:
                continue
            bx = self.box(a)
            for b, tk in a.t.wl:
                if ovl(b, bx):
                    w(tk)
            for b, tk in a.t.rl:
                if ovl(b, bx):
                    w(tk)

    def _commit(self, tok, reads, writes):
        for a in reads:
            if a is None or not a.t.track:
                continue
            a.t.rl.append((self.box(a), tok))
            if len(a.t.rl) > 40:
                a.t.rl = merge_entries(a.t.rl)
        for a in writes:
            if not a.t.track:
                continue
            bx = self.box(a)
            t = a.t
            t.wl = [e for e in t.wl if not inside(e[0], bx)] + [(bx, tok)]
            t.rl = [e for e in t.rl if not inside(e[0], bx)]
            if len(t.wl) > 40:
                t.wl = merge_entries(t.wl)

    def op(self, eng, fn, reads, writes):
        self._deps(eng, reads, writes)
        last = eng.flush(True)
        ins = fn()
        if last is not None:
            ins._wait_ge(last[0], last[1])
        self.ninstr += 1
        tok = eng.issue(ins)
        self._commit(tok, reads, writes)
        return tok

    def dma(self, out, in_, eng=None, is_out=False):
        eng = eng or self.sp
        d = self.dsems[eng.name]
        j = d["nxt"]
        d["nxt"] = (j + 1) % NDMASEM
        sem = d["sems"][j]
        if d["use"][j] > 0:
            eng.wait((sem, 16 * d["use"][j]))
        self._deps(eng, [in_], [out])
        last = eng.flush(True)
        ins = eng.e.dma_start(out=out.ap, in_=in_.ap)
        if last is not None:
            ins._wait_ge(last[0], last[1])
        self.ninstr += 1
        d["use"][j] += 1
        tok = (sem, 16 * d["use"][j])
        ins.then_inc(sem, 16)
        self._commit(tok, [in_], [out])
        if is_out:
            self.out_toks.append(tok)
        return tok

    def finish(self):
        for tk in self.out_toks:
            self.sp.wait(tk)
        for e in self.engs:
            if e is not self.sp:
                self.sp.wait(e.last_tok())
        for nm, d in self.dsems.items():
            for j in range(NDMASEM):
                if d["use"][j] > 0:
                    self.sp.wait((d["sems"][j], 16 * d["use"][j]))
        self.sp.flush(False)

    def mmg(self, out, pairs):
        n = len(pairs)
        rd = []
        for l, r in pairs:
            rd += [l, r]
        self._deps(self.pe, rd, [out])
        last = self.pe.flush(True)
        ins = None
        for i, (l, r) in enumerate(pairs):
            ins = self.nc.tensor.matmul(out.ap, l.ap, r.ap, start=(i == 0), stop=(i == n - 1))
            if i == 0 and last is not None:
                ins._wait_ge(last[0], last[1])
            self.ninstr += 1
        tok = self.pe.issue(ins)
        self._commit(tok, rd, [out])
        return tok

    def mm(self, out, lhsT, rhs):
        return self.mmg(out, [(lhsT, rhs)])

    def mmq(self, items):
        bases = sorted(set(l.ap.base_partition() for _, l, _ in items))
        if len(bases) > 1:
            tok = None
            for bse in bases:
                tok = self.mmq([it for it in items if it[1].ap.base_partition() == bse])
            return tok
        rd, wr = [], []
        for o, l, r in items:
            rd += [l, r]
            wr.append(o)
        self._deps(self.pe, rd, wr)
        last = self.pe.flush(True)
        ins = None
        for i, (o, l, r) in enumerate(items):
            ins = self.nc.tensor.matmul(o.ap, l.ap, r.ap, start=True, stop=True)
            if i == 0 and last is not None:
                ins._wait_ge(last[0], last[1])
            self.ninstr += 1
        tok = self.pe.issue(ins)
        self._commit(tok, rd, wr)
        return tok

    def mm1(self, out, lhsT, rhs, start, stop):
        pe = self.pe
        self._deps(pe, [lhsT, rhs], [out] if start else [])
        last = pe.flush(True)
        ins = self.nc.tensor.matmul(out.ap, lhsT.ap, rhs.ap, start=start, stop=stop)
        if last is not None:
            ins._wait_ge(last[0], last[1])
        self.ninstr += 1
        tok = pe.issue(ins)
        self._commit(tok, [lhsT, rhs], [out])
        return tok

    def trq(self, items):
        rd, wr = [], []
        for o, i_, idn in items:
            rd += [i_, idn]
            wr.append(o)
        self._deps(self.pe, rd, wr)
        last = self.pe.flush(True)
        ins = None
        for i, (o, i_, idn) in enumerate(items):
            ins = self.nc.tensor.transpose(o.ap, i_.ap, idn.ap)
            if i == 0 and last is not None:
                ins._wait_ge(last[0], last[1])
            self.ninstr += 1
        tok = self.pe.issue(ins)
        self._commit(tok, rd, wr)
        return tok

    def transpose(self, out, in_, ident):
        return self.op(self.pe, lambda: self.nc.tensor.transpose(out.ap, in_.ap, ident.ap), [in_, ident], [out])

    def activation(self, out, in_, func, bias=None, scale=None, accum_out=None):
        eng = self.act
        kw = {}
        rd = [in_]
        if bias is not None:
            if isinstance(bias, AV):
                kw["bias"] = bias.ap
                rd.append(bias)
            else:
                kw["bias"] = float(bias)
        if scale is not None:
            if isinstance(scale, AV):
                kw["scale"] = scale.ap
                rd.append(scale)
            else:
                kw["scale"] = float(scale)
        wr = [out]
        if accum_out is not None:
            kw["accum_out"] = accum_out.ap
            wr.append(accum_out)
        return self.op(eng, lambda: eng.e.activation(out.ap, in_.ap, func, **kw), rd, wr)

    def tt(self, out, a, b, op, eng=None):
        eng = eng or self.dve
        return self.op(eng, lambda: eng.e.tensor_tensor(out.ap, a.ap, b.ap, op), [a, b], [out])

    def ts(self, out, a, s1, op0, s2=None, op1=None, eng=None):
        eng = eng or self.dve
        rd = [a]
        s1v = s1.ap if isinstance(s1, AV) else float(s1)
        s2v = s2.ap if isinstance(s2, AV) else (None if s2 is None else float(s2))
        if isinstance(s1, AV):
            rd.append(s1)
        if isinstance(s2, AV):
            rd.append(s2)
        if op1 is None:
            return self.op(eng, lambda: eng.e.tensor_scalar(out.ap, a.ap, s1v, None, op0), rd, [out])
        return self.op(eng, lambda: eng.e.tensor_scalar(out.ap, a.ap, s1v, s2v, op0, op1), rd, [out])

    def stt(self, out, a, s, b, op0, op1, eng=None):
        eng = eng or self.dve
        rd = [a, b]
        sv = s.ap if isinstance(s, AV) else float(s)
        if isinstance(s, AV):
            rd.append(s)
        return self.op(eng, lambda: eng.e.scalar_tensor_tensor(out.ap, a.ap, sv, b.ap, op0, op1), rd, [out])

    def copy(self, out, in_, eng=None):
        eng = eng or self.dve
        if eng is self.act:
            return self.op(eng, lambda: eng.e.copy(out.ap, in_.ap), [in_], [out])
        return self.op(eng, lambda: eng.e.tensor_copy(out.ap, in_.ap), [in_], [out])

    def memset(self, out, val, eng=None):
        eng = eng or self.dve
        return self.op(eng, lambda: eng.e.memset(out.ap, val), [], [out])

    def recip(self, out, in_):
        return self.op(self.dve, lambda: self.nc.vector.reciprocal(out.ap, in_.ap), [in_], [out])

    def reduce(self, out, in_, op, axis=AX.X):
        return self.op(self.dve, lambda: self.nc.vector.tensor_reduce(out.ap, in_.ap, axis, op), [in_], [out])

    def scan(self, out, d0, d1, init, op0, op1):
        return self.op(self.dve, lambda: self.nc.vector.tensor_tensor_scan(out.ap, d0.ap, d1.ap, init, op0, op1),
                       [d0, d1], [out])


class Scope:
    def __init__(self, k):
        self.k = k
        self.es = ExitStack()
        self.tiles = []

    def sb(self, shape, dt=F32, name=None):
        k = self.k
        k.ntile += 1
        name = f"{name or 's'}_{k.ntile}"
        h = self.es.enter_context(k.nc.sbuf_tensor(name, list(shape), dt))
        t = Tile(h, name)
        t.scoped = True
        k.stamp += 1
        t.born = k.stamp
        self.tiles.append(t)
        return t

    def close(self):
        k = self.k
        toks = []
        for t in self.tiles:
            toks += [tk for _, tk in t.wl] + [tk for _, tk in t.rl]
        toks = compress(toks)
        k.stamp += 1
        for e in k.engs:
            e.pending.append((k.stamp, toks))
            if len(e.pending) > 6:
                st0 = e.pending[0][0]
                e.pending = [(st0, compress(e.pending[0][1] + e.pending[1][1]))] + e.pending[2:]
        self.es.close()


def win_chunk_cols():
    ch = []
    for i in range(8):
        ch.append(A0 + i * 128 + np.arange(128))
    ch.append(A0 + 1024 + np.arange(16))
    for i in range(6):
        ch.append(B0 + i * 128 + np.arange(128))
    ch.append(B0 + 768 + np.arange(128))
    ch.append(B0 + 896 + np.arange(64))
    ch.append(B0 + 960 + np.arange(128))
    for i in range(6):
        ch.append(C0 + i * 128 + np.arange(128))
    d = np.arange(128)
    perm = (d // 32) * 32 + ((d % 32) ^ 8)
    for i in range(4):
        ch.append(C0 + i * 128 + perm)
    for i in range(6):
        ch.append(D0 + i * 128 + np.arange(128))
    return ch


def rope_tables():
    t = np.arange(1024)
    pos = np.stack([t // 64, t % 64], -1).astype(np.float32)
    n_freq = 8
    inv = (10000.0 ** (-np.arange(n_freq, dtype=np.float32) / n_freq)).astype(np.float32)
    ang = pos[:, :, None] * inv
    cos, sin = np.cos(ang), np.sin(ang)
    ct = np.zeros((64, 1024), np.float32)
    st = np.zeros((64, 1024), np.float32)
    for m in range(2):
        for a in range(2):
            for b in range(2):
                for f in range(8):
                    row = m * 32 + a * 16 + b * 8 + f
                    ct[row] = cos[:, a, f]
                    st[row] = sin[:, a, f] * (-1.0 if b == 0 else 1.0)
    return np.stack([ct, st], 1)


def const_tables():
    i = np.arange(128)[:, None]
    j = np.arange(128)[None, :]
    ident = (i == j)
    low = (j < i)
    up = (j > i)
    lowi = (j <= i)
    upi = (j >= i)
    bd = (i // 64 == j // 64)
    d16 = (i // 16 == j // 16)
    d32 = (i // 32 == j // 32)
    d64 = (i // 64 == j // 64)
    cst = np.stack([ident, low, up, lowi, upi, bd, d16, d32 & ~d16, d64 & ~d32, ~d64], 1).astype(np.float32)
    sel = np.zeros((8, 8, 128), np.float32)
    for r in range(8):
        sel[r, r, :] = 1.0
    cq = np.arange(64)[:, None]
    ck = np.arange(64)[None, :]
    cstart = np.clip(cq - 8, 0, 48)
    inwin = (ck >= cstart) & (ck < cstart + 16)
    wmask = np.where(inwin, 0.0, -30000.0).astype(np.float32)
    mfb = np.zeros((8, 2), np.float32)
    mfb[0:4, 0] = 1.0
    mfb[4:8, 1] = 1.0
    return cst, sel, wmask, mfb


def prep_shared(inp):
    f32 = np.float32
    sh = {}
    w_mod = inp["w_mod"]
    sh["wmod_c"] = np.ascontiguousarray(w_mod.reshape(L_, 8, 128, 72, 128).transpose(0, 3, 2, 1, 4))
    sh["bmod"] = np.ascontiguousarray(inp["b_mod"].reshape(L_, 72, 128).transpose(0, 2, 1))
    fi = inp["ffn_in"].reshape(L_, 2, 8, 128, 2, 22, 128)
    sh["wgu_c"] = np.ascontiguousarray(fi.transpose(0, 1, 5, 4, 3, 2, 6)).reshape(L_, 2, 44, 128, 8, 128)
    fo = np.zeros((L_, 2, 3072, 1024), f32)
    fo[:, :, :2816] = inp["ffn_out"]
    fo = fo.reshape(L_, 2, 3, 8, 128, 8, 128)
    sh["wd_c"] = np.ascontiguousarray(fo.transpose(0, 1, 5, 2, 4, 3, 6))
    cols = win_chunk_cols()
    wi = inp["w_in"].reshape(L_, 8, 128, INC)
    winc = np.zeros((L_, 34, 128, 8, 128), f32)
    for c, cc in enumerate(cols):
        winc[:, c, :, :, :len(cc)] = wi[:, :, :, cc].transpose(0, 2, 1, 3)
    sh["win_c"] = winc
    sh["wout_c"] = np.ascontiguousarray(inp["w_out"].reshape(L_, 8, 128, 8, 128).transpose(0, 3, 2, 1, 4))
    vec = np.zeros((L_, 128, NV), f32)
    p = np.arange(128)
    for n in range(3):
        for kc in range(8):
            vec[:, :, n * 8 + kc] = inp["norm_pre"][:, n, kc * 128:(kc + 1) * 128]
            vec[:, :, 24 + n * 8 + kc] = inp["norm_post"][:, n, kc * 128:(kc + 1) * 128]
    for c in range(6):
        for j in range(5):
            vec[:, :, 48 + c * 5 + j] = inp["dn_conv"][:, j, c * 128:(c + 1) * 128]
    vec[:, :, 78] = inp["dn_norm"][:, p % 64]
    for i in range(9):
        cc = cols[9 + i] - B0
        for q in range(2):
            vec[:, :len(cc), 79 + i * 2 + q] = inp["rw_mu"][:, q, :][:, cc]
    for d in range(2):
        for hp in range(2):
            vec[:, :, 97 + d * 2 + hp] = inp["rw_w0"][:, d, hp * 128:(hp + 1) * 128]
    rk = inp["rw_r_k"].reshape(L_, 256)
    for hp in range(2):
        sl = slice(hp * 128, (hp + 1) * 128)
        vec[:, :, 101 + hp] = inp["rw_a0"][:, sl]
        vec[:, :, 103 + hp] = inp["rw_k_k"][:, sl]
        vec[:, :, 105 + hp] = inp["rw_k_a"][:, sl]
        vec[:, :, 107 + hp] = rk[:, sl]
        vec[:, :, 109 + hp] = inp["rw_ln"][:, 0, sl]
        vec[:, :, 111 + hp] = inp["rw_ln"][:, 1, sl]
    vec[:, :, 113] = inp["da_norm"][:, p % 64]
    sh["vec"] = vec
    dnp = np.zeros((L_, 128, 16), f32)
    dnp[:, :, 0:8] = inp["dn_dt_bias"].reshape(L_, 1, 8)
    dnp[:, :, 8:16] = inp["dn_a_log"].reshape(L_, 1, 8)
    sh["dnp"] = dnp
    dnr = np.zeros((L_, 8, 2), f32)
    dnr[:, :, 0] = inp["dn_dt_bias"].reshape(L_, 8)
    dnr[:, :, 1] = inp["dn_a_log"].reshape(L_, 8)
    sh["dnr"] = dnr
    sh["lamv"] = np.ascontiguousarray(np.broadcast_to(inp["da_lambda"].reshape(L_, 1, 128), (L_, 128, 128))).astype(f32)
    rwm = np.zeros((L_, 128, 3, 256), f32)
    rwm[:, :, 0, :] = inp["rw_w2"].reshape(L_, 128, 256)
    rwm[:, :64, 1, :] = inp["rw_a2"]
    rwm[:, :, 2, :] = inp["rw_g2"]
    sh["rwm"] = rwm
    cq = np.arange(64)[:, None]
    ck = np.arange(64)[None, :]
    dc = np.clip(ck - cq + 15, 0, 30)
    nb = inp["na_bias"][:, :, :, dc]
    sh["nab"] = np.ascontiguousarray(nb.transpose(0, 3, 1, 2, 4))
    cst, sel, wmask, mfb = const_tables()
    sh["cst"] = cst
    sh["sel"] = sel
    sh["wmask"] = wmask
    sh["mfb"] = mfb
    sh["rope"] = rope_tables()
    return sh


def prep_core(inp, c):
    f32 = np.float32
    b = c % 4
    d = {}
    xp = inp["x_prompt"][4 * c:4 * c + 4].reshape(1024, 1024)
    xs = inp["x_sample"][b]
    xa = np.concatenate([xp, xs], 0)
    d["xin"] = np.ascontiguousarray(xa.T.reshape(8, 128, 2048).transpose(1, 0, 2))
    cond = np.stack([inp["c_ctx"], inp["c"][b]], 0)
    d["condT"] = np.ascontiguousarray(cond.T.reshape(8, 128, 2).transpose(1, 0, 2))
    ck_ = inp["cache_diff_k"][b]
    d["cdk"] = np.ascontiguousarray(ck_.transpose(0, 3, 4, 1, 2).reshape(L_, 64, 4, 256))
    cv = inp["cache_diff_v"][b].reshape(L_, 4, 2, 128, 64)
    d["cdv"] = np.ascontiguousarray(cv.transpose(0, 3, 2, 1, 4)).reshape(L_, 128, 2, 256)
    nk = inp["cache_na_k"][b].reshape(L_, 2, 2, 256, 64)
    d["cnk"] = np.ascontiguousarray(nk.transpose(0, 2, 4, 1, 3)).reshape(L_, 128, 2, 256)
    nv = inp["cache_na_v"][b].reshape(L_, 4, 4, 64, 64)
    d["cnv"] = np.ascontiguousarray(nv.transpose(0, 3, 2, 1, 4)).reshape(L_, 64, 4, 256)
    zd = np.stack([inp["state_dn_fwd"][b], inp["state_dn_bwd"][b]], 1)
    zd = zd.reshape(L_, 2, 2, 2, 64, 64)
    d["zdn"] = np.ascontiguousarray(zd.transpose(0, 3, 4, 1, 2, 5)).reshape(L_, 128, 2, 2, 64)
    zr = np.stack([inp["state_rwkv_fwd"][b], inp["state_rwkv_bwd"][b]], 1)
    zr = zr.reshape(L_, 2, 2, 2, 64, 64)
    d["zrw"] = np.ascontiguousarray(zr.transpose(0, 3, 5, 1, 2, 4)).reshape(L_, 128, 2, 2, 64)
    return d


IN_SHAPES = {
    "xin": [128, 8, 2048], "condT": [128, 8, 2], "cst": [128, 10, 128], "sel": [8, 8, 128], "wmask": [64, 64],
    "mfb": [8, 2], "rope": [64, 2, 1024],
    "wmod_c": [L_, 72, 128, 8, 128], "bmod": [L_, 128, 72], "wgu_c": [L_, 2, 44, 128, 8, 128],
    "wd_c": [L_, 2, 8, 3, 128, 8, 128], "win_c": [L_, 34, 128, 8, 128], "wout_c": [L_, 8, 128, 8, 128],
    "vec": [L_, 128, NV], "dnp": [L_, 128, 16], "dnr": [L_, 8, 2], "lamv": [L_, 128, 128], "rwm": [L_, 128, 3, 256],
    "nab": [L_, 64, 4, 15, 64],
    "cdk": [L_, 64, 4, 256], "cdv": [L_, 128, 2, 256], "cnk": [L_, 128, 2, 256], "cnv": [L_, 64, 4, 256],
    "zdn": [L_, 128, 2, 2, 64], "zrw": [L_, 128, 2, 2, 64],
}
OUT_SHAPES = {
    "y_out": [128, 8, 2048], "o_dk": [L_, 4, 64, 1024], "o_dv": [L_, 128, 8, 256], "o_nk": [L_, 128, 2, 1024],
    "o_nv": [L_, 128, 8, 256], "o_dn": [L_, 2, 4, 128, 2, 64], "o_rw": [L_, 2, 4, 128, 2, 64],
}


class Prog:
    pass


def build(cfg=None):
    p1 = _build(cfg, None)
    return _build(cfg, p1.k.needed_rec)


def _build(cfg, needed):
    cfg = cfg or {}
    NL = cfg.get("nl", L_)
    stages = cfg.get("stages", ("ffn0", "mix", "ffn1"))
    mixers = cfg.get("mixers", "ABCD")
    dbg = cfg.get("dbg", {})
    k = K(needed)
    nc = k.nc
    P = Prog()
    P.k = k
    P.dbg_out = {}
    DI = {n: k.dram(n, s, F32, "ExternalInput") for n, s in IN_SHAPES.items()}
    DO = {n: k.dram(n, s, F32, "ExternalOutput") for n, s in OUT_SHAPES.items()}

    def dump(name, av, shape, dt=F32):
        d = k.dram("dbg_" + name, list(shape), dt, "ExternalOutput")
        P.dbg_out["dbg_" + name] = (list(shape), dt)
        k.dma(d[:], av, is_out=True)

    X = [[k.sb([128, 1024], F32, f"x{g}_{kc}") for kc in range(8)] for g in range(2)]
    CF = k.sb([128, 6, 128], F32, "cf")
    IDf, LOWf, UPf, LOWIf, UPIf, BDf = (CF[:, i, :] for i in range(6))
    MK = k.sb([128, 4, 128], BF16, "mk")
    IDb = k.sb([128, 128], BF16, "idb")
    ONESb = k.sb([128, 128], BF16, "onesb")
    BDb = k.sb([128, 128], BF16, "bdb")
    ONESf = k.sb([128, 128], F32, "onesf")
    NM = k.sb([128, 4, 128], F32, "nm")
    WS = [k.sb([128, 8, 128], BF16, f"ws{i}") for i in range(NSLOT)]
    BK = [k.ps([128, 512], F32, f"bk{i}") for i in range(8)]
    VEC = k.sb([128, NV], F32, "vec")
    BMOD = k.sb([128, 72], F32, "bmod")
    MOD = k.sb([128, 72, 2], F32, "mod")
    MVA = k.sb([128, 2, 3, 8], F32, "mva")
    MVG = k.sb([128, 2, 3, 8], F32, "mvg")
    SC32 = k.sb([128, 8, 2], F32, "sc32")
    SCb = k.sb([128, 8, 2], BF16, "scb")
    st = dict(bank=0, ws=0)

    def nb():
        b = BK[st["bank"] % 8]
        st["bank"] += 1
        return b

    def bfh(bank):
        return bank.h.bitcast(BF16)

    def wchunk(dram_av, nk=8):
        s = WS[st["ws"] % NSLOT]
        st["ws"] += 1
        k.dma(s[:, 0:nk, :], dram_av, eng=k.pool)
        return s

    k.dma(CF[:], DI["cst"][:, 0:6, :])
    k.copy(IDb[:], IDf)
    k.copy(BDb[:], BDf)
    sc0 = Scope(k)
    MKF = sc0.sb([128, 4, 128], F32, "mkf")
    k.dma(MKF[:], DI["cst"][:, 6:10, :])
    k.copy(MK[:], MKF[:])
    sc0.close()
    k.memset(ONESb[:], 1.0)
    k.memset(ONESf[:], 1.0)
    for i in range(4):
        k.ts(NM[:, i, :], CF[:, 1 + i, :], -1.0, ALU.add, -NEGBIG, ALU.mult)
    for g in range(2):
        for kc in range(8):
            k.dma(X[g][kc][:], DI["xin"][:, kc, g * 1024:(g + 1) * 1024])
    k.dma(SC32[:], DI["condT"][:])
    k.activation(SCb[:], SC32[:], AF.Silu)

    TS = [slice(0, 512), slice(512, 1024)]

    def rstd_tile(srcs, SQ, RT, RS):
        bank = nb()
        for kc in range(8):
            if kc % 2 == 0:
                k.activation(SQ[0][:], srcs[kc], AF.Square)
            else:
                k.tt(SQ[1][:], srcs[kc], srcs[kc], ALU.mult)
            k.mm1(bank[:], ONESb[:], SQ[kc % 2][:], kc == 0, kc == 7)
        k.activation(RT[:], bank[:], AF.Sqrt, bias=1e-6, scale=1.0 / 1024)
        k.recip(RS[:], RT[:])

    def norm_mod(g, n, H, SQ, RT, RS, TMP):
        for tt in range(2):
            rstd_tile([X[g][kc][:, TS[tt]] for kc in range(8)], SQ, RT, RS)
            for kc in range(8):
                k.stt(TMP[kc % 2][:], X[g][kc][:, TS[tt]], MVA[:, g, n, kc:kc + 1], RS[:], ALU.mult, ALU.mult)
                k.activation(H[:, kc, TS[tt]], TMP[kc % 2][:], AF.Identity, bias=MOD[:, (3 * n) * 8 + kc, g:g + 1])

    def post(g, n, Y, SQ, RT, RS, TMP):
        for tt in range(2):
            rstd_tile([Y[:, kc, TS[tt]] for kc in range(8)], SQ, RT, RS)
            for kc in range(8):
                k.stt(TMP[kc % 2][:], Y[:, kc, TS[tt]], MVG[:, g, n, kc:kc + 1], RS[:], ALU.mult, ALU.mult)
                k.tt(X[g][kc][:, TS[tt]], X[g][kc][:, TS[tt]], TMP[kc % 2][:], ALU.add)

    def ffn(l, g, f):
        n = 0 if f == 0 else 2
        sc = Scope(k)
        H = sc.sb([128, 8, 1024], BF16, "h")
        ACT_ = sc.sb([128, 22, 1024], BF16, "act")
        Y = sc.sb([128, 8, 1024], F32, "y")
        SQ = [sc.sb([128, 512], BF16, "sq") for _ in range(2)]
        RT = sc.sb([128, 512], F32, "rt")
        RS = sc.sb([128, 512], F32, "rs")
        TMP = [sc.sb([128, 512], F32, "tmp") for _ in range(2)]
        norm_mod(g, n, H, SQ, RT, RS, TMP)
        for j in range(22):
            sg = wchunk(DI["wgu_c"][l, f, 2 * j])
            su = wchunk(DI["wgu_c"][l, f, 2 * j + 1])
            for tt in range(2):
                pg = nb()
                pu = nb()
                k.mmg(pg[:], [(sg[:, kc, :], H[:, kc, TS[tt]]) for kc in range(8)])
                k.mmg(pu[:], [(su[:, kc, :], H[:, kc, TS[tt]]) for kc in range(8)])
                k.activation(TMP[tt][:], pg[:], AF.Silu)
                k.tt(ACT_[:, j, TS[tt]], TMP[tt][:], pu[:], ALU.mult)
        for m in range(8):
            sls = [wchunk(DI["wd_c"][l, f, m, c, :, 0:(8 if c < 2 else 6), :], nk=(8 if c < 2 else 6)) for c in range(3)]
            for tt in range(2):
                py = nb()
                k.mmg(py[:], [(sls[j // 8][:, j % 8, :], ACT_[:, j, TS[tt]]) for j in range(22)])
                k.copy(Y[:, m, TS[tt]], py[:], eng=k.act)
        post(g, n, Y, SQ, RT, RS, TMP)
        sc.close()

    P.X, P.nb, P.bfh, P.wchunk, P.DI, P.DO, P.dump = X, nb, bfh, wchunk, DI, DO, dump
    P.consts = dict(IDf=IDf, LOWf=LOWf, UPf=UPf, LOWIf=LOWIf, UPIf=UPIf, BDf=BDf, IDb=IDb, ONESb=ONESb, BDb=BDb,
                    ONESf=ONESf, NM=NM, CF=CF, MK=MK)
    P.VEC, P.MOD, P.MVA, P.MVG, P.TS = VEC, MOD, MVA, MVG, TS
    P.rstd_tile, P.norm_mod, P.post = rstd_tile, norm_mod, post

    for l in range(NL):
        k.dma(VEC[:], DI["vec"][l])
        k.dma(BMOD[:], DI["bmod"][l])
        pm = nb()
        pmv = AV(pm.h[:, 0:144].rearrange("p (j g) -> p j g", g=2), pm)
        for j in range(72):
            sl = wchunk(DI["wmod_c"][l, j])
            k.mmg(AV(pm.h[:, 2 * j:2 * j + 2], pm), [(sl[:, kc, :], SCb[:, kc, :]) for kc in range(8)])
        for g in range(2):
            k.tt(MOD[:, :, g], AV(pm.h[:, 0:144].rearrange("p (j g) -> p j g", g=2)[:, :, g], pm), BMOD[:], ALU.add)
        for g in range(2):
            for n in range(3):
                wt = 1.0 if n == 1 else 0.5
                k.stt(MVA[:, g, n, :], MOD[:, (3 * n + 1) * 8:(3 * n + 2) * 8, g], 1.0, VEC[:, n * 8:(n + 1) * 8],
                      ALU.add, ALU.mult)
                k.stt(MVG[:, g, n, :], MOD[:, (3 * n + 2) * 8:(3 * n + 3) * 8, g], wt,
                      VEC[:, 24 + n * 8:24 + (n + 1) * 8], ALU.mult, ALU.mult)
        if "mod" in dbg:
            dump(f"mod{l}", MOD[:], [128, 72, 2])
        for g in range(2):
            if "ffn0" in stages:
                ffn(l, g, 0)
            if "x1" in dbg:
                for kc in range(8):
                    dump(f"x1_{l}_{g}_{kc}", X[g][kc][:], [128, 1024])
            if "mix" in stages:
                mixer_phase(P, l, g, mixers, dbg)
            if "x2" in dbg:
                for kc in range(8):
                    dump(f"x2_{l}_{g}_{kc}", X[g][kc][:], [128, 1024])
            if "ffn1" in stages:
                ffn(l, g, 1)

    for g in range(2):
        for kc in range(8):
            k.dma(DO["y_out"][:, kc, g * 1024:(g + 1) * 1024], X[g][kc][:], is_out=True)
    k.finish()
    return P


def mixer_phase(P, l, g, mixers, dbg):
    k = P.k
    nb, wchunk, DI, DO, TS = P.nb, P.wchunk, P.DI, P.DO, P.TS
    sc = Scope(k)
    OTs = [None] * 4
    OTs[0] = sc.sb([128, 2, 1024], BF16, "ot0")
    H1 = sc.sb([128, 8, 1024], BF16, "h1")
    SQ = [sc.sb([128, 512], BF16, "sq") for _ in range(2)]
    RT = sc.sb([128, 512], F32, "rt")
    RS = sc.sb([128, 512], F32, "rs")
    TMP = [sc.sb([128, 512], F32, "tmp") for _ in range(2)]
    P.norm_mod(g, 1, H1, SQ, RT, RS, TMP)
    M = Prog()
    M.P, M.l, M.g, M.H1, M.SQ, M.RT, M.RS, M.TMP = P, l, g, H1, SQ, RT, RS, TMP
    M.dbg = dbg

    def pj(slot, col0, Mrows):
        outs = []
        for tt in range(2):
            b = nb()
            k.mmg(b[0:Mrows, :], [(slot[:, kc, col0:col0 + Mrows], H1[:, kc, TS[tt]]) for kc in range(8)])
            outs.append(b)
        return outs

    def proj_tm(chunks, ntok):
        slots = [wchunk(DI["win_c"][l, c]) for c in chunks]
        for i in range(1024 // ntok):
            b = nb()
            for ci, s in enumerate(slots):
                k.mmg(b[0:ntok, ci * 128:(ci + 1) * 128],
                      [(H1[:, kc, i * ntok:(i + 1) * ntok], s[:, kc, :]) for kc in range(8)])
            yield i, b
    M.pj, M.proj_tm = pj, proj_tm
    fns = dict(A=mixer_A, B=mixer_B, C=mixer_C, D=mixer_D)
    for ci, name in enumerate("ABCD"):
        if ci == 1:
            for c2 in range(1, 4):
                OTs[c2] = sc.sb([128, 2, 1024], BF16, f"ot{c2}")
        M.OT = OTs[ci]
        if name in mixers:
            fns[name](M)
        else:
            k.memset(OTs[ci][:], 0.0)
    if "ot" in dbg:
        for ci in range(4):
            P.dump(f"ot_{l}_{g}_{ci}", OTs[ci][:], [128, 2, 1024], BF16)
    Y = sc.sb([128, 8, 1024], F32, "y")
    for m in range(8):
        s = wchunk(DI["wout_c"][l, m])
        for tt in range(2):
            py = nb()
            k.mmg(py[:], [(s[:, kc, :], OTs[kc // 2][:, kc % 2, TS[tt]]) for kc in range(8)])
            k.copy(Y[:, m, TS[tt]], py[:], eng=k.act)
    P.post(g, 1, Y, SQ, RT, RS, TMP)
    sc.close()


def skewed(units):
    if not units:
        return
    units[0][0]()
    for u in range(len(units)):
        if u + 1 < len(units):
            units[u + 1][0]()
        units[u][1]()


def ot_from_tokmajor(M, O, c0, ntok, ntile, norm=None):
    k = M.P.k
    nb = M.P.nb
    IDf = M.P.consts["IDf"]
    for i in range(ntile):
        b = nb()
        k.trq([(b[:, c * ntok:(c + 1) * ntok], O[0:ntok, i, c * 128:(c + 1) * 128], AV(IDf.ap[0:ntok, 0:ntok], IDf.t))
               for c in range(2)])
        src = AV(b.h[:, 0:2 * ntok].rearrange("p (c n) -> p c n", c=2), b)
        k.copy(M.OT[:, 0:2, i * ntok:(i + 1) * ntok], src, eng=(k.act if i % 2 else k.dve))


def mixer_D(M):
    P, l, g, H1, OT = M.P, M.l, M.g, M.H1, M.OT
    k = P.k
    nb, wchunk, DI, DO, TS, bfh = P.nb, P.wchunk, P.DI, P.DO, P.TS, P.bfh
    IDb = P.consts["IDb"]
    sc = Scope(k)
    QT = sc.sb([128, 2, 1024], BF16, "dq")
    KT = sc.sb([128, 2, 1024], BF16, "dk")
    KF = [sc.sb([128, 512], F32, "dkf") for _ in range(2)]
    for hp in range(2):
        s = wchunk(DI["win_c"][l, 28 + hp])
        bs = M.pj(s, 0, 128)
        for tt in range(2):
            k.copy(QT[:, hp, TS[tt]], bs[tt][:], eng=k.act)
        s = wchunk(DI["win_c"][l, 30 + hp])
        bs = M.pj(s, 0, 128)
        for tt in range(2):
            if g == 0:
                k.copy(KF[tt][:], bs[tt][:])
                k.dma(DO["o_nk"][l, :, hp, TS[tt]], KF[tt][:], is_out=True)
                k.copy(KT[:, hp, TS[tt]], KF[tt][:], eng=k.act)
            else:
                k.copy(KT[:, hp, TS[tt]], bs[tt][:], eng=k.act)
    SUM = sc.sb([128, 4], F32, "dsum")
    RS4 = sc.sb([128, 4], F32, "drs")
    MX = sc.sb([128, 4], F32, "dmx")
    NMX = sc.sb([128, 4], F32, "dnmx")
    if g == 0:
        V = sc.sb([128, 8, 256], BF16, "dv")
        VF = [sc.sb([128, 256], F32, "dvf") for _ in range(2)]
        for i, b in M.proj_tm([32, 33], 128):
            k.copy(VF[i % 2][:], b[:, 0:256])
            k.dma(DO["o_nv"][l, :, i, :], VF[i % 2][:], is_out=True)
            k.copy(V[:, i, :], VF[i % 2][:], eng=k.act)
        O = sc.sb([128, 8, 256], F32, "do")
        Pb = [sc.sb([128, 4, 256], BF16, "dp") for _ in range(2)]
        PT = [sc.sb([128, 8, 128], BF16, "dpt") for _ in range(2)]
        SUMs = [sc.sb([128, 4], F32, "dsum2") for _ in range(2)]
        units = []
        u = 0
        for s_ in range(4):
            for qb in range(2):
                stt_ = {}

                def pa(u=u, s_=s_, qb=qb, stt_=stt_):
                    ti = 2 * s_ + qb
                    qsl = slice(ti * 128, (ti + 1) * 128)
                    ksl = slice(s_ * 256, (s_ + 1) * 256)
                    bsc = [nb(), nb()]
                    for i2 in range(2):
                        k.mmq([(bsc[i2][:, hh * 256:(hh + 1) * 256], QT[hh * 64:hh * 64 + 64, i2, qsl], KT[hh * 64:hh * 64 + 64, i2, ksl])
                               for hh in range(2)])
                    for i2 in range(2):
                        k.reduce(MX[:, 2 * i2:2 * i2 + 2], AV(bsc[i2].h[:, :].rearrange("p (h n) -> p h n", h=2), bsc[i2]), ALU.max)
                    k.ts(NMX[:], MX[:], -0.125, ALU.mult)
                    pb = Pb[u % 2]
                    for h in range(4):
                        k.activation(pb[:, h, :], bsc[h // 2][:, (h % 2) * 256:(h % 2 + 1) * 256], AF.Exp,
                                     bias=NMX[:, h:h + 1], scale=0.125, accum_out=SUMs[u % 2][:, h:h + 1])

                def pb_(u=u, s_=s_, qb=qb):
                    ti = 2 * s_ + qb
                    pb = Pb[u % 2]
                    bt = nb()
                    bth = bfh(bt)
                    k.trq([(AV(bth[:, (h * 2 + c) * 128:(h * 2 + c + 1) * 128], bt), pb[:, h, c * 128:(c + 1) * 128], IDb[:])
                           for h in range(4) for c in range(2)])
                    pt = PT[u % 2]
                    k.copy(pt[:], AV(bth[:, 0:1024].rearrange("p (j n) -> p j n", j=8), bt), eng=k.act)
                    po = nb()
                    for h in range(4):
                        k.mmg(po[:, h * 64:(h + 1) * 64], [(pt[:, h * 2 + c, :], V[:, 2 * s_ + c, h * 64:(h + 1) * 64]) for c in range(2)])
                    k.recip(RS4[:], SUMs[u % 2][:])
                    k.tt(AV(O.h[:, ti, :].rearrange("p (h d) -> p h d", h=4), O),
                         AV(po.h[:, 0:256].rearrange("p (h d) -> p h d", h=4), po),
                         AV(RS4.h[:, :].unsqueeze(2).to_broadcast([128, 4, 64]), RS4), ALU.mult)
                units.append((pa, pb_))
                u += 1
        skewed(units)
        ot_from_tokmajor(M, O, 6, 128, 8)
    else:
        KcT = sc.sb([128, 2, 256], BF16, "dkc")
        Vc = sc.sb([64, 4, 256], BF16, "dvc")
        TB = sc.sb([64, 4, 15, 64], F32, "dtb")
        WM = sc.sb([64, 64], F32, "dwm")
        k.dma(KcT[:], DI["cnk"][l], eng=k.pool)
        k.dma(Vc[:], DI["cnv"][l], eng=k.pool)
        k.dma(TB[:], DI["nab"][l])
        k.dma(WM[:], DI["wmask"][:])
        for h in range(4):
            k.tt(TB[:, h, :, :], TB[:, h, :, :], AV(WM.h[:, :].unsqueeze(1).to_broadcast([64, 15, 64]), WM), ALU.add)
        V64 = sc.sb([64, 16, 256], BF16, "dv64")
        for i, b in M.proj_tm([32, 33], 64):
            k.copy(V64[:, i, :], b[0:64, 0:256], eng=(k.act if i % 2 else k.dve))
        O64 = sc.sb([64, 16, 256], F32, "do64")
        S = [sc.sb([64, 768], F32, "ds") for _ in range(2)]
        Pb = [sc.sb([64, 768], BF16, "dp") for _ in range(2)]
        PT = [sc.sb([64, 12, 64], BF16, "dpt") for _ in range(2)]
        SUMs = [sc.sb([64, 1], F32, "dsum2") for _ in range(2)]
        units = []
        u = 0
        for r in range(16):
            for h in range(4):
                def pa(u=u, r=r, h=h):
                    rs = min(max(r - 4, 0), 8)
                    dr0 = rs - r + 7
                    hp, hb = h // 2, (h % 2) * 64
                    b1, b2 = nb(), nb()
                    q = QT[hb:hb + 64, hp, r * 64:(r + 1) * 64]
                    k.mm(b1[0:64, :], q, KT[hb:hb + 64, hp, rs * 64:rs * 64 + 512])
                    k.mm(b2[0:64, 0:256], q, KcT[hb:hb + 64, hp, :])
                    s_ = S[u % 2]
                    k.stt(s_[:, 0:512], b1[0:64, :], 0.125,
                          AV(TB.h[:, h, dr0:dr0 + 8, :].rearrange("p a b -> p (a b)"), TB), ALU.mult, ALU.add)
                    k.activation(s_[:, 512:768], b2[0:64, 0:256], AF.Copy, scale=0.125)
                    k.reduce(MX[0:64, 0:1], s_[:], ALU.max)
                    k.ts(NMX[0:64, 0:1], MX[0:64, 0:1], -1.0, ALU.mult)
                    k.activation(Pb[u % 2][:], s_[:], AF.Exp, bias=NMX[0:64, 0:1], scale=1.0, accum_out=SUMs[u % 2][:, 0:1])

                def pb_(u=u, r=r, h=h):
                    rs = min(max(r - 4, 0), 8)
                    pb = Pb[u % 2]
                    bt = nb()
                    bth = bfh(bt)
                    k.trq([(AV(bth[0:64, j * 64:(j + 1) * 64], bt), pb[:, j * 64:(j + 1) * 64], IDb[0:64, 0:64]) for j in range(12)])
                    pt = PT[u % 2]
                    k.copy(pt[:], AV(bth[0:64, 0:768].rearrange("p (j n) -> p j n", j=12), bt), eng=k.act)
                    po = nb()
                    k.mmg(po[0:64, 0:64], [(pt[:, j, :], V64[:, rs + j, h * 64:(h + 1) * 64]) for j in range(8)] +
                          [(pt[:, 8 + c, :], Vc[:, c, h * 64:(h + 1) * 64]) for c in range(4)])
                    k.recip(RS4[0:64, 0:1], SUMs[u % 2][:, 0:1])
                    k.ts(O64[:, r, h * 64:(h + 1) * 64], po[0:64, 0:64], RS4[0:64, 0:1], ALU.mult)
                units.append((pa, pb_))
                u += 1
        skewed(units)
        ot_from_tokmajor(M, O64, 6, 64, 16)
    sc.close()


def interleave(gens):
    gens = list(gens)
    while gens:
        for g_ in list(gens):
            try:
                next(g_)
            except StopIteration:
                gens.remove(g_)


def tri_solve(M, N0, B0, Y0, SV, nprob, U, wt_slices):
    for _ in tri_solve_g(M, N0, B0, Y0, SV, nprob, U, wt_slices):
        pass


def tri_solve_g(M, N0, B0, Y0, SV, nprob, U, wt_slices):
    k = M.P.k
    nb = M.P.nb
    IDb, MK = M.P.consts["IDb"], M.P.consts["MK"]
    np_ = nprob
    cnt = [0]

    def T():
        t = SV[cnt[0] % len(SV)]
        cnt[0] += 1
        return t

    def mk(i):
        if "MK4" in M.P.consts:
            return M.P.consts["MK4"][:, i, 0:np_, :]
        return AV(MK.h[:, i, :].unsqueeze(1).to_broadcast([128, np_, 128]), MK)

    def idb():
        if "ID4" in M.P.consts:
            return M.P.consts["ID4"][:, 0:np_, :]
        return AV(IDb.h[:, :].unsqueeze(1).to_broadcast([128, np_, 128]), IDb)

    def v(t):
        return t[:, 0:np_, :]

    def pv(b):
        return AV(b.h[:, 0:np_ * 128].rearrange("p (q n) -> p q n", q=np_), b)

    def mmb(l, r):
        b = nb()
        if MMQ:
            k.mmq([(b[:, q * 128:(q + 1) * 128], l[:, q, :], r[:, q, :]) for q in range(np_)])
        else:
            for q in range(np_):
                k.mm(b[:, q * 128:(q + 1) * 128], l[:, q, :], r[:, q, :])
        return b
    ei = [0]

    def ev(dst, b, add=None, eng=None):
        if add is None:
            e = eng or k.act
            if "E1" in M.dbg:
                e = k.dve
            ei[0] += 1
            k.copy(v(dst), pv(b), eng=e)
        else:
            k.tt(v(dst), pv(b), add, ALU.add)
    Nd, Bd, IpNd = T(), T(), T()
    k.tt(v(Nd), v(N0), mk(0), ALU.mult)
    k.tt(v(Bd), v(B0), mk(0), ALU.mult)
    k.tt(v(IpNd), v(Nd), idb(), ALU.add)
    yield
    if "S1" in M.dbg:
        return
    b1, b2 = mmb(Bd, Nd), mmb(Nd, Bd)
    N2, B2, IpB2 = T(), T(), T()
    ev(N2, b1)
    ev(B2, b2)
    ev(IpB2, b2, add=idb())
    yield
    if "S2" in M.dbg:
        return
    if "X1" in M.dbg:
        b3 = mmb(Bd, Nd)
        return
    if "X2" in M.dbg:
        b3 = mmb(IpNd, Nd)
        return
    if "X3" in M.dbg:
        b3 = mmb(Nd, IpB2)
        return
    b3 = mmb(IpNd, IpB2)
    if "X4" in M.dbg:
        return
    P1T = T()
    ev(P1T, b3)
    yield
    if "S2a" in M.dbg:
        return
    b1, b2 = mmb(B2, N2), mmb(N2, B2)
    N4, B4, IpB4 = T(), T(), T()
    ev(N4, b1)
    ev(B4, b2)
    ev(IpB4, b2, add=idb())
    yield
    if "S2b" in M.dbg:
        return
    b1 = mmb(B4, N4)
    IpN8 = T()
    ev(IpN8, b1, add=idb())
    yield
    if "S2c" in M.dbg:
        return
    b1 = mmb(IpB4, IpN8)
    P2 = T()
    ev(P2, b1)
    yield
    if "S2d" in M.dbg:
        return
    b1, b2 = mmb(P1T, P2), mmb(P2, P1T)
    Mc, MTc = T(), T()
    ev(Mc, b1)
    ev(MTc, b2)
    yield
    if "S3" in M.dbg:
        return
    for lev in (1, 2, 3):
        No, Bo = T(), T()
        k.tt(v(No), v(N0), mk(lev), ALU.mult)
        if lev < 3:
            k.tt(v(Bo), v(B0), mk(lev), ALU.mult)
            b1 = mmb(Bo, Mc)
            Tt = T()
            ev(Tt, b1)
        b2 = mmb(No, MTc)
        Tp = T()
        ev(Tp, b2)
        yield
        if lev < 3:
            b1 = mmb(MTc, Tt)
            Mn = T()
            ev(Mn, b1, add=v(Mc))
        b2 = mmb(Mc, Tp)
        MTn = T()
        ev(MTn, b2, add=v(MTc))
        yield
        if lev < 3:
            Mc = Mn
        MTc = MTn
    if "S4" in M.dbg:
        return
    bu = nb()
    k.mmq([(bu[:, q * 64:(q + 1) * 64], MTc[:, q, :], Y0[:, q, 0:64]) for q in range(np_)])
    k.copy(U[:, 0:np_, :], AV(bu.h[:, 0:np_ * 64].rearrange("p (q n) -> p q n", q=np_), bu), eng=k.dve)
    bw = nb()
    k.mmq([(bw[wt_slices[q][1]:wt_slices[q][1] + 64, (q // 2) * 128:(q // 2 + 1) * 128], Y0[:, q, 64:128], MTc[:, q, :])
           for q in range(np_)])
    for q in range(np_):
        hb = wt_slices[q][1]
        k.copy(wt_slices[q][0], bw[hb:hb + 64, (q // 2) * 128:(q // 2 + 1) * 128], eng=k.act)


def mixer_A(M):
    P, l, g, H1, OT = M.P, M.l, M.g, M.H1, M.OT
    k = P.k
    nb, wchunk, DI, DO, TS, bfh = P.nb, P.wchunk, P.DI, P.DO, P.TS, P.bfh
    C = P.consts
    IDb, BDb, IDf, ONESf, NM = C["IDb"], C["BDb"], C["IDf"], C["ONESf"], C["NM"]
    LOWIf, UPIf = C["LOWIf"], C["UPIf"]
    VEC = P.VEC
    nseq = 4 if g == 0 else 1
    Ls = 1024 // nseq
    sc = Scope(k)
    QT = sc.sb([128, 2, 1024], BF16, "aq")
    KT = sc.sb([128, 2, 1024], BF16, "ak")
    Kt = sc.sb([128, 8, 256], BF16, "akt")
    Vt = sc.sb([128, 8, 256], BF16, "avt")
    GS = sc.sb([128, 2, 1024], BF16, "ags")
    O = sc.sb([128, 8, 256], F32, "ao")
    ROWS = sc.sb([8, 3, 1024], F32, "arows")
    COLS = sc.sb([128, 9, 8, 8], F32, "acols")
    cLA, cL2, cG, cGH, cBET, cEGH, cEG, cED, cEGT = range(9)
    SELt = sc.sb([8, 8, 128], F32, "asel")
    Zf = sc.sb([128, 2, 2, 64], F32, "azf")
    Zb = sc.sb([128, 2, 2, 64], BF16, "azb")
    k.dma(SELt[:], DI["sel"][:])
    k.memset(O[:], 0.0)
    sc1 = Scope(k)
    RAW = sc1.sb([128, 1024], F32, "araw")
    CV = sc1.sb([128, 1024], F32, "acv")
    XS = sc1.sb([128, 1024], F32, "axs")
    RAWv = AV(RAW.h[:, :].rearrange("p (s t) -> p s t", s=nseq), RAW)
    CVv = AV(CV.h[:, :].rearrange("p (s t) -> p s t", s=nseq), CV)

    def v3(t, a, b):
        return AV(t.h[:, :].rearrange("p (s t) -> p s t", s=nseq)[:, :, a:b], t)
    for c in range(6):
        s = wchunk(DI["win_c"][l, c])
        bs = M.pj(s, 0, 128)
        for tt in range(2):
            k.copy(RAW[:, TS[tt]], bs[tt][:], eng=(k.act if tt else k.dve))
        wc = lambda j: VEC[:, 48 + c * 5 + j:48 + c * 5 + j + 1]
        k.ts(CV[:], RAW[:], wc(2), ALU.mult)
        for j in (0, 1, 3, 4):
            o = j - 2
            t0, t1 = max(0, -o), Ls - max(0, o)
            k.stt(v3(CV, t0, t1), v3(RAW, t0 + o, t1 + o), wc(j), v3(CV, t0, t1), ALU.mult, ALU.add)
        k.activation(XS[:], CV[:], AF.Silu)
        kind, hp = c // 2, c % 2
        if kind < 2:
            dst = QT if kind == 0 else KT
            for tt in range(2):
                k.activation(M.SQ[0][:], XS[:, TS[tt]], AF.Square)
                b = nb()
                k.mm(b[:], BDb[:], M.SQ[0][:])
                k.activation(M.RT[:], b[:], AF.Sqrt, bias=1e-6, scale=1.0)
                k.recip(M.RS[:], M.RT[:])
                if kind == 0:
                    k.stt(dst[:, hp, TS[tt]], XS[:, TS[tt]], 0.125, M.RS[:], ALU.mult, ALU.mult)
                else:
                    k.tt(XS[:, TS[tt]], XS[:, TS[tt]], M.RS[:], ALU.mult)
                    k.copy(dst[:, hp, TS[tt]], XS[:, TS[tt]], eng=k.act)
        if kind >= 1:
            dstt = Kt if kind == 1 else Vt
            for i in range(0, 8, 4):
                b = nb()
                k.trq([(b[:, ii * 128:(ii + 1) * 128], XS[:, (i + ii) * 128:(i + ii + 1) * 128], IDf) for ii in range(4)])
                k.copy(dstt[:, i:i + 4, hp * 128:(hp + 1) * 128],
                       AV(b.h[:, :].rearrange("p (i n) -> p i n", i=4), b), eng=(k.act if i else k.dve))
    for hp in range(2):
        s = wchunk(DI["win_c"][l, 6 + hp])
        bs = M.pj(s, 0, 128)
        for tt in range(2):
            k.activation(GS[:, hp, TS[tt]], bs[tt][:], AF.Silu)
    sc1.close()
    if "A1" in M.dbg:
        k.memset(OT[:], 0.0)
        sc.close()
        return
    sc1 = Scope(k)
    DNR = sc1.sb([8, 2], F32, "adnr")
    NEGAr = sc1.sb([8, 1], F32, "anegar")
    MFB = sc1.sb([8, 2], F32, "amfb")
    DNP = sc1.sb([128, 16], F32, "adnp")
    NEGAc = sc1.sb([128, 8], F32, "anegac")
    RST = sc1.sb([8, 1024], F32, "arst")
    RA = sc1.sb([8, 1024], F32, "ara")
    RB = sc1.sb([8, 1024], F32, "arb")
    RC = sc1.sb([8, 1024], F32, "arc")
    TOT = sc1.sb([8, 8], F32, "atot")
    k.dma(DNR[:], DI["dnr"][l])
    k.dma(MFB[:], DI["mfb"][:])
    k.dma(DNP[:], DI["dnp"][l])
    k.activation(NEGAr[:], DNR[:, 1:2], AF.Exp)
    k.ts(NEGAr[:], NEGAr[:], -1.0, ALU.mult)
    k.activation(NEGAc[:], DNP[:, 8:16], AF.Exp)
    k.ts(NEGAc[:], NEGAc[:], -1.0, ALU.mult)
    k.memset(RST[:], 1.0)
    k.memset(AV(RST.h[:, 0:1024:128], RST), 0.0)
    s = wchunk(DI["win_c"][l, 8])
    ba = M.pj(s, 0, 8)
    bb_ = M.pj(s, 8, 8)
    for tt in range(2):
        k.activation(RA[:, TS[tt]], ba[tt][0:8, :], AF.Exp, bias=DNR[:, 0:1])
        k.activation(RB[:, TS[tt]], bb_[tt][0:8, :], AF.Exp, scale=-1.0)
    k.activation(RA[:], RA[:], AF.Ln, bias=1.0)
    k.ts(RA[:], RA[:], NEGAr[:, 0:1], ALU.mult)
    k.activation(ROWS[:, 1, :], RB[:], AF.Ln, bias=1.0)
    k.scan(RB[:], RST[:], RA[:], 0.0, ALU.mult, ALU.add)
    k.copy(TOT[:], AV(RB.h[:, 127:1024:128], RB))
    r3 = lambda t: AV(t.h[:, :].rearrange("p (c n) -> p c n", c=8), t)
    k.tt(r3(RC), r3(RA), r3(RB), ALU.subtract)
    k.tt(r3(RC), r3(RC), AV(TOT.h[:, :].unsqueeze(2).to_broadcast([8, 8, 128]), TOT), ALU.add)
    k.ts(ROWS[:, 2, :], RB[:], MFB[:, 0:1], ALU.mult)
    k.stt(ROWS[:, 2, :], RC[:], MFB[:, 1:2], ROWS[:, 2, :], ALU.mult, ALU.add)
    k.tt(ROWS[:, 1, :], ROWS[:, 2, :], ROWS[:, 1, :], ALU.subtract)
    k.ts(ROWS[:, 0, :], ROWS[:, 2, :], -1.0, ALU.mult)
    ABc = sc1.sb([128, 8, 16], F32, "aabc")
    TC = sc1.sb([128, 8, 8], F32, "atc")
    for i, b in M.proj_tm([8], 128):
        k.copy(ABc[:, i, :], b[:, 0:16], eng=(k.act if i % 2 else k.dve))
    k.tt(TC[:], ABc[:, :, 0:8], AV(DNP.h[:, 0:8].unsqueeze(1).to_broadcast([128, 8, 8]), DNP), ALU.add)
    k.activation(TC[:], TC[:], AF.Exp)
    k.activation(TC[:], TC[:], AF.Ln, bias=1.0)
    k.tt(COLS[:, cLA, :, :], TC[:], AV(NEGAc.h[:, :].unsqueeze(1).to_broadcast([128, 8, 8]), NEGAc), ALU.mult)
    k.activation(TC[:], ABc[:, :, 8:16], AF.Exp, scale=-1.0)
    k.activation(COLS[:, cL2, :, :], TC[:], AF.Ln, bias=1.0)
    bg = nb()
    its = []
    for i in range(8):
        its.append((bg[:, i * 16:i * 16 + 4], UPIf, COLS[:, cLA, i, 0:4]))
        its.append((bg[:, i * 16 + 4:i * 16 + 8], LOWIf, COLS[:, cLA, i, 4:8]))
        its.append((bg[:, i * 16 + 8:i * 16 + 16], ONESf[:], COLS[:, cLA, i, :]))
    k.mmq(its)
    bgv = AV(bg.h[:, 0:128].rearrange("p (i c) -> p i c", i=8), bg)
    k.copy(COLS[:, cG, :, :], AV(bgv.ap[:, :, 0:8], bg))
    k.tt(COLS[:, cGH, :, :], COLS[:, cG, :, :], COLS[:, cL2, :, :], ALU.subtract)
    k.activation(COLS[:, cBET, :, :], COLS[:, cL2, :, :], AF.Exp, scale=-1.0)
    k.activation(COLS[:, cEGH, :, :], COLS[:, cGH, :, :], AF.Exp)
    k.activation(COLS[:, cEG, :, :], COLS[:, cG, :, :], AF.Exp)
    k.tt(TC[:], AV(bgv.ap[:, :, 8:16], bg), COLS[:, cG, :, :], ALU.subtract)
    k.activation(COLS[:, cED, :, :], TC[:], AF.Exp)
    k.activation(COLS[:, cEGT, :, :], AV(bgv.ap[:, :, 8:16], bg), AF.Exp)
    sc1.close()
    if "A" in M.dbg and g == 0:
        P.dump("A_QT", QT[:], [128, 2, 1024], BF16)
        P.dump("A_KT", KT[:], [128, 2, 1024], BF16)
        P.dump("A_Kt", Kt[:], [128, 8, 256], BF16)
        P.dump("A_Vt", Vt[:], [128, 8, 256], BF16)
        P.dump("A_GS", GS[:], [128, 2, 1024], BF16)
        P.dump("A_ROWS", ROWS[:], [8, 3, 1024])
        P.dump("A_COLS", COLS[:], [128, 9, 8, 8])
    if "A2" in M.dbg:
        k.memset(OT[:], 0.0)
        sc.close()
        return
    sc2 = Scope(k)
    EX = sc2.sb([128, 3, 4, 128], F32, "aex")
    Dm = EX
    NBYs = [[sc2.sb([128, 4, 128], BF16, "anby") for _ in range(3)] for _ in range(2)]
    SVs = [[sc2.sb([128, 4, 128], BF16, "asv") for _ in range(11)] for _ in range(2)]
    QKT = [sc2.sb([128, 4, 128], BF16, "aqkt") for _ in range(2)]
    KDEC = [sc2.sb([128, 4, 64], BF16, "akdec") for _ in range(2)]
    U = [sc2.sb([128, 4, 64], F32, "au") for _ in range(2)]
    WT = [sc2.sb([128, 2, 128], BF16, "awt") for _ in range(2)]
    VNs = [sc2.sb([128, 4, 64], BF16, "avn") for _ in range(2)]
    T1s = [sc2.sb([128, 4, 64], F32, "at1") for _ in range(2)]
    if g == 1:
        k.dma(Zf[:], DI["zdn"][l])
        k.copy(Zb[:], Zf[:])

    def bc4(t, kind, n, r0, w):
        return AV(t.h[:, kind, n, r0:r0 + 4].unsqueeze(2).to_broadcast([128, 4, w]), t)

    def v4(t, n, w):
        return AV(t.h[:, n, :].rearrange("p (h d) -> p h d", h=4), t)

    def prep(d, n, slot):
        tok = slice(n * 128, (n + 1) * 128)
        r0 = d * 4
        ms, msT, miT = (0, 1, 3) if d == 0 else (1, 0, 2)
        bx = [nb(), nb(), nb()]
        for kind in range(3):
            k.mmq([(bx[kind][:, h * 128:(h + 1) * 128], SELt[:, r0 + h, :], ROWS[:, kind, tok]) for h in range(4)])
        for h in range(4):
            r = r0 + h
            hs = slice(h * 128, (h + 1) * 128)
            k.stt(EX[:, 0, h, :], bx[0][:, hs], COLS[:, cGH, n, r:r + 1], NM[:, ms, :], ALU.add, ALU.add)
            k.stt(EX[:, 1, h, :], bx[1][:, hs], COLS[:, cG, n, r:r + 1], NM[:, msT, :], ALU.subtract, ALU.add)
            k.stt(EX[:, 2, h, :], bx[2][:, hs], COLS[:, cG, n, r:r + 1], NM[:, miT, :], ALU.subtract, ALU.add)
        k.activation(Dm[:], EX[:], AF.Exp)
        bG, bQ = nb(), nb()
        HH = [(h, h // 2, (h % 2) * 64) for h in range(4)]
        k.mmq([(bG[:, h * 128:(h + 1) * 128], KT[hb:hb + 64, hp, tok], KT[hb:hb + 64, hp, tok]) for h, hp, hb in HH])
        k.mmq([(bQ[:, h * 128:(h + 1) * 128], KT[hb:hb + 64, hp, tok], QT[hb:hb + 64, hp, tok]) for h, hp, hb in HH])
        b4 = lambda b: AV(b.h[:, :].rearrange("p (h n) -> p h n", h=4), b)
        N0, B0, Y0 = NBYs[slot]
        k.stt(N0[:], b4(bG), -1.0, Dm[:, 0, :, :], ALU.mult, ALU.mult)
        k.stt(B0[:], b4(bG), -1.0, Dm[:, 1, :, :], ALU.mult, ALU.mult)
        k.tt(QKT[slot][:], b4(bQ), Dm[:, 2, :, :], ALU.mult)
        k.tt(Y0[:, :, 0:64], v4(Vt, n, 64), bc4(COLS, cBET, n, r0, 64), ALU.mult)
        k.tt(Y0[:, :, 64:128], v4(Kt, n, 64), bc4(COLS, cEGH, n, r0, 64), ALU.mult)
        k.tt(KDEC[slot][:], v4(Kt, n, 64), bc4(COLS, cED, n, r0, 64), ALU.mult)
        wts = [(WT[slot][(h % 2) * 64:(h % 2) * 64 + 64, h // 2, :], (h % 2) * 64) for h in range(4)]
        if "A" in M.dbg and g == 0 and n == 0 and d == 0:
            P.dump("A_N0", N0[:], [128, 4, 128], BF16)
            P.dump("A_B0", B0[:], [128, 4, 128], BF16)
            P.dump("A_Y0", Y0[:], [128, 4, 128], BF16)
            P.dump("A_QKT", QKT[slot][:], [128, 4, 128], BF16)
        return tri_solve_g(M, N0, B0, Y0, SVs[slot], 4, U[slot], wts)
        if "A" in M.dbg and g == 0 and n == 0 and d == 0:
            P.dump("A_U", U[slot][:], [128, 4, 64])
            P.dump("A_WT", WT[slot][:], [128, 2, 128], BF16)

    def step(d, n, slot):
        tok = slice(n * 128, (n + 1) * 128)
        r0 = d * 4
        pv = nb()
        HH = [(h, h // 2, (h % 2) * 64) for h in range(4)]
        k.mmq([(pv[:, h * 64:(h + 1) * 64], WT[slot][hb:hb + 64, hp, :], Zb[hb:hb + 64, d, hp, :]) for h, hp, hb in HH])
        p4 = lambda b: AV(b.h[:, 0:256].rearrange("p (h n) -> p h n", h=4), b)
        vn = VNs[slot]
        k.tt(vn[:], U[slot][:], p4(pv), ALU.subtract)
        yield
        po1, po2 = nb(), nb()
        k.mmq([(po1[:, h * 64:(h + 1) * 64], QT[hb:hb + 64, hp, tok], Zb[hb:hb + 64, d, hp, :]) for h, hp, hb in HH])
        k.mmq([(po2[:, h * 64:(h + 1) * 64], QKT[slot][:, h, :], vn[:, h, :]) for h, hp, hb in HH])
        t1 = T1s[slot]
        k.tt(t1[:], p4(po1), bc4(COLS, cEG, n, r0, 64), ALU.mult)
        k.tt(t1[:], t1[:], p4(po2), ALU.add)
        k.tt(v4(O, n, 64), v4(O, n, 64), t1[:], ALU.add)
        pz = nb()
        k.mmq([(pz[hb:hb + 64, hp * 64:(hp + 1) * 64], KDEC[slot][:, h, :], vn[:, h, :]) for h, hp, hb in HH])
        yield
        for h in range(4):
            hp, hb = h // 2, (h % 2) * 64
            r = r0 + h
            k.stt(Zf[hb:hb + 64, d, hp, :], Zf[hb:hb + 64, d, hp, :], COLS[hb:hb + 64, cEGT, n, r:r + 1],
                  pz[hb:hb + 64, hp * 64:(hp + 1) * 64], ALU.mult, ALU.add)
        k.copy(Zb[:, d, :, :], Zf[:, d, :, :], eng=k.act)

    cps = 8 // nseq
    for s_ in range(nseq):
        if g == 0:
            k.memset(Zf[:], 0.0)
            k.memset(Zb[:], 0.0)
        for i in range(cps):
            nf = s_ * cps + i
            nbk = s_ * cps + (cps - 1 - i)
            g0 = prep(0, nf, 0)
            g1 = prep(1, nbk, 1)
            interleave([g0, g1])
            interleave([step(0, nf, 0), step(1, nbk, 1)])
        if g == 0:
            for d in range(2):
                k.dma(DO["o_dn"][l, d, s_], Zf[:, d, :, :], is_out=True)
    sc2.close()
    OTF = sc.sb([128, 2, 1024], F32, "aotf")
    for i in range(8):
        b = nb()
        k.trq([(b[:, c * 128:(c + 1) * 128], O[:, i, c * 128:(c + 1) * 128], IDf) for c in range(2)])
        k.copy(OTF[:, :, i * 128:(i + 1) * 128], AV(b.h[:, 0:256].rearrange("p (c n) -> p c n", c=2), b),
               eng=(k.act if i % 2 else k.dve))
    for c in range(2):
        for tt in range(2):
            k.activation(M.SQ[0][:], OTF[:, c, TS[tt]], AF.Square)
            b = nb()
            k.mm(b[:], BDb[:], M.SQ[0][:])
            k.activation(M.RT[:], b[:], AF.Sqrt, bias=1e-6, scale=1.0 / 64)
            k.recip(M.RS[:], M.RT[:])
            k.stt(M.TMP[0][:], OTF[:, c, TS[tt]], VEC[:, 78:79], M.RS[:], ALU.mult, ALU.mult)
            k.tt(OT[:, c, TS[tt]], M.TMP[0][:], GS[:, c, TS[tt]], ALU.mult)
    sc.close()


def mixer_B(M):
    P, l, g, H1, OT = M.P, M.l, M.g, M.H1, M.OT
    k = P.k
    nb, wchunk, DI, DO, TS, bfh = P.nb, P.wchunk, P.DI, P.DO, P.TS, P.bfh
    C = P.consts
    IDb, BDb, IDf, BDf, CF = C["IDb"], C["BDb"], C["IDf"], C["BDf"], C["CF"]
    VEC = P.VEC
    nseq = 4 if g == 0 else 1
    Ls = 1024 // nseq
    cps = 8 // nseq
    sc = Scope(k)
    BON = sc.sb([128, 2, 1024], BF16, "bbon")
    Yacc = sc.sb([128, 8, 256], F32, "byacc")
    TW = sc.sb([128, 1024], BF16, "btw")
    AL = sc.sb([64, 1024], BF16, "bal")
    SGL = sc.sb([128, 1024], BF16, "bsgl")
    RWM = sc.sb([128, 3, 256], BF16, "brwm")
    Zf = sc.sb([128, 2, 2, 64], F32, "bzf")
    C0 = sc.sb([128, 9], F32, "bc0")
    OMK = sc.sb([128, 2], F32, "bomk")
    RST = sc.sb([128, 1024], BF16, "brst")
    RAWh = [None]
    k.dma(RWM[:], DI["rwm"][l], eng=k.pool)
    k.memset(Yacc[:], 0.0)
    k.memset(RST[:], 1.0)
    k.memset(AV(RST.h[:, 0:1024:128], RST), 0.0)
    mu = lambda i, q: VEC[:, 79 + 2 * i + q:79 + 2 * i + q + 1]
    k.ts(C0[:], AV(VEC.h[:, 79:97:2], VEC), -1.0, ALU.mult, 1.0, ALU.add)
    k.tt(C0[:], C0[:], AV(VEC.h[:, 80:98:2], VEC), ALU.subtract)
    k.ts(OMK[:], VEC[:, 105:107], -1.0, ALU.mult, 1.0, ALU.add)
    if g == 1:
        k.dma(Zf[:], DI["zrw"][l])

    def v3(t, a, b, rows=128):
        return AV(t.h[0:rows, :].rearrange("p (s t) -> p s t", s=nseq)[:, :, a:b], t)

    def proj_shift(ci, rows, dst):
        i = ci - 9
        s = wchunk(DI["win_c"][l, ci])
        bs = M.pj(s, 0, rows)
        RAW = RAWh[0]
        for tt in range(2):
            k.copy(RAW[0:rows, TS[tt]], bs[tt][0:rows, :], eng=(k.act if tt else k.dve))
        k.ts(dst, RAW[0:rows, :], C0[0:rows, i:i + 1], ALU.mult)
        dt = dst.t
        k.stt(v3(dt, 1, Ls, rows), v3(RAW, 0, Ls - 1, rows), AV(mu(i, 0).ap[0:rows], VEC), v3(dt, 1, Ls, rows), ALU.mult, ALU.add)
        k.stt(v3(dt, 0, Ls - 1, rows), v3(RAW, 1, Ls, rows), AV(mu(i, 1).ap[0:rows], VEC), v3(dt, 0, Ls - 1, rows), ALU.mult, ALU.add)
    scs = Scope(k)
    RAWh[0] = scs.sb([128, 1024], F32, "braw")
    SH = scs.sb([128, 1024], F32, "bsh")
    proj_shift(15, 128, SH[:])
    k.activation(TW[:], SH[:], AF.Tanh)
    proj_shift(16, 64, SH[0:64, :])
    k.copy(AL[:], SH[0:64, :])
    proj_shift(17, 128, SH[:])
    k.activation(SGL[:], SH[:], AF.Sigmoid)
    scs.close()
    for hp in range(2):
        sch = Scope(k)
        R = sch.sb([128, 1024], F32, "br")
        Kx = sch.sb([128, 1024], F32, "bk")
        A_ = sch.sb([128, 1024], F32, "ba")
        KK = sch.sb([128, 1024], F32, "bkk")
        VTt = sch.sb([128, 8, 128], BF16, "bvtt")
        scv = Scope(k)
        RAWh[0] = scv.sb([128, 1024], F32, "braw")
        V = scv.sb([128, 1024], F32, "bv")
        proj_shift(9 + hp, 128, R[:])
        proj_shift(11 + hp, 128, Kx[:])
        proj_shift(13 + hp, 128, V[:])
        for tt in range(2):
            b = nb()
            k.mm(b[:], RWM[0:64, 1, hp * 128:(hp + 1) * 128], AL[0:64, TS[tt]])
            k.activation(A_[:, TS[tt]], b[:], AF.Sigmoid, bias=VEC[:, 101 + hp:102 + hp])
        k.ts(KK[:], Kx[:], VEC[:, 103 + hp:104 + hp], ALU.mult)
        for tt in range(2):
            k.activation(M.SQ[0][:], KK[:, TS[tt]], AF.Square)
            b = nb()
            k.mm(b[:], BDb[:], M.SQ[0][:])
            k.activation(M.RT[:], b[:], AF.Sqrt, bias=1e-6, scale=1.0)
            k.recip(M.RS[:], M.RT[:])
            k.tt(KK[:, TS[tt]], KK[:, TS[tt]], M.RS[:], ALU.mult)
        for tt in range(2):
            k.ts(M.TMP[0][:], A_[:, TS[tt]], VEC[:, 105 + hp:106 + hp], ALU.mult, OMK[:, hp:hp + 1], ALU.add)
            k.tt(Kx[:, TS[tt]], Kx[:, TS[tt]], M.TMP[0][:], ALU.mult)
            k.tt(A_[:, TS[tt]], KK[:, TS[tt]], A_[:, TS[tt]], ALU.mult)
            k.stt(M.SQ[1][:], R[:, TS[tt]], VEC[:, 107 + hp:108 + hp], Kx[:, TS[tt]], ALU.mult, ALU.mult)
            b = nb()
            k.mm(b[:], BDb[:], M.SQ[1][:])
            k.tt(BON[:, hp, TS[tt]], b[:], V[:, TS[tt]], ALU.mult)
        for i0 in range(0, 8, 4):
            b = nb()
            k.trq([(b[:, ii * 128:(ii + 1) * 128], V[:, (i0 + ii) * 128:(i0 + ii + 1) * 128], IDf) for ii in range(4)])
            k.copy(VTt[:, i0:i0 + 4, :], AV(b.h[:, :].rearrange("p (i n) -> p i n", i=4), b), eng=(k.act if i0 else k.dve))
        scv.close()
        for d in range(2):
            scd = Scope(k)
            RT_ = scd.sb([128, 1024], BF16, "brt")
            KT_ = scd.sb([128, 1024], BF16, "bkt")
            BT_ = scd.sb([128, 1024], BF16, "bbt")
            CT_ = scd.sb([128, 1024], BF16, "bct")
            KTt = scd.sb([128, 8, 128], BF16, "bktt")
            BTt = scd.sb([128, 8, 128], BF16, "bbtt")
            REF = scd.sb([128, 8], F32, "bref")
            NB1 = scd.sb([128, 8], F32, "bnb1")
            NB2 = scd.sb([128, 8], F32, "bnb2")
            ER = scd.sb([128, 8], F32, "ber")
            EEND = scd.sb([128, 8], F32, "beend")
            TOT = scd.sb([128, 8], F32, "btot")
            sce = Scope(k)
            SIG = sce.sb([128, 1024], F32, "bsig")
            CUM = sce.sb([128, 1024], F32, "bcum")
            EB = sce.sb([128, 1024], F32, "beb")
            for tt in range(2):
                b = nb()
                k.mm(b[:], RWM[d * 64:(d + 1) * 64, 0, hp * 128:(hp + 1) * 128], TW[d * 64:(d + 1) * 64, TS[tt]])
                k.activation(SIG[:, TS[tt]], b[:], AF.Sigmoid, bias=VEC[:, 97 + d * 2 + hp:98 + d * 2 + hp])
            k.scan(CUM[:], RST[:], SIG[:], 0.0, ALU.mult, ALU.add)
            c3 = lambda t: AV(t.h[:, :].rearrange("p (c n) -> p c n", c=8), t)
            if d == 1:
                k.copy(TOT[:], AV(CUM.h[:, 127:1024:128], CUM))
                k.tt(c3(CUM), c3(SIG), c3(CUM), ALU.subtract)
                k.tt(c3(CUM), c3(CUM), AV(TOT.h[:, :].unsqueeze(2).to_broadcast([128, 8, 128]), TOT), ALU.add)
            k.tt(SIG[:], CUM[:], SIG[:], ALU.subtract)
            k.copy(REF[:], AV(CUM.h[:, 64:1024:128], CUM))
            k.ts(NB1[:], REF[:], WC, ALU.mult)
            k.ts(NB2[:], REF[:], -WC, ALU.mult)
            k.activation(ER[:], REF[:], AF.Exp, scale=-WC)
            cs = lambda n: slice(n * 128, (n + 1) * 128)
            for n in range(8):
                k.activation(EB[:, cs(n)], CUM[:, cs(n)], AF.Exp, bias=NB1[:, n:n + 1], scale=-WC)
            k.tt(RT_[:], R[:], EB[:], ALU.mult)
            endc = 127 if d == 0 else 0
            k.copy(EEND[:], AV(EB.h[:, endc:1024:128], EB))
            for n in range(8):
                k.activation(EB[:, cs(n)], CUM[:, cs(n)], AF.Exp, bias=NB2[:, n:n + 1], scale=WC)
            k.tt(KT_[:], Kx[:], EB[:], ALU.mult)
            k.tt(BT_[:], A_[:], EB[:], ALU.mult)
            for n in range(8):
                k.activation(EB[:, cs(n)], SIG[:, cs(n)], AF.Exp, bias=NB1[:, n:n + 1], scale=-WC)
            k.tt(CT_[:], KK[:], EB[:], ALU.mult)
            for (src_, dst_, neg) in ((KT_, KTt, False), (BT_, BTt, True)):
                for i0 in range(0, 8, 4):
                    b = nb()
                    bh_ = bfh(b)
                    k.trq([(AV(bh_[:, ii * 128:(ii + 1) * 128], b), src_[:, (i0 + ii) * 128:(i0 + ii + 1) * 128], IDb[:]) for ii in range(4)])
                    srcv = AV(bh_[:, 0:512].rearrange("p (i n) -> p i n", i=4), b)
                    if neg:
                        k.ts(dst_[:, i0:i0 + 4, :], srcv, -1.0, ALU.mult)
                    else:
                        k.copy(dst_[:, i0:i0 + 4, :], srcv, eng=k.act)
            sce.close()
            scc = Scope(k)
            NBY = [scc.sb([128, 4, 128], BF16, "bnby") for _ in range(3)]
            SV = [scc.sb([128, 4, 128], BF16, "bsv") for _ in range(11)]
            LkT = scc.sb([128, 4, 128], BF16, "blkt")
            G1Ts = [scc.sb([128, 4, 128], BF16, "bg1t") for _ in range(2)]
            G2Ts = [scc.sb([128, 4, 128], BF16, "bg2t") for _ in range(2)]
            Us = [scc.sb([128, 4, 64], F32, "bu") for _ in range(2)]
            WTs = [scc.sb([128, 2, 128], BF16, "bwt") for _ in range(2)]
            Pm = scc.sb([128, 2, 64], BF16, "bpm")
            Z0f = scc.sb([128, 64], F32, "bz0f")
            Z0b = scc.sb([128, 64], BF16, "bz0b")
            ms, msT, miT = (1, 2, 4) if d == 0 else (2, 1, 3)
            mkk = lambda i: AV(CF.h[:, i, :].unsqueeze(1).to_broadcast([128, 4, 128]), CF)
            b4 = lambda b: AV(b.h[:, 0:512].rearrange("p (q n) -> p q n", q=4), b)
            work = []
            for s_ in range(nseq):
                for i in range(0, cps, 2):
                    pair = [s_ * cps + (ii if d == 0 else cps - 1 - ii) for ii in (i, i + 1)]
                    work.append((s_, pair, i == 0, i + 2 >= cps))

            def prepsolve(wi):
                s_, pair, first, last_ = work[wi]
                par = wi % 2
                G1T, G2T, U, WT = G1Ts[par], G2Ts[par], Us[par], WTs[par]
                N0, B0, Y0 = NBY
                bN, bB, bL, bG1, bG2 = nb(), nb(), nb(), nb(), nb()
                PQ = [(ci * 2 + q, cs(n), q * 64) for ci, n in enumerate(pair) for q in range(2)]
                sl_ = lambda t, tok, hb: t[hb:hb + 64, tok]
                for bk_, ta, tb in ((bN, CT_, BT_), (bB, BT_, CT_), (bL, KT_, CT_), (bG1, KT_, RT_), (bG2, BT_, RT_)):
                    k.mmq([(bk_[:, p_ * 128:(p_ + 1) * 128], sl_(ta, tok, hb), sl_(tb, tok, hb)) for p_, tok, hb in PQ])
                k.stt(N0[:], b4(bN), -1.0, mkk(ms), ALU.mult, ALU.mult)
                k.stt(B0[:], b4(bB), -1.0, mkk(msT), ALU.mult, ALU.mult)
                k.tt(LkT[:], b4(bL), mkk(msT), ALU.mult)
                yield
                k.tt(G1T[:], b4(bG1), mkk(miT), ALU.mult)
                k.stt(G2T[:], b4(bG2), -1.0, mkk(miT), ALU.mult, ALU.mult)
                bt = nb()
                bth = bfh(bt)
                k.trq([(AV(bth[:, ci * 128:(ci + 1) * 128], bt), CT_[:, cs(n)], IDb[:]) for ci, n in enumerate(pair)])
                k.copy(Y0[:, :, 64:128], AV(bth[:, 0:256].rearrange("p (q n) -> p q n", q=4), bt), eng=k.act)
                by = nb()
                k.mmq([(by[:, (ci * 2 + q) * 64:(ci * 2 + q + 1) * 64], LkT[:, ci * 2 + q, :], VTt[:, n, q * 64:(q + 1) * 64])
                       for ci, n in enumerate(pair) for q in range(2)])
                k.copy(Y0[:, :, 0:64], AV(by.h[:, 0:256].rearrange("p (q n) -> p q n", q=4), by))
                yield
                wts = [(WT[(p_ % 2) * 64:(p_ % 2) * 64 + 64, p_ // 2, :], (p_ % 2) * 64) for p_ in range(4)]
                yield from tri_solve_g(M, N0, B0, Y0, SV, 4, U, wts)

            def seqpart(wi):
                s_, pair, first, last_ = work[wi]
                par = wi % 2
                G1T, G2T, U, WT = G1Ts[par], G2Ts[par], Us[par], WTs[par]
                if g == 0 and first:
                    k.memset(Zf[:, d, hp, :], 0.0)
                for ci, n in enumerate(pair):
                    tok = cs(n)
                    k.ts(Z0f[:], Zf[:, d, hp, :], ER[:, n:n + 1], ALU.mult)
                    k.copy(Z0b[:], Z0f[:], eng=k.act)
                    bp = nb()
                    k.mmq([(bp[:, q * 64:(q + 1) * 64], WT[q * 64:q * 64 + 64, ci, :], Z0b[q * 64:q * 64 + 64, :]) for q in range(2)])
                    yield
                    k.tt(Pm[:], U[:, ci * 2:ci * 2 + 2, :], AV(bp.h[:, 0:128].rearrange("p (q n) -> p q n", q=2), bp), ALU.add)
                    bo = nb()
                    for q in range(2):
                        hb = q * 64
                        p_ = ci * 2 + q
                        o = bo[:, q * 64:(q + 1) * 64]
                        k.mm1(o, RT_[hb:hb + 64, tok], Z0b[hb:hb + 64, :], True, False)
                        k.mm1(o, G1T[:, p_, :], VTt[:, n, q * 64:(q + 1) * 64], False, False)
                        k.mm1(o, G2T[:, p_, :], Pm[:, q, :], False, True)
                    k.tt(Yacc[:, n, hp * 128:(hp + 1) * 128], Yacc[:, n, hp * 128:(hp + 1) * 128], bo[:, 0:128], ALU.add)
                    bz = nb()
                    for q in range(2):
                        hb = q * 64
                        o = bz[hb:hb + 64, 0:64]
                        k.mm1(o, KTt[:, n, q * 64:(q + 1) * 64], VTt[:, n, q * 64:(q + 1) * 64], True, False)
                        k.mm1(o, BTt[:, n, q * 64:(q + 1) * 64], Pm[:, q, :], False, True)
                    yield
                    k.tt(Z0f[:], Z0f[:], bz[:, 0:64], ALU.add)
                    k.ts(Zf[:, d, hp, :], Z0f[:], EEND[:, n:n + 1], ALU.mult)
                    yield
                if g == 0 and last_:
                    k.dma(DO["o_rw"][l, d, s_, :, hp, :], Zf[:, d, hp, :], is_out=True)

            interleave([prepsolve(0)])
            for wi in range(len(work)):
                gl_ = [seqpart(wi)]
                if wi + 1 < len(work):
                    gl_.append(prepsolve(wi + 1))
                interleave(gl_)
            scc.close()
            scd.close()
        sch.close()
    OTF = sc.sb([128, 2, 1024], F32, "botf")
    for i in range(8):
        b = nb()
        k.trq([(b[:, c * 128:(c + 1) * 128], Yacc[:, i, c * 128:(c + 1) * 128], IDf) for c in range(2)])
        k.copy(OTF[:, :, i * 128:(i + 1) * 128], AV(b.h[:, 0:256].rearrange("p (c n) -> p c n", c=2), b),
               eng=(k.act if i % 2 else k.dve))
    MEAN = sc.sb([128, 512], F32, "bmean")
    for hp in range(2):
        for tt in range(2):
            y = OTF[:, hp, TS[tt]]
            b = nb()
            k.mm(b[:], BDf, y)
            k.ts(MEAN[:], b[:], 1.0 / 64, ALU.mult)
            k.tt(y, y, MEAN[:], ALU.subtract)
            k.activation(M.TMP[0][:], y, AF.Square)
            b = nb()
            k.mm(b[:], BDf, M.TMP[0][:])
            k.activation(M.RT[:], b[:], AF.Sqrt, bias=64e-5, scale=1.0 / 64)
            k.recip(M.RS[:], M.RT[:])
            k.stt(M.TMP[0][:], y, VEC[:, 109 + hp:110 + hp], M.RS[:], ALU.mult, ALU.mult)
            k.ts(M.TMP[0][:], M.TMP[0][:], VEC[:, 111 + hp:112 + hp], ALU.add)
            k.tt(M.TMP[0][:], M.TMP[0][:], BON[:, hp, TS[tt]], ALU.add)
            b = nb()
            k.mm(b[:], RWM[:, 2, hp * 128:(hp + 1) * 128], SGL[:, TS[tt]])
            k.tt(OT[:, hp, TS[tt]], M.TMP[0][:], b[:], ALU.mult)
    sc.close()


def mixer_C(M):
    P, l, g, H1, OT = M.P, M.l, M.g, M.H1, M.OT
    k = P.k
    nb, wchunk, DI, DO, TS, bfh = P.nb, P.wchunk, P.DI, P.DO, P.TS, P.bfh
    IDb, BDb, IDf = P.consts["IDb"], P.consts["BDb"], P.consts["IDf"]
    VEC = P.VEC
    lam_init = 0.8 - 0.6 * math.exp(-0.3 * l)
    SCL = 32 ** -0.5
    sc = Scope(k)
    QT = sc.sb([64, 4, 1024], BF16, "cq")
    KT = sc.sb([64, 4, 1024], BF16, "ck")
    NLA = sc.sb([128, 1], F32, "nla")
    DNC = sc.sb([128, 1], F32, "dnc")
    V = sc.sb([128, 8, 256], BF16, "cv")
    O = sc.sb([128, 8, 256], F32, "co")
    sc_outer = sc
    sc = Scope(k)
    if g == 0:
        KF = [sc.sb([64, 512], F32, "ckf") for _ in range(2)]
        VF = [sc.sb([128, 256], F32, "cvf") for _ in range(2)]
    LAM = sc.sb([128, 128], F32, "lam")
    PR = sc.sb([128, 2, 32], F32, "lpr")
    S2 = sc.sb([128, 2], F32, "ls2")
    E2 = sc.sb([128, 2], F32, "le2")
    k.dma(LAM[:], DI["lamv"][l])
    k.tt(PR[:, 0, :], LAM[:, 0:32], LAM[:, 32:64], ALU.mult)
    k.tt(PR[:, 1, :], LAM[:, 64:96], LAM[:, 96:128], ALU.mult)
    k.reduce(S2[:], PR[:], ALU.add)
    k.activation(E2[:], S2[:], AF.Exp)
    k.tt(NLA[:], E2[:, 1:2], E2[:, 0:1], ALU.subtract)
    k.ts(NLA[:], NLA[:], -lam_init, ALU.add)
    k.ts(DNC[:], VEC[:, 113:114], 1.0 - lam_init, ALU.mult)
    if g == 1:
        ROPE = sc.sb([64, 2, 1024], F32, "rope")
        k.dma(ROPE[:], DI["rope"][:])
        R1 = [sc.sb([64, 512], F32, "r1") for _ in range(2)]
        R2 = [sc.sb([64, 512], F32, "r2") for _ in range(2)]
    for qk in range(2):
        dst = QT if qk == 0 else KT
        for hp in range(2):
            sx = wchunk(DI["win_c"][l, 18 + 2 * qk + hp])
            if g == 1:
                sy = wchunk(DI["win_c"][l, 24 + 2 * qk + hp])
            for hh in range(2):
                h = 2 * hp + hh
                bx = M.pj(sx, hh * 64, 64)
                if g == 1:
                    by = M.pj(sy, hh * 64, 64)
                for tt in range(2):
                    if g == 0:
                        if qk == 0:
                            k.copy(dst[:, h, TS[tt]], bx[tt][0:64, :], eng=k.act)
                        else:
                            k.copy(KF[tt][:], bx[tt][0:64, :])
                            k.dma(DO["o_dk"][l, h, :, TS[tt]], KF[tt][:], is_out=True)
                            k.copy(dst[:, h, TS[tt]], KF[tt][:], eng=k.act)
                    else:
                        k.tt(R1[tt][:], bx[tt][0:64, :], ROPE[:, 0, TS[tt]], ALU.mult)
                        k.tt(R2[tt][:], by[tt][0:64, :], ROPE[:, 1, TS[tt]], ALU.mult)
                        k.tt(dst[:, h, TS[tt]], R1[tt][:], R2[tt][:], ALU.add)
    for i, b in M.proj_tm([22, 23], 128):
        if g == 0:
            k.copy(VF[i % 2][:], b[:, 0:256])
            k.dma(DO["o_dv"][l, :, i, :], VF[i % 2][:], is_out=True)
            k.copy(V[:, i, :], VF[i % 2][:], eng=k.act)
        else:
            k.copy(V[:, i, :], b[:, 0:256], eng=(k.act if i % 2 else k.dve))
    sc.close()
    sc = Scope(k)
    MX = sc.sb([128, 4], F32, "cmx")
    MX1 = sc.sb([128, 2], F32, "cmx1")
    NMX = sc.sb([128, 2], F32, "cnmx")
    SUM = sc.sb([128, 2, 4], F32, "csum")
    SUM1 = sc.sb([128, 2], F32, "csum1")
    RS2 = sc.sb([128, 2], F32, "crs")
    C1 = sc.sb([128, 1], F32, "cc1")
    if g == 0:
        NKEY = 256
    else:
        NKEY = 1280
        KcT = sc.sb([64, 4, 256], BF16, "ckc")
        Vc = sc.sb([128, 2, 256], BF16, "cvc")
        k.dma(KcT[:], DI["cdk"][l], eng=k.pool)
        k.dma(Vc[:], DI["cdv"][l], eng=k.pool)
    NCH = NKEY // 128
    Pm = [sc.sb([128, 2, NKEY], BF16, "cp") for _ in range(2)]
    W = [sc.sb([128, NKEY], BF16, "cw")]
    PT = [sc.sb([128, NCH, 128], BF16, "cpt")]
    SUM1s = [sc.sb([128, 2], F32, "csum1b") for _ in range(2)]
    units = []
    u = 0
    for ti in range(8):
        for h in range(4):
            def pa(u=u, ti=ti, h=h):
                qsl = slice(ti * 128, (ti + 1) * 128)
                s_ = ti // 2
                pm = Pm[u % 2]
                for m in range(2):
                    q = QT[m * 32:(m + 1) * 32, h, qsl]
                    if g == 0:
                        b = nb()
                        segs = [(b, 0, 256, KT[m * 32:(m + 1) * 32, h, s_ * 256:(s_ + 1) * 256])]
                    else:
                        ba, bb, bc = nb(), nb(), nb()
                        segs = [(ba, 0, 512, KT[m * 32:(m + 1) * 32, h, 0:512]),
                                (bb, 512, 512, KT[m * 32:(m + 1) * 32, h, 512:1024]),
                                (bc, 1024, 256, KcT[m * 32:(m + 1) * 32, h, :])]
                    for si, (bk_, off, n, kop) in enumerate(segs):
                        k.mm(bk_[:, 0:n], q, kop)
                        k.reduce(MX[:, si:si + 1], bk_[:, 0:n], ALU.max)
                    if len(segs) > 1:
                        k.reduce(MX1[:, m:m + 1], MX[:, 0:len(segs)], ALU.max)
                        k.ts(NMX[:, m:m + 1], MX1[:, m:m + 1], -SCL, ALU.mult)
                    else:
                        k.ts(NMX[:, m:m + 1], MX[:, 0:1], -SCL, ALU.mult)
                    for si, (bk_, off, n, kop) in enumerate(segs):
                        k.activation(pm[:, m, off:off + n], bk_[:, 0:n], AF.Exp, bias=NMX[:, m:m + 1], scale=SCL,
                                     accum_out=SUM[:, m, si:si + 1])
                    if len(segs) > 1:
                        k.reduce(SUM1s[u % 2][:, m:m + 1], SUM[:, m, 0:len(segs)], ALU.add)
                    else:
                        k.copy(SUM1s[u % 2][:, m:m + 1], SUM[:, m, 0:1])

            def pb_(u=u, ti=ti, h=h):
                s_ = ti // 2
                pm = Pm[u % 2]
                k.recip(RS2[:], SUM1s[u % 2][:])
                k.tt(C1[:], RS2[:, 1:2], NLA[:], ALU.mult)
                w = W[0]
                k.ts(w[:], pm[:, 1, :], C1[:, 0:1], ALU.mult)
                k.stt(w[:], pm[:, 0, :], RS2[:, 0:1], w[:], ALU.mult, ALU.add)
                pt = PT[0]
                for c0 in range(0, NCH, 8):
                    nchunk = min(8, NCH - c0)
                    bt = nb()
                    bth = bfh(bt)
                    k.trq([(AV(bth[:, c * 128:(c + 1) * 128], bt), w[:, (c0 + c) * 128:(c0 + c + 1) * 128], IDb[:]) for c in range(nchunk)])
                    k.copy(pt[:, c0:c0 + nchunk, :], AV(bth[:, 0:nchunk * 128].rearrange("p (j n) -> p j n", j=nchunk), bt),
                           eng=(k.act if (c0 // 8) % 2 == 0 else k.dve))
                po = nb()
                if g == 0:
                    prs = [(pt[:, c, :], V[:, 2 * s_ + c, h * 64:(h + 1) * 64]) for c in range(2)]
                else:
                    prs = [(pt[:, c, :], V[:, c, h * 64:(h + 1) * 64]) for c in range(8)] + \
                          [(pt[:, 8 + c, :], Vc[:, c, h * 64:(h + 1) * 64]) for c in range(2)]
                k.mmg(po[:, 0:64], prs)
                k.copy(O[:, ti, h * 64:(h + 1) * 64], po[:, 0:64], eng=k.act)
            units.append((pa, pb_))
            u += 1
    skewed(units)
    sc.close()
    sc = sc_outer
    OTF = sc.sb([128, 2, 1024], F32, "cotf")
    for i in range(8):
        b = nb()
        k.trq([(b[:, c * 128:(c + 1) * 128], O[:, i, c * 128:(c + 1) * 128], IDf) for c in range(2)])
        k.copy(OTF[:, :, i * 128:(i + 1) * 128], AV(b.h[:, 0:256].rearrange("p (c n) -> p c n", c=2), b),
               eng=(k.act if i % 2 else k.dve))
    for c in range(2):
        for tt in range(2):
            k.activation(M.SQ[0][:], OTF[:, c, TS[tt]], AF.Square)
            b = nb()
            k.mm(b[:], BDb[:], M.SQ[0][:])
            k.activation(M.RT[:], b[:], AF.Sqrt, bias=1e-6, scale=1.0 / 64)
            k.recip(M.RS[:], M.RT[:])
            k.stt(OT[:, c, TS[tt]], OTF[:, c, TS[tt]], DNC[:, 0:1], M.RS[:], ALU.mult, ALU.mult)
    sc.close()


def assemble(outs):
    f32 = np.float32
    y_prompt = np.zeros((32, 256, 1024), f32)
    y_sample = np.zeros((4, 1024, 1024), f32)
    ndk = np.zeros((32, L_, 4, 256, 2, 32), f32)
    ndv = np.zeros((32, L_, 4, 256, 64), f32)
    nnk = np.zeros((32, L_, 4, 256, 64), f32)
    nnv = np.zeros((32, L_, 4, 256, 64), f32)
    dnf = np.zeros((32, L_, 4, 64, 64), f32)
    dnb = np.zeros((32, L_, 4, 64, 64), f32)
    rwf = np.zeros((32, L_, 4, 64, 64), f32)
    rwb = np.zeros((32, L_, 4, 64, 64), f32)
    for c, o in enumerate(outs):
        y = np.asarray(o["y_out"])
        sl = slice(4 * c, 4 * c + 4)
        y_prompt[sl] = y[:, :, :1024].transpose(2, 1, 0).reshape(4, 256, 1024)
        if c < 4:
            y_sample[c] = y[:, :, 1024:].transpose(2, 1, 0).reshape(1024, 1024)
        dk = np.asarray(o["o_dk"]).reshape(L_, 4, 2, 32, 4, 256)
        ndk[sl] = dk.transpose(4, 0, 1, 5, 2, 3)
        for name, dst in (("o_dv", ndv), ("o_nv", nnv)):
            v = np.asarray(o[name]).transpose(0, 2, 1, 3).reshape(L_, 4, 256, 4, 64)
            dst[sl] = v.transpose(1, 0, 3, 2, 4)
        nk = np.asarray(o["o_nk"]).reshape(L_, 2, 64, 2, 4, 256)
        nnk[sl] = nk.transpose(4, 0, 3, 1, 5, 2).reshape(4, L_, 4, 256, 64)
        dn = np.asarray(o["o_dn"]).reshape(L_, 2, 4, 2, 64, 2, 64)
        dn = dn.transpose(1, 2, 0, 5, 3, 4, 6).reshape(2, 4, L_, 4, 64, 64)
        dnf[sl], dnb[sl] = dn[0], dn[1]
        rw = np.asarray(o["o_rw"]).reshape(L_, 2, 4, 2, 64, 2, 64)
        rw = rw.transpose(1, 2, 0, 5, 3, 6, 4).reshape(2, 4, L_, 4, 64, 64)
        rwf[sl], rwb[sl] = rw[0], rw[1]
    return (y_prompt, y_sample, ndk, ndv, nnk, nnv, dnf, dnb, rwf, rwb)


def kernel(**inputs):
    inp = {k_: np.asarray(v, dtype=np.float32) for k_, v in inputs.items()}
    sh = prep_shared(inp)
    in_maps = []
    for c in range(8):
        d = dict(sh)
        d.update(prep_core(inp, c))
        in_maps.append(d)
    P = build({})
    res = run_bass_kernel_spmd(P.k.nc, in_maps, core_ids=list(range(8)))
    return assemble(res.results)
```

```python
import math
from contextlib import ExitStack
import numpy as np
import concourse.bass as bass
import concourse.mybir as mybir
from concourse.bass_utils import run_bass_kernel_spmd

F32 = mybir.dt.float32
BF16 = mybir.dt.bfloat16
AF = mybir.ActivationFunctionType
ALU = mybir.AluOpType
AX = mybir.AxisListType

EPOCH = 30000
NDMASEM = 20
NSLOT = 7
PE_SKIP_OWN = False
MMQ = True

L_ = 4
A0, B0, C0, D0 = 0, 1040, 2128, 2896
INC = 3664
NV = 114
NEGBIG = -60000.0
WC = 0.6065306597126334


class Tile:
    def __init__(self, h, name, track=True):
        self.h = h
        self.name = name
        self.wl = []
        self.rl = []
        self.track = track
        self.scoped = False
        self.psum = False
        self.born = 0

    def __getitem__(self, idx):
        return AV(self.h[idx], self)

    def v(self, fn):
        return AV(fn(self.h), self)


class AV:
    def __init__(self, ap, t):
        self.ap = ap
        self.t = t


class Eng:
    def __init__(self, K, name, e):
        self.K = K
        self.name = name
        self.e = e
        self.sems = []
        self.count = 0
        self.nidx = 0
        self.cnt_at = {}
        self.seen = {}
        self.pending = []
        self.wq = []

    def next_inc(self):
        k = self.count // EPOCH
        while len(self.sems) <= k:
            self.sems.append(self.K.nc.alloc_semaphore(f"s_{self.name}_{len(self.sems)}"))
        return self.sems[k], self.count % EPOCH + 1

    def issue(self, ins):
        idx = self.nidx
        self.nidx += 1
        need = self.K.needed
        if need is None or idx in need[self.name]:
            sem, val = self.next_inc()
            ins.then_inc(sem, 1)
            self.count += 1
            self.cnt_at[idx] = (sem, val)
        return ("e", self, idx)

    def last_tok(self):
        if self.nidx == 0:
            return None
        return ("e", self, self.nidx - 1)

    def wait(self, tok):
        if tok is None:
            return
        if tok[0] == "e":
            _, T, idx = tok
            if self.seen.get(T.name, -1) >= idx:
                return
            self.seen[T.name] = idx
            self.K.needed_rec[T.name].add(idx)
            self.wq.append(T.cnt_at[idx])
            return
        sem, val = tok
        key = id(sem)
        if self.seen.get(key, 0) >= val:
            return
        self.wq.append((sem, val))
        self.seen[key] = val

    def flush(self, keep_last=False):
        wq, self.wq = self.wq, []
        last = None
        if keep_last and wq:
            last = wq.pop()
        for sem, val in wq:
            self.e.wait_ge(sem, val)
            self.K.nwait += 1
        return last


WHOLE = (0, 1 << 30, 0, 1 << 30)
DT_BYTES = {F32: 4, BF16: 2}


def ovl(a, b):
    return a[0] < b[1] and b[0] < a[1] and a[2] < b[3] and b[2] < a[3]


def inside(a, b):
    return a[0] >= b[0] and a[1] <= b[1] and a[2] >= b[2] and a[3] <= b[3]


def merge_entries(ents):
    best = {}
    for bx, tk in ents:
        kk = tok_key(tk)
        if kk not in best:
            best[kk] = (bx, tk)
        else:
            b0, t0 = best[kk]
            ub = (min(b0[0], bx[0]), max(b0[1], bx[1]), min(b0[2], bx[2]), max(b0[3], bx[3]))
            best[kk] = (ub, tk if tok_ord(tk) > tok_ord(t0) else t0)
    return list(best.values())


def tok_key(tk):
    return (tk[1].name,) if tk[0] == "e" else (id(tk[0]),)


def tok_ord(tk):
    return tk[2] if tk[0] == "e" else tk[1]


def compress(toks):
    best = {}
    for tk in toks:
        kk = tok_key(tk)
        if kk not in best or tok_ord(best[kk]) < tok_ord(tk):
            best[kk] = tk
    return list(best.values())


class K:
    def __init__(self, needed=None):
        self.nc = bass.Bass("TRN2", target_bir_lowering=False)
        nc = self.nc
        self.needed = needed
        self.nwait = 0
        self.pe = Eng(self, "pe", nc.tensor)
        self.dve = Eng(self, "dve", nc.vector)
        self.act = Eng(self, "act", nc.scalar)
        self.pool = Eng(self, "pool", nc.gpsimd)
        self.sp = Eng(self, "sp", nc.sync)
        self.engs = [self.pe, self.dve, self.act, self.pool, self.sp]
        self.needed_rec = {e.name: set() for e in self.engs}
        self.dsems = {}
        for e in (self.pool, self.sp):
            self.dsems[e.name] = dict(sems=[nc.alloc_semaphore(f"dma_{e.name}{i}") for i in range(NDMASEM)],
                                      use=[0] * NDMASEM, nxt=0)
        self.out_toks = []
        self.ntile = 0
        self.ninstr = 0
        self.stamp = 0

    def sb(self, shape, dt=F32, name=None):
        self.ntile += 1
        name = f"{name or 't'}_{self.ntile}"
        return Tile(self.nc.alloc_sbuf_tensor(name, list(shape), dt), name)

    def ps(self, shape, dt=F32, name=None):
        self.ntile += 1
        name = f"{name or 'p'}_{self.ntile}"
        t = Tile(self.nc.alloc_psum_tensor(name, list(shape), dt), name)
        t.psum = True
        return t

    def dram(self, name, shape, dt, kind):
        return Tile(self.nc.dram_tensor(name, list(shape), dt, kind=kind), name, track=False)

    @staticmethod
    def box(a):
        if a.t.psum:
            return WHOLE
        ap = a.ap
        pairs = ap.ap
        esz = DT_BYTES[ap.dtype]
        row = pairs[0][0] * esz
        off = int(ap.offset) * esz
        if row <= 0:
            return WHOLE
        p0, f0 = off // row, off % row
        ext = esz
        for st, cnt in pairs[1:]:
            if st < 0:
                return WHOLE
            ext += (cnt - 1) * st * esz
        return (p0, p0 + pairs[0][1], f0, f0 + ext)

    def _deps(self, eng, reads, writes):
        skip = (eng is self.pe and PE_SKIP_OWN)

        def w(tk):
            if skip and tk[0] == "e" and tk[1] is eng:
                return
            eng.wait(tk)
        if eng.pending:
            born = 0
            for a in list(reads) + list(writes):
                if a is not None and a.t.scoped and a.t.born > born:
                    born = a.t.born
            if born:
                keep = []
                for st_, toks in eng.pending:
                    if st_ <= born:
                        for tk in toks:
                            eng.wait(tk)
                    else:
                        keep.append((st_, toks))
                eng.pending = keep
        for a in reads:
            if a is None or not a.t.track:
                continue
            bx = self.box(a)
            for b, tk in a.t.wl:
                if ovl(b, bx):
                    w(tk)
            if a.t.psum:
                for b, tk in a.t.rl:
                    if tk[0] == "e" and tk[1] is not eng:
                        w(tk)
        for a in writes:
            if not a.t.track:
                continue
            bx = self.box(a)
            for b, tk in a.t.wl:
                if ovl(b, bx):
                    w(tk)
            for b, tk in a.t.rl:
                if ovl(b, bx):
                    w(tk)

    def _commit(self, tok, reads, writes):
        for a in reads:
            if a is None or not a.t.track:
                continue
            a.t.rl.append((self.box(a), tok))
            if len(a.t.rl) > 40:
                a.t.rl = merge_entries(a.t.rl)
        for a in writes:
            if not a.t.track:
                continue
            bx = self.box(a)
            t = a.t
            t.wl = [e for e in t.wl if not inside(e[0], bx)] + [(bx, tok)]
            t.rl = [e for e in t.rl if not inside(e[0], bx)]
            if len(t.wl) > 40:
                t.wl = merge_entries(t.wl)

    def op(self, eng, fn, reads, writes):
        self._deps(eng, reads, writes)
        last = eng.flush(True)
        ins = fn()
        if last is not None:
            ins._wait_ge(last[0], last[1])
        self.ninstr += 1
        tok = eng.issue(ins)
        self._commit(tok, reads, writes)
        return tok

    def dma(self, out, in_, eng=None, is_out=False):
        eng = eng or self.sp
        d = self.dsems[eng.name]
        j = d["nxt"]
        d["nxt"] = (j + 1) % NDMASEM
        sem = d["sems"][j]
        if d["use"][j] > 0:
            eng.wait((sem, 16 * d["use"][j]))
        self._deps(eng, [in_], [out])
        last = eng.flush(True)
        ins = eng.e.dma_start(out=out.ap, in_=in_.ap)
        if last is not None:
            ins._wait_ge(last[0], last[1])
        self.ninstr += 1
        d["use"][j] += 1
        tok = (sem, 16 * d["use"][j])
        ins.then_inc(sem, 16)
        self._commit(tok, [in_], [out])
        if is_out:
            self.out_toks.append(tok)
        return tok

    def finish(self):
        for tk in self.out_toks:
            self.sp.wait(tk)
        for e in self.engs:
            if e is not self.sp:
                self.sp.wait(e.last_tok())
        for nm, d in self.dsems.items():
            for j in range(NDMASEM):
                if d["use"][j] > 0:
                    self.sp.wait((d["sems"][j], 16 * d["use"][j]))
        self.sp.flush(False)

    def mmg(self, out, pairs):
        n = len(pairs)
        rd = []
        for l, r in pairs:
            rd += [l, r]
        self._deps(self.pe, rd, [out])
        last = self.pe.flush(True)
        ins = None
        for i, (l, r) in enumerate(pairs):
            ins = self.nc.tensor.matmul(out.ap, l.ap, r.ap, start=(i == 0), stop=(i == n - 1))
            if i == 0 and last is not None:
                ins._wait_ge(last[0], last[1])
            self.ninstr += 1
        tok = self.pe.issue(ins)
        self._commit(tok, rd, [out])
        return tok

    def mm(self, out, lhsT, rhs):
        return self.mmg(out, [(lhsT, rhs)])

    def mmq(self, items):
        bases = sorted(set(l.ap.base_partition() for _, l, _ in items))
        if len(bases) > 1:
            tok = None
            for bse in bases:
                tok = self.mmq([it for it in items if it[1].ap.base_partition() == bse])
            return tok
        rd, wr = [], []
        for o, l, r in items:
            rd += [l, r]
            wr.append(o)
        self._deps(self.pe, rd, wr)
        last = self.pe.flush(True)
        ins = None
        for i, (o, l, r) in enumerate(items):
            ins = self.nc.tensor.matmul(o.ap, l.ap, r.ap, start=True, stop=True)
            if i == 0 and last is not None:
                ins._wait_ge(last[0], last[1])
            self.ninstr += 1
        tok = self.pe.issue(ins)
        self._commit(tok, rd, wr)
        return tok

    def mm1(self, out, lhsT, rhs, start, stop):
        pe = self.pe
        self._deps(pe, [lhsT, rhs], [out] if start else [])
        last = pe.flush(True)
        ins = self.nc.tensor.matmul(out.ap, lhsT.ap, rhs.ap, start=start, stop=stop)
        if last is not None:
            ins._wait_ge(last[0], last[1])
        self.ninstr += 1
        tok = pe.issue(ins)
        self._commit(tok, [lhsT, rhs], [out])
        return tok

    def trq(self, items):
        rd, wr = [], []
        for o, i_, idn in items:
            rd += [i_, idn]
            wr.append(o)
        self._deps(self.pe, rd, wr)
        last = self.pe.flush(True)
        ins = None
        for i, (o, i_, idn) in enumerate(items):
            ins = self.nc.tensor.transpose(o.ap, i_.ap, idn.ap)
            if i == 0 and last is not None:
                ins._wait_ge(last[0], last[1])
            self.ninstr += 1
        tok = self.pe.issue(ins)
        self._commit(tok, rd, wr)
        return tok

    def transpose(self, out, in_, ident):
        return self.op(self.pe, lambda: self.nc.tensor.transpose(out.ap, in_.ap, ident.ap), [in_, ident], [out])

    def activation(self, out, in_, func, bias=None, scale=None, accum_out=None):
        eng = self.act
        kw = {}
        rd = [in_]
        if bias is not None:
            if isinstance(bias, AV):
                kw["bias"] = bias.ap
                rd.append(bias)
            else:
                kw["bias"] = float(bias)
        if scale is not None:
            if isinstance(scale, AV):
                kw["scale"] = scale.ap
                rd.append(scale)
            else:
                kw["scale"] = float(scale)
        wr = [out]
        if accum_out is not None:
            kw["accum_out"] = accum_out.ap
            wr.append(accum_out)
        return self.op(eng, lambda: eng.e.activation(out.ap, in_.ap, func, **kw), rd, wr)

    def tt(self, out, a, b, op, eng=None):
        eng = eng or self.dve
        return self.op(eng, lambda: eng.e.tensor_tensor(out.ap, a.ap, b.ap, op), [a, b], [out])

    def ts(self, out, a, s1, op0, s2=None, op1=None, eng=None):
        eng = eng or self.dve
        rd = [a]
        s1v = s1.ap if isinstance(s1, AV) else float(s1)
        s2v = s2.ap if isinstance(s2, AV) else (None if s2 is None else float(s2))
        if isinstance(s1, AV):
            rd.append(s1)
        if isinstance(s2, AV):
            rd.append(s2)
        if op1 is None:
            return self.op(eng, lambda: eng.e.tensor_scalar(out.ap, a.ap, s1v, None, op0), rd, [out])
        return self.op(eng, lambda: eng.e.tensor_scalar(out.ap, a.ap, s1v, s2v, op0, op1), rd, [out])

    def stt(self, out, a, s, b, op0, op1, eng=None):
        eng = eng or self.dve
        rd = [a, b]
        sv = s.ap if isinstance(s, AV) else float(s)
        if isinstance(s, AV):
            rd.append(s)
        return self.op(eng, lambda: eng.e.scalar_tensor_tensor(out.ap, a.ap, sv, b.ap, op0, op1), rd, [out])

    def copy(self, out, in_, eng=None):
        eng = eng or self.dve
        if eng is self.act:
            return self.op(eng, lambda: eng.e.copy(out.ap, in_.ap), [in_], [out])
        return self.op(eng, lambda: eng.e.tensor_copy(out.ap, in_.ap), [in_], [out])

    def memset(self, out, val, eng=None):
        eng = eng or self.dve
        return self.op(eng, lambda: eng.e.memset(out.ap, val), [], [out])

    def recip(self, out, in_):
        return self.op(self.dve, lambda: self.nc.vector.reciprocal(out.ap, in_.ap), [in_], [out])

    def reduce(self, out, in_, op, axis=AX.X):
        return self.op(self.dve, lambda: self.nc.vector.tensor_reduce(out.ap, in_.ap, axis, op), [in_], [out])

    def scan(self, out, d0, d1, init, op0, op1):
        return self.op(self.dve, lambda: self.nc.vector.tensor_tensor_scan(out.ap, d0.ap, d1.ap, init, op0, op1),
                       [d0, d1], [out])


class Scope:
    def __init__(self, k):
        self.k = k
        self.es = ExitStack()
        self.tiles = []

    def sb(self, shape, dt=F32, name=None):
        k = self.k
        k.ntile += 1
        name = f"{name or 's'}_{k.ntile}"
        h = self.es.enter_context(k.nc.sbuf_tensor(name, list(shape), dt))
        t = Tile(h, name)
        t.scoped = True
        k.stamp += 1
        t.born = k.stamp
        self.tiles.append(t)
        return t

    def close(self):
        k = self.k
        toks = []
        for t in self.tiles:
            toks += [tk for _, tk in t.wl] + [tk for _, tk in t.rl]
        toks = compress(toks)
        k.stamp += 1
        for e in k.engs:
            e.pending.append((k.stamp, toks))
            if len(e.pending) > 6:
                st0 = e.pending[0][0]
                e.pending = [(st0, compress(e.pending[0][1] + e.pending[1][1]))] + e.pending[2:]
        self.es.close()


def win_chunk_cols():
    ch = []
    for i in range(8):
        ch.append(A0 + i * 128 + np.arange(128))
    ch.append(A0 + 1024 + np.arange(16))
    for i in range(6):
        ch.append(B0 + i * 128 + np.arange(128))
    ch.append(B0 + 768 + np.arange(128))
    ch.append(B0 + 896 + np.arange(64))
    ch.append(B0 + 960 + np.arange(128))
    for i in range(6):
        ch.append(C0 + i * 128 + np.arange(128))
    d = np.arange(128)
    perm = (d // 32) * 32 + ((d % 32) ^ 8)
    for i in range(4):
        ch.append(C0 + i * 128 + perm)
    for i in range(6):
        ch.append(D0 + i * 128 + np.arange(128))
    return ch


def rope_tables():
    t = np.arange(1024)
    pos = np.stack([t // 64, t % 64], -1).astype(np.float32)
    n_freq = 8
    inv = (10000.0 ** (-np.arange(n_freq, dtype=np.float32) / n_freq)).astype(np.float32)
    ang = pos[:, :, None] * inv
    cos, sin = np.cos(ang), np.sin(ang)
    ct = np.zeros((64, 1024), np.float32)
    st = np.zeros((64, 1024), np.float32)
    for m in range(2):
        for a in range(2):
            for b in range(2):
                for f in range(8):
                    row = m * 32 + a * 16 + b * 8 + f
                    ct[row] = cos[:, a, f]
                    st[row] = sin[:, a, f] * (-1.0 if b == 0 else 1.0)
    return np.stack([ct, st], 1)


def const_tables():
    i = np.arange(128)[:, None]
    j = np.arange(128)[None, :]
    ident = (i == j)
    low = (j < i)
    up = (j > i)
    lowi = (j <= i)
    upi = (j >= i)
    bd = (i // 64 == j // 64)
    d16 = (i // 16 == j // 16)
    d32 = (i // 32 == j // 32)
    d64 = (i // 64 == j // 64)
    cst = np.stack([ident, low, up, lowi, upi, bd, d16, d32 & ~d16, d64 & ~d32, ~d64], 1).astype(np.float32)
    sel = np.zeros((8, 8, 128), np.float32)
    for r in range(8):
        sel[r, r, :] = 1.0
    cq = np.arange(64)[:, None]
    ck = np.arange(64)[None, :]
    cstart = np.clip(cq - 8, 0, 48)
    inwin = (ck >= cstart) & (ck < cstart + 16)
    wmask = np.where(inwin, 0.0, -30000.0).astype(np.float32)
    mfb = np.zeros((8, 2), np.float32)
    mfb[0:4, 0] = 1.0
    mfb[4:8, 1] = 1.0
    return cst, sel, wmask, mfb


def prep_shared(inp):
    f32 = np.float32
    sh = {}
    w_mod = inp["w_mod"]
    sh["wmod_c"] = np.ascontiguousarray(w_mod.reshape(L_, 8, 128, 72, 128).transpose(0, 3, 2, 1, 4))
    sh["bmod"] = np.ascontiguousarray(inp["b_mod"].reshape(L_, 72, 128).transpose(0, 2, 1))
    fi = inp["ffn_in"].reshape(L_, 2, 8, 128, 2, 22, 128)
    sh["wgu_c"] = np.ascontiguousarray(fi.transpose(0, 1, 5, 4, 3, 2, 6)).reshape(L_, 2, 44, 128, 8, 128)
    fo = np.zeros((L_, 2, 3072, 1024), f32)
    fo[:, :, :2816] = inp["ffn_out"]
    fo = fo.reshape(L_, 2, 3, 8, 128, 8, 128)
    sh["wd_c"] = np.ascontiguousarray(fo.transpose(0, 1, 5, 2, 4, 3, 6))
    cols = win_chunk_cols()
    wi = inp["w_in"].reshape(L_, 8, 128, INC)
    winc = np.zeros((L_, 34, 128, 8, 128), f32)
    for c, cc in enumerate(cols):
        winc[:, c, :, :, :len(cc)] = wi[:, :, :, cc].transpose(0, 2, 1, 3)
    sh["win_c"] = winc
    sh["wout_c"] = np.ascontiguousarray(inp["w_out"].reshape(L_, 8, 128, 8, 128).transpose(0, 3, 2, 1, 4))
    vec = np.zeros((L_, 128, NV), f32)
    p = np.arange(128)
    for n in range(3):
        for kc in range(8):
            vec[:, :, n * 8 + kc] = inp["norm_pre"][:, n, kc * 128:(kc + 1) * 128]
            vec[:, :, 24 + n * 8 + kc] = inp["norm_post"][:, n, kc * 128:(kc + 1) * 128]
    for c in range(6):
        for j in range(5):
            vec[:, :, 48 + c * 5 + j] = inp["dn_conv"][:, j, c * 128:(c + 1) * 128]
    vec[:, :, 78] = inp["dn_norm"][:, p % 64]
    for i in range(9):
        cc = cols[9 + i] - B0
        for q in range(2):
            vec[:, :len(cc), 79 + i * 2 + q] = inp["rw_mu"][:, q, :][:, cc]
    for d in range(2):
        for hp in range(2):
            vec[:, :, 97 + d * 2 + hp] = inp["rw_w0"][:, d, hp * 128:(hp + 1) * 128]
    rk = inp["rw_r_k"].reshape(L_, 256)
    for hp in range(2):
        sl = slice(hp * 128, (hp + 1) * 128)
        vec[:, :, 101 + hp] = inp["rw_a0"][:, sl]
        vec[:, :, 103 + hp] = inp["rw_k_k"][:, sl]
        vec[:, :, 105 + hp] = inp["rw_k_a"][:, sl]
        vec[:, :, 107 + hp] = rk[:, sl]
        vec[:, :, 109 + hp] = inp["rw_ln"][:, 0, sl]
        vec[:, :, 111 + hp] = inp["rw_ln"][:, 1, sl]
    vec[:, :, 113] = inp["da_norm"][:, p % 64]
    sh["vec"] = vec
    dnp = np.zeros((L_, 128, 16), f32)
    dnp[:, :, 0:8] = inp["dn_dt_bias"].reshape(L_, 1, 8)
    dnp[:, :, 8:16] = inp["dn_a_log"].reshape(L_, 1, 8)
    sh["dnp"] = dnp
    dnr = np.zeros((L_, 8, 2), f32)
    dnr[:, :, 0] = inp["dn_dt_bias"].reshape(L_, 8)
    dnr[:, :, 1] = inp["dn_a_log"].reshape(L_, 8)
    sh["dnr"] = dnr
    sh["lamv"] = np.ascontiguousarray(np.broadcast_to(inp["da_lambda"].reshape(L_, 1, 128), (L_, 128, 128))).astype(f32)
    rwm = np.zeros((L_, 128, 3, 256), f32)
    rwm[:, :, 0, :] = inp["rw_w2"].reshape(L_, 128, 256)
    rwm[:, :64, 1, :] = inp["rw_a2"]
    rwm[:, :, 2, :] = inp["rw_g2"]
    sh["rwm"] = rwm
    cq = np.arange(64)[:, None]
    ck = np.arange(64)[None, :]
    dc = np.clip(ck - cq + 15, 0, 30)
    nb = inp["na_bias"][:, :, :, dc]
    sh["nab"] = np.ascontiguousarray(nb.transpose(0, 3, 1, 2, 4))
    cst, sel, wmask, mfb = const_tables()
    sh["cst"] = cst
    sh["sel"] = sel
    sh["wmask"] = wmask
    sh["mfb"] = mfb
    sh["rope"] = rope_tables()
    return sh


def prep_core(inp, c):
    f32 = np.float32
    b = c % 4
    d = {}
    xp = inp["x_prompt"][4 * c:4 * c + 4].reshape(1024, 1024)
    xs = inp["x_sample"][b]
    xa = np.concatenate([xp, xs], 0)
    d["xin"] = np.ascontiguousarray(xa.T.reshape(8, 128, 2048).transpose(1, 0, 2))
    cond = np.stack([inp["c_ctx"], inp["c"][b]], 0)
    d["condT"] = np.ascontiguousarray(cond.T.reshape(8, 128, 2).transpose(1, 0, 2))
    ck_ = inp["cache_diff_k"][b]
    d["cdk"] = np.ascontiguousarray(ck_.transpose(0, 3, 4, 1, 2).reshape(L_, 64, 4, 256))
    cv = inp["cache_diff_v"][b].reshape(L_, 4, 2, 128, 64)
    d["cdv"] = np.ascontiguousarray(cv.transpose(0, 3, 2, 1, 4)).reshape(L_, 128, 2, 256)
    nk = inp["cache_na_k"][b].reshape(L_, 2, 2, 256, 64)
    d["cnk"] = np.ascontiguousarray(nk.transpose(0, 2, 4, 1, 3)).reshape(L_, 128, 2, 256)
    nv = inp["cache_na_v"][b].reshape(L_, 4, 4, 64, 64)
    d["cnv"] = np.ascontiguousarray(nv.transpose(0, 3, 2, 1, 4)).reshape(L_, 64, 4, 256)
    zd = np.stack([inp["state_dn_fwd"][b], inp["state_dn_bwd"][b]], 1)
    zd = zd.reshape(L_, 2, 2, 2, 64, 64)
    d["zdn"] = np.ascontiguousarray(zd.transpose(0, 3, 4, 1, 2, 5)).reshape(L_, 128, 2, 2, 64)
    zr = np.stack([inp["state_rwkv_fwd"][b], inp["state_rwkv_bwd"][b]], 1)
    zr = zr.reshape(L_, 2, 2, 2, 64, 64)
    d["zrw"] = np.ascontiguousarray(zr.transpose(0, 3, 5, 1, 2, 4)).reshape(L_, 128, 2, 2, 64)
    return d


IN_SHAPES = {
    "xin": [128, 8, 2048], "condT": [128, 8, 2], "cst": [128, 10, 128], "sel": [8, 8, 128], "wmask": [64, 64],
    "mfb": [8, 2], "rope": [64, 2, 1024],
    "wmod_c": [L_, 72, 128, 8, 128], "bmod": [L_, 128, 72], "wgu_c": [L_, 2, 44, 128, 8, 128],
    "wd_c": [L_, 2, 8, 3, 128, 8, 128], "win_c": [L_, 34, 128, 8, 128], "wout_c": [L_, 8, 128, 8, 128],
    "vec": [L_, 128, NV], "dnp": [L_, 128, 16], "dnr": [L_, 8, 2], "lamv": [L_, 128, 128], "rwm": [L_, 128, 3, 256],
    "nab": [L_, 64, 4, 15, 64],
    "cdk": [L_, 64, 4, 256], "cdv": [L_, 128, 2, 256], "cnk": [L_, 128, 2, 256], "cnv": [L_, 64, 4, 256],
    "zdn": [L_, 128, 2, 2, 64], "zrw": [L_, 128, 2, 2, 64],
}
OUT_SHAPES = {
    "y_out": [128, 8, 2048], "o_dk": [L_, 4, 64, 1024], "o_dv": [L_, 128, 8, 256], "o_nk": [L_, 128, 2, 1024],
    "o_nv": [L_, 128, 8, 256], "o_dn": [L_, 2, 4, 128, 2, 64], "o_rw": [L_, 2, 4, 128, 2, 64],
}


class Prog:
    pass


def build(cfg=None):
    p1 = _build(cfg, None)
    return _build(cfg, p1.k.needed_rec)


def _build(cfg, needed):
    cfg = cfg or {}
    NL = cfg.get("nl", L_)
    stages = cfg.get("stages", ("ffn0", "mix", "ffn1"))
    mixers = cfg.get("mixers", "ABCD")
    dbg = cfg.get("dbg", {})
    k = K(needed)
    nc = k.nc
    P = Prog()
    P.k = k
    P.dbg_out = {}
    DI = {n: k.dram(n, s, F32, "ExternalInput") for n, s in IN_SHAPES.items()}
    DO = {n: k.dram(n, s, F32, "ExternalOutput") for n, s in OUT_SHAPES.items()}

    def dump(name, av, shape, dt=F32):
        d = k.dram("dbg_" + name, list(shape), dt, "ExternalOutput")
        P.dbg_out["dbg_" + name] = (list(shape), dt)
        k.dma(d[:], av, is_out=True)

    X = [[k.sb([128, 1024], F32, f"x{g}_{kc}") for kc in range(8)] for g in range(2)]
    CF = k.sb([128, 6, 128], F32, "cf")
    IDf, LOWf, UPf, LOWIf, UPIf, BDf = (CF[:, i, :] for i in range(6))
    MK = k.sb([128, 4, 128], BF16, "mk")
    IDb = k.sb([128, 128], BF16, "idb")
    ONESb = k.sb([128, 128], BF16, "onesb")
    BDb = k.sb([128, 128], BF16, "bdb")
    ONESf = k.sb([128, 128], F32, "onesf")
    NM = k.sb([128, 4, 128], F32, "nm")
    WS = [k.sb([128, 8, 128], BF16, f"ws{i}") for i in range(NSLOT)]
    BK = [k.ps([128, 512], F32, f"bk{i}") for i in range(8)]
    VEC = k.sb([128, NV], F32, "vec")
    BMOD = k.sb([128, 72], F32, "bmod")
    MOD = k.sb([128, 72, 2], F32, "mod")
    MVA = k.sb([128, 2, 3, 8], F32, "mva")
    MVG = k.sb([128, 2, 3, 8], F32, "mvg")
    SC32 = k.sb([128, 8, 2], F32, "sc32")
    SCb = k.sb([128, 8, 2], BF16, "scb")
    st = dict(bank=0, ws=0)

    def nb():
        b = BK[st["bank"] % 8]
        st["bank"] += 1
        return b

    def bfh(bank):
        return bank.h.bitcast(BF16)

    def wchunk(dram_av, nk=8):
        s = WS[st["ws"] % NSLOT]
        st["ws"] += 1
        k.dma(s[:, 0:nk, :], dram_av, eng=k.pool)
        return s

    k.dma(CF[:], DI["cst"][:, 0:6, :])
    k.copy(IDb[:], IDf)
    k.copy(BDb[:], BDf)
    sc0 = Scope(k)
    MKF = sc0.sb([128, 4, 128], F32, "mkf")
    k.dma(MKF[:], DI["cst"][:, 6:10, :])
    k.copy(MK[:], MKF[:])
    sc0.close()
    k.memset(ONESb[:], 1.0)
    k.memset(ONESf[:], 1.0)
    for i in range(4):
        k.ts(NM[:, i, :], CF[:, 1 + i, :], -1.0, ALU.add, -NEGBIG, ALU.mult)
    for g in range(2):
        for kc in range(8):
            k.dma(X[g][kc][:], DI["xin"][:, kc, g * 1024:(g + 1) * 1024])
    k.dma(SC32[:], DI["condT"][:])
    k.activation(SCb[:], SC32[:], AF.Silu)

    TS = [slice(0, 512), slice(512, 1024)]

    def rstd_tile(srcs, SQ, RT, RS):
        bank = nb()
        for kc in range(8):
            k.activation(SQ[kc % 2][:], srcs[kc], AF.Square)
            k.mm1(bank[:], ONESb[:], SQ[kc % 2][:], kc == 0, kc == 7)
        k.activation(RT[:], bank[:], AF.Sqrt, bias=1e-6, scale=1.0 / 1024)
        k.recip(RS[:], RT[:])

    def norm_mod(g, n, H, SQ, RT, RS, TMP):
        for tt in range(2):
            rstd_tile([X[g][kc][:, TS[tt]] for kc in range(8)], SQ, RT, RS)
            for kc in range(8):
                k.stt(TMP[kc % 2][:], X[g][kc][:, TS[tt]], MVA[:, g, n, kc:kc + 1], RS[:], ALU.mult, ALU.mult)
                k.activation(H[:, kc, TS[tt]], TMP[kc % 2][:], AF.Identity, bias=MOD[:, (3 * n) * 8 + kc, g:g + 1])

    def post(g, n, Y, SQ, RT, RS, TMP):
        for tt in range(2):
            rstd_tile([Y[:, kc, TS[tt]] for kc in range(8)], SQ, RT, RS)
            for kc in range(8):
                k.stt(TMP[kc % 2][:], Y[:, kc, TS[tt]], MVG[:, g, n, kc:kc + 1], RS[:], ALU.mult, ALU.mult)
                k.tt(X[g][kc][:, TS[tt]], X[g][kc][:, TS[tt]], TMP[kc % 2][:], ALU.add)

    def ffn(l, g, f):
        n = 0 if f == 0 else 2
        sc = Scope(k)
        H = sc.sb([128, 8, 1024], BF16, "h")
        ACT_ = sc.sb([128, 22, 1024], BF16, "act")
        Y = sc.sb([128, 8, 1024], F32, "y")
        SQ = [sc.sb([128, 512], BF16, "sq") for _ in range(2)]
        RT = sc.sb([128, 512], F32, "rt")
        RS = sc.sb([128, 512], F32, "rs")
        TMP = [sc.sb([128, 512], F32, "tmp") for _ in range(2)]
        norm_mod(g, n, H, SQ, RT, RS, TMP)
        for j in range(22):
            sg = wchunk(DI["wgu_c"][l, f, 2 * j])
            su = wchunk(DI["wgu_c"][l, f, 2 * j + 1])
            for tt in range(2):
                pg = nb()
                pu = nb()
                k.mmg(pg[:], [(sg[:, kc, :], H[:, kc, TS[tt]]) for kc in range(8)])
                k.mmg(pu[:], [(su[:, kc, :], H[:, kc, TS[tt]]) for kc in range(8)])
                k.activation(TMP[tt][:], pg[:], AF.Silu)
                k.tt(ACT_[:, j, TS[tt]], TMP[tt][:], pu[:], ALU.mult)
        for m in range(8):
            sls = [wchunk(DI["wd_c"][l, f, m, c, :, 0:(8 if c < 2 else 6), :], nk=(8 if c < 2 else 6)) for c in range(3)]
            for tt in range(2):
                py = nb()
                k.mmg(py[:], [(sls[j // 8][:, j % 8, :], ACT_[:, j, TS[tt]]) for j in range(22)])
                k.copy(Y[:, m, TS[tt]], py[:], eng=k.act)
        post(g, n, Y, SQ, RT, RS, TMP)
        sc.close()

    P.X, P.nb, P.bfh, P.wchunk, P.DI, P.DO, P.dump = X, nb, bfh, wchunk, DI, DO, dump
    P.consts = dict(IDf=IDf, LOWf=LOWf, UPf=UPf, LOWIf=LOWIf, UPIf=UPIf, BDf=BDf, IDb=IDb, ONESb=ONESb, BDb=BDb,
                    ONESf=ONESf, NM=NM, CF=CF, MK=MK)
    P.VEC, P.MOD, P.MVA, P.MVG, P.TS = VEC, MOD, MVA, MVG, TS
    P.rstd_tile, P.norm_mod, P.post = rstd_tile, norm_mod, post

    for l in range(NL):
        k.dma(VEC[:], DI["vec"][l])
        k.dma(BMOD[:], DI["bmod"][l])
        pm = nb()
        pmv = AV(pm.h[:, 0:144].rearrange("p (j g) -> p j g", g=2), pm)
        for j in range(72):
            sl = wchunk(DI["wmod_c"][l, j])
            k.mmg(AV(pm.h[:, 2 * j:2 * j + 2], pm), [(sl[:, kc, :], SCb[:, kc, :]) for kc in range(8)])
        for g in range(2):
            k.tt(MOD[:, :, g], AV(pm.h[:, 0:144].rearrange("p (j g) -> p j g", g=2)[:, :, g], pm), BMOD[:], ALU.add)
        for g in range(2):
            for n in range(3):
                wt = 1.0 if n == 1 else 0.5
                k.stt(MVA[:, g, n, :], MOD[:, (3 * n + 1) * 8:(3 * n + 2) * 8, g], 1.0, VEC[:, n * 8:(n + 1) * 8],
                      ALU.add, ALU.mult)
                k.stt(MVG[:, g, n, :], MOD[:, (3 * n + 2) * 8:(3 * n + 3) * 8, g], wt,
                      VEC[:, 24 + n * 8:24 + (n + 1) * 8], ALU.mult, ALU.mult)
        if "mod" in dbg:
            dump(f"mod{l}", MOD[:], [128, 72, 2])
        for g in range(2):
            if "ffn0" in stages:
                ffn(l, g, 0)
            if "x1" in dbg:
                for kc in range(8):
                    dump(f"x1_{l}_{g}_{kc}", X[g][kc][:], [128, 1024])
            if "mix" in stages:
                mixer_phase(P, l, g, mixers, dbg)
            if "x2" in dbg:
                for kc in range(8):
                    dump(f"x2_{l}_{g}_{kc}", X[g][kc][:], [128, 1024])
            if "ffn1" in stages:
                ffn(l, g, 1)

    for g in range(2):
        for kc in range(8):
            k.dma(DO["y_out"][:, kc, g * 1024:(g + 1) * 1024], X[g][kc][:], is_out=True)
    k.finish()
    return P


def mixer_phase(P, l, g, mixers, dbg):
    k = P.k
    nb, wchunk, DI, DO, TS = P.nb, P.wchunk, P.DI, P.DO, P.TS
    sc = Scope(k)
    OTs = [None] * 4
    OTs[0] = sc.sb([128, 2, 1024], BF16, "ot0")
    H1 = sc.sb([128, 8, 1024], BF16, "h1")
    SQ = [sc.sb([128, 512], BF16, "sq") for _ in range(2)]
    RT = sc.sb([128, 512], F32, "rt")
    RS = sc.sb([128, 512], F32, "rs")
    TMP = [sc.sb([128, 512], F32, "tmp") for _ in range(2)]
    P.norm_mod(g, 1, H1, SQ, RT, RS, TMP)
    M = Prog()
    M.P, M.l, M.g, M.H1, M.SQ, M.RT, M.RS, M.TMP = P, l, g, H1, SQ, RT, RS, TMP
    M.dbg = dbg

    def pj(slot, col0, Mrows):
        outs = []
        for tt in range(2):
            b = nb()
            k.mmg(b[0:Mrows, :], [(slot[:, kc, col0:col0 + Mrows], H1[:, kc, TS[tt]]) for kc in range(8)])
            outs.append(b)
        return outs

    def proj_tm(chunks, ntok):
        slots = [wchunk(DI["win_c"][l, c]) for c in chunks]
        for i in range(1024 // ntok):
            b = nb()
            for ci, s in enumerate(slots):
                k.mmg(b[0:ntok, ci * 128:(ci + 1) * 128],
                      [(H1[:, kc, i * ntok:(i + 1) * ntok], s[:, kc, :]) for kc in range(8)])
            yield i, b
    M.pj, M.proj_tm = pj, proj_tm
    fns = dict(A=mixer_A, B=mixer_B, C=mixer_C, D=mixer_D)
    for ci, name in enumerate("ABCD"):
        if ci == 1:
            for c2 in range(1, 4):
                OTs[c2] = sc.sb([128, 2, 1024], BF16, f"ot{c2}")
        M.OT = OTs[ci]
        if name in mixers:
            fns[name](M)
        else:
            k.memset(OTs[ci][:], 0.0)
    if "ot" in dbg:
        for ci in range(4):
            P.dump(f"ot_{l}_{g}_{ci}", OTs[ci][:], [128, 2, 1024], BF16)
    Y = sc.sb([128, 8, 1024], F32, "y")
    for m in range(8):
        s = wchunk(DI["wout_c"][l, m])
        for tt in range(2):
            py = nb()
            k.mmg(py[:], [(s[:, kc, :], OTs[kc // 2][:, kc % 2, TS[tt]]) for kc in range(8)])
            k.copy(Y[:, m, TS[tt]], py[:], eng=k.act)
    P.post(g, 1, Y, SQ, RT, RS, TMP)
    sc.close()


def skewed(units):
    if not units:
        return
    units[0][0]()
    for u in range(len(units)):
        if u + 1 < len(units):
            units[u + 1][0]()
        units[u][1]()


def ot_from_tokmajor(M, O, c0, ntok, ntile, norm=None):
    k = M.P.k
    nb = M.P.nb
    IDf = M.P.consts["IDf"]
    for i in range(ntile):
        b = nb()
        k.trq([(b[:, c * ntok:(c + 1) * ntok], O[0:ntok, i, c * 128:(c + 1) * 128], AV(IDf.ap[0:ntok, 0:ntok], IDf.t))
               for c in range(2)])
        src = AV(b.h[:, 0:2 * ntok].rearrange("p (c n) -> p c n", c=2), b)
        k.copy(M.OT[:, 0:2, i * ntok:(i + 1) * ntok], src, eng=(k.act if i % 2 else k.dve))


def mixer_D(M):
    P, l, g, H1, OT = M.P, M.l, M.g, M.H1, M.OT
    k = P.k
    nb, wchunk, DI, DO, TS, bfh = P.nb, P.wchunk, P.DI, P.DO, P.TS, P.bfh
    IDb = P.consts["IDb"]
    sc = Scope(k)
    QT = sc.sb([128, 2, 1024], BF16, "dq")
    KT = sc.sb([128, 2, 1024], BF16, "dk")
    KF = [sc.sb([128, 512], F32, "dkf") for _ in range(2)]
    for hp in range(2):
        s = wchunk(DI["win_c"][l, 28 + hp])
        bs = M.pj(s, 0, 128)
        for tt in range(2):
            k.copy(QT[:, hp, TS[tt]], bs[tt][:], eng=k.act)
        s = wchunk(DI["win_c"][l, 30 + hp])
        bs = M.pj(s, 0, 128)
        for tt in range(2):
            if g == 0:
                k.copy(KF[tt][:], bs[tt][:])
                k.dma(DO["o_nk"][l, :, hp, TS[tt]], KF[tt][:], is_out=True)
                k.copy(KT[:, hp, TS[tt]], KF[tt][:], eng=k.act)
            else:
                k.copy(KT[:, hp, TS[tt]], bs[tt][:], eng=k.act)
    SUM = sc.sb([128, 4], F32, "dsum")
    RS4 = sc.sb([128, 4], F32, "drs")
    MX = sc.sb([128, 4], F32, "dmx")
    NMX = sc.sb([128, 4], F32, "dnmx")
    if g == 0:
        V = sc.sb([128, 8, 256], BF16, "dv")
        VF = [sc.sb([128, 256], F32, "dvf") for _ in range(2)]
        for i, b in M.proj_tm([32, 33], 128):
            k.copy(VF[i % 2][:], b[:, 0:256])
            k.dma(DO["o_nv"][l, :, i, :], VF[i % 2][:], is_out=True)
            k.copy(V[:, i, :], VF[i % 2][:], eng=k.act)
        O = sc.sb([128, 8, 256], F32, "do")
        Pb = [sc.sb([128, 4, 256], BF16, "dp") for _ in range(2)]
        PT = [sc.sb([128, 8, 128], BF16, "dpt") for _ in range(2)]
        SUMs = [sc.sb([128, 4], F32, "dsum2") for _ in range(2)]
        units = []
        u = 0
        for s_ in range(4):
            for qb in range(2):
                stt_ = {}

                def pa(u=u, s_=s_, qb=qb, stt_=stt_):
                    ti = 2 * s_ + qb
                    qsl = slice(ti * 128, (ti + 1) * 128)
                    ksl = slice(s_ * 256, (s_ + 1) * 256)
                    bsc = [nb(), nb()]
                    for i2 in range(2):
                        k.mmq([(bsc[i2][:, hh * 256:(hh + 1) * 256], QT[hh * 64:hh * 64 + 64, i2, qsl], KT[hh * 64:hh * 64 + 64, i2, ksl])
                               for hh in range(2)])
                    for i2 in range(2):
                        k.reduce(MX[:, 2 * i2:2 * i2 + 2], AV(bsc[i2].h[:, :].rearrange("p (h n) -> p h n", h=2), bsc[i2]), ALU.max)
                    k.ts(NMX[:], MX[:], -0.125, ALU.mult)
                    pb = Pb[u % 2]
                    for h in range(4):
                        k.activation(pb[:, h, :], bsc[h // 2][:, (h % 2) * 256:(h % 2 + 1) * 256], AF.Exp,
                                     bias=NMX[:, h:h + 1], scale=0.125, accum_out=SUMs[u % 2][:, h:h + 1])

                def pb_(u=u, s_=s_, qb=qb):
                    ti = 2 * s_ + qb
                    pb = Pb[u % 2]
                    bt = nb()
                    bth = bfh(bt)
                    k.trq([(AV(bth[:, (h * 2 + c) * 128:(h * 2 + c + 1) * 128], bt), pb[:, h, c * 128:(c + 1) * 128], IDb[:])
                           for h in range(4) for c in range(2)])
                    pt = PT[u % 2]
                    k.copy(pt[:], AV(bth[:, 0:1024].rearrange("p (j n) -> p j n", j=8), bt), eng=k.act)
                    po = nb()
                    for h in range(4):
                        k.mmg(po[:, h * 64:(h + 1) * 64], [(pt[:, h * 2 + c, :], V[:, 2 * s_ + c, h * 64:(h + 1) * 64]) for c in range(2)])
                    k.recip(RS4[:], SUMs[u % 2][:])
                    k.tt(AV(O.h[:, ti, :].rearrange("p (h d) -> p h d", h=4), O),
                         AV(po.h[:, 0:256].rearrange("p (h d) -> p h d", h=4), po),
                         AV(RS4.h[:, :].unsqueeze(2).to_broadcast([128, 4, 64]), RS4), ALU.mult)
                units.append((pa, pb_))
                u += 1
        skewed(units)
        ot_from_tokmajor(M, O, 6, 128, 8)
    else:
        KcT = sc.sb([128, 2, 256], BF16, "dkc")
        Vc = sc.sb([64, 4, 256], BF16, "dvc")
        TB = sc.sb([64, 4, 15, 64], F32, "dtb")
        WM = sc.sb([64, 64], F32, "dwm")
        k.dma(KcT[:], DI["cnk"][l], eng=k.pool)
        k.dma(Vc[:], DI["cnv"][l], eng=k.pool)
        k.dma(TB[:], DI["nab"][l])
        k.dma(WM[:], DI["wmask"][:])
        for h in range(4):
            k.tt(TB[:, h, :, :], TB[:, h, :, :], AV(WM.h[:, :].unsqueeze(1).to_broadcast([64, 15, 64]), WM), ALU.add)
        V64 = sc.sb([64, 16, 256], BF16, "dv64")
        for i, b in M.proj_tm([32, 33], 64):
            k.copy(V64[:, i, :], b[0:64, 0:256], eng=(k.act if i % 2 else k.dve))
        O64 = sc.sb([64, 16, 256], F32, "do64")
        S = [sc.sb([64, 768], F32, "ds") for _ in range(2)]
        Pb = [sc.sb([64, 768], BF16, "dp") for _ in range(2)]
        PT = [sc.sb([64, 12, 64], BF16, "dpt") for _ in range(2)]
        SUMs = [sc.sb([64, 1], F32, "dsum2") for _ in range(2)]
        units = []
        u = 0
        for r in range(16):
            for h in range(4):
                def pa(u=u, r=r, h=h):
                    rs = min(max(r - 4, 0), 8)
                    dr0 = rs - r + 7
                    hp, hb = h // 2, (h % 2) * 64
                    b1, b2 = nb(), nb()
                    q = QT[hb:hb + 64, hp, r * 64:(r + 1) * 64]
                    k.mm(b1[0:64, :], q, KT[hb:hb + 64, hp, rs * 64:rs * 64 + 512])
                    k.mm(b2[0:64, 0:256], q, KcT[hb:hb + 64, hp, :])
                    s_ = S[u % 2]
                    k.stt(s_[:, 0:512], b1[0:64, :], 0.125,
                          AV(TB.h[:, h, dr0:dr0 + 8, :].rearrange("p a b -> p (a b)"), TB), ALU.mult, ALU.add)
                    k.activation(s_[:, 512:768], b2[0:64, 0:256], AF.Copy, scale=0.125)
                    k.reduce(MX[0:64, 0:1], s_[:], ALU.max)
                    k.ts(NMX[0:64, 0:1], MX[0:64, 0:1], -1.0, ALU.mult)
                    k.activation(Pb[u % 2][:], s_[:], AF.Exp, bias=NMX[0:64, 0:1], scale=1.0, accum_out=SUMs[u % 2][:, 0:1])

                def pb_(u=u, r=r, h=h):
                    rs = min(max(r - 4, 0), 8)
                    pb = Pb[u % 2]
                    bt = nb()
                    bth = bfh(bt)
                    k.trq([(AV(bth[0:64, j * 64:(j + 1) * 64], bt), pb[:, j * 64:(j + 1) * 64], IDb[0:64, 0:64]) for j in range(12)])
                    pt = PT[u % 2]
                    k.copy(pt[:], AV(bth[0:64, 0:768].rearrange("p (j n) -> p j n", j=12), bt), eng=k.act)
                    po = nb()
                    k.mmg(po[0:64, 0:64], [(pt[:, j, :], V64[:, rs + j, h * 64:(h + 1) * 64]) for j in range(8)] +
                          [(pt[:, 8 + c, :], Vc[:, c, h * 64:(h + 1) * 64]) for c in range(4)])
                    k.recip(RS4[0:64, 0:1], SUMs[u % 2][:, 0:1])
                    k.ts(O64[:, r, h * 64:(h + 1) * 64], po[0:64, 0:64], RS4[0:64, 0:1], ALU.mult)
                units.append((pa, pb_))
                u += 1
        skewed(units)
        ot_from_tokmajor(M, O64, 6, 64, 16)
    sc.close()


def interleave(gens):
    gens = list(gens)
    while gens:
        for g_ in list(gens):
            try:
                next(g_)
            except StopIteration:
                gens.remove(g_)


def tri_solve(M, N0, B0, Y0, SV, nprob, U, wt_slices):
    for _ in tri_solve_g(M, N0, B0, Y0, SV, nprob, U, wt_slices):
        pass


def tri_solve_g(M, N0, B0, Y0, SV, nprob, U, wt_slices):
    k = M.P.k
    nb = M.P.nb
    IDb, MK = M.P.consts["IDb"], M.P.consts["MK"]
    np_ = nprob
    cnt = [0]

    def T():
        t = SV[cnt[0] % len(SV)]
        cnt[0] += 1
        return t

    def mk(i):
        if "MK4" in M.P.consts:
            return M.P.consts["MK4"][:, i, 0:np_, :]
        return AV(MK.h[:, i, :].unsqueeze(1).to_broadcast([128, np_, 128]), MK)

    def idb():
        if "ID4" in M.P.consts:
            return M.P.consts["ID4"][:, 0:np_, :]
        return AV(IDb.h[:, :].unsqueeze(1).to_broadcast([128, np_, 128]), IDb)

    def v(t):
        return t[:, 0:np_, :]

    def pv(b):
        return AV(b.h[:, 0:np_ * 128].rearrange("p (q n) -> p q n", q=np_), b)

    def mmb(l, r):
        b = nb()
        if MMQ:
            k.mmq([(b[:, q * 128:(q + 1) * 128], l[:, q, :], r[:, q, :]) for q in range(np_)])
        else:
            for q in range(np_):
                k.mm(b[:, q * 128:(q + 1) * 128], l[:, q, :], r[:, q, :])
        return b
    ei = [0]

    def ev(dst, b, add=None, eng=None):
        if add is None:
            e = eng or k.act
            if "E1" in M.dbg:
                e = k.dve
            ei[0] += 1
            k.copy(v(dst), pv(b), eng=e)
        else:
            k.tt(v(dst), pv(b), add, ALU.add)
    Nd, Bd, IpNd = T(), T(), T()
    k.tt(v(Nd), v(N0), mk(0), ALU.mult)
    k.tt(v(Bd), v(B0), mk(0), ALU.mult)
    k.tt(v(IpNd), v(Nd), idb(), ALU.add)
    yield
    if "S1" in M.dbg:
        return
    b1, b2 = mmb(Bd, Nd), mmb(Nd, Bd)
    N2, B2, IpB2 = T(), T(), T()
    ev(N2, b1)
    ev(B2, b2)
    ev(IpB2, b2, add=idb())
    yield
    if "S2" in M.dbg:
        return
    if "X1" in M.dbg:
        b3 = mmb(Bd, Nd)
        return
    if "X2" in M.dbg:
        b3 = mmb(IpNd, Nd)
        return
    if "X3" in M.dbg:
        b3 = mmb(Nd, IpB2)
        return
    b3 = mmb(IpNd, IpB2)
    if "X4" in M.dbg:
        return
    P1T = T()
    ev(P1T, b3)
    yield
    if "S2a" in M.dbg:
        return
    b1, b2 = mmb(B2, N2), mmb(N2, B2)
    N4, B4, IpB4 = T(), T(), T()
    ev(N4, b1)
    ev(B4, b2)
    ev(IpB4, b2, add=idb())
    yield
    if "S2b" in M.dbg:
        return
    b1 = mmb(B4, N4)
    IpN8 = T()
    ev(IpN8, b1, add=idb())
    yield
    if "S2c" in M.dbg:
        return
    b1 = mmb(IpB4, IpN8)
    P2 = T()
    ev(P2, b1)
    yield
    if "S2d" in M.dbg:
        return
    b1, b2 = mmb(P1T, P2), mmb(P2, P1T)
    Mc, MTc = T(), T()
    ev(Mc, b1)
    ev(MTc, b2)
    yield
    if "S3" in M.dbg:
        return
    for lev in (1, 2, 3):
        No, Bo = T(), T()
        k.tt(v(No), v(N0), mk(lev), ALU.mult)
        if lev < 3:
            k.tt(v(Bo), v(B0), mk(lev), ALU.mult)
            b1 = mmb(Bo, Mc)
            Tt = T()
            ev(Tt, b1)
        b2 = mmb(No, MTc)
        Tp = T()
        ev(Tp, b2)
        yield
        if lev < 3:
            b1 = mmb(MTc, Tt)
            Mn = T()
            ev(Mn, b1, add=v(Mc))
        b2 = mmb(Mc, Tp)
        MTn = T()
        ev(MTn, b2, add=v(MTc))
        yield
        if lev < 3:
            Mc = Mn
        MTc = MTn
    if "S4" in M.dbg:
        return
    bu = nb()
    k.mmq([(bu[:, q * 64:(q + 1) * 64], MTc[:, q, :], Y0[:, q, 0:64]) for q in range(np_)])
    k.copy(U[:, 0:np_, :], AV(bu.h[:, 0:np_ * 64].rearrange("p (q n) -> p q n", q=np_), bu), eng=k.dve)
    bw = nb()
    k.mmq([(bw[wt_slices[q][1]:wt_slices[q][1] + 64, (q // 2) * 128:(q // 2 + 1) * 128], Y0[:, q, 64:128], MTc[:, q, :])
           for q in range(np_)])
    for q in range(np_):
        hb = wt_slices[q][1]
        k.copy(wt_slices[q][0], bw[hb:hb + 64, (q // 2) * 128:(q // 2 + 1) * 128], eng=k.act)


def mixer_A(M):
    P, l, g, H1, OT = M.P, M.l, M.g, M.H1, M.OT
    k = P.k
    nb, wchunk, DI, DO, TS, bfh = P.nb, P.wchunk, P.DI, P.DO, P.TS, P.bfh
    C = P.consts
    IDb, BDb, IDf, ONESf, NM = C["IDb"], C["BDb"], C["IDf"], C["ONESf"], C["NM"]
    LOWIf, UPIf = C["LOWIf"], C["UPIf"]
    VEC = P.VEC
    nseq = 4 if g == 0 else 1
    Ls = 1024 // nseq
    sc = Scope(k)
    QT = sc.sb([128, 2, 1024], BF16, "aq")
    KT = sc.sb([128, 2, 1024], BF16, "ak")
    Kt = sc.sb([128, 8, 256], BF16, "akt")
    Vt = sc.sb([128, 8, 256], BF16, "avt")
    GS = sc.sb([128, 2, 1024], BF16, "ags")
    O = sc.sb([128, 8, 256], F32, "ao")
    ROWS = sc.sb([8, 3, 1024], F32, "arows")
    COLS = sc.sb([128, 9, 8, 8], F32, "acols")
    cLA, cL2, cG, cGH, cBET, cEGH, cEG, cED, cEGT = range(9)
    SELt = sc.sb([8, 8, 128], F32, "asel")
    Zf = sc.sb([128, 2, 2, 64], F32, "azf")
    Zb = sc.sb([128, 2, 2, 64], BF16, "azb")
    k.dma(SELt[:], DI["sel"][:])
    k.memset(O[:], 0.0)
    sc1 = Scope(k)
    RAW = sc1.sb([128, 1024], F32, "araw")
    CV = sc1.sb([128, 1024], F32, "acv")
    XS = sc1.sb([128, 1024], F32, "axs")
    RAWv = AV(RAW.h[:, :].rearrange("p (s t) -> p s t", s=nseq), RAW)
    CVv = AV(CV.h[:, :].rearrange("p (s t) -> p s t", s=nseq), CV)

    def v3(t, a, b):
        return AV(t.h[:, :].rearrange("p (s t) -> p s t", s=nseq)[:, :, a:b], t)
    for c in range(6):
        s = wchunk(DI["win_c"][l, c])
        bs = M.pj(s, 0, 128)
        for tt in range(2):
            k.copy(RAW[:, TS[tt]], bs[tt][:], eng=(k.act if tt else k.dve))
        wc = lambda j: VEC[:, 48 + c * 5 + j:48 + c * 5 + j + 1]
        k.ts(CV[:], RAW[:], wc(2), ALU.mult)
        for j in (0, 1, 3, 4):
            o = j - 2
            t0, t1 = max(0, -o), Ls - max(0, o)
            k.stt(v3(CV, t0, t1), v3(RAW, t0 + o, t1 + o), wc(j), v3(CV, t0, t1), ALU.mult, ALU.add)
        k.activation(XS[:], CV[:], AF.Silu)
        kind, hp = c // 2, c % 2
        if kind < 2:
            dst = QT if kind == 0 else KT
            for tt in range(2):
                k.activation(M.SQ[0][:], XS[:, TS[tt]], AF.Square)
                b = nb()
                k.mm(b[:], BDb[:], M.SQ[0][:])
                k.activation(M.RT[:], b[:], AF.Sqrt, bias=1e-6, scale=1.0)
                k.recip(M.RS[:], M.RT[:])
                if kind == 0:
                    k.stt(dst[:, hp, TS[tt]], XS[:, TS[tt]], 0.125, M.RS[:], ALU.mult, ALU.mult)
                else:
                    k.tt(XS[:, TS[tt]], XS[:, TS[tt]], M.RS[:], ALU.mult)
                    k.copy(dst[:, hp, TS[tt]], XS[:, TS[tt]], eng=k.act)
        if kind >= 1:
            dstt = Kt if kind == 1 else Vt
            for i in range(0, 8, 4):
                b = nb()
                k.trq([(b[:, ii * 128:(ii + 1) * 128], XS[:, (i + ii) * 128:(i + ii + 1) * 128], IDf) for ii in range(4)])
                k.copy(dstt[:, i:i + 4, hp * 128:(hp + 1) * 128],
                       AV(b.h[:, :].rearrange("p (i n) -> p i n", i=4), b), eng=(k.act if i else k.dve))
    for hp in range(2):
        s = wchunk(DI["win_c"][l, 6 + hp])
        bs = M.pj(s, 0, 128)
        for tt in range(2):
            k.activation(GS[:, hp, TS[tt]], bs[tt][:], AF.Silu)
    sc1.close()
    if "A1" in M.dbg:
        k.memset(OT[:], 0.0)
        sc.close()
        return
    sc1 = Scope(k)
    DNR = sc1.sb([8, 2], F32, "adnr")
    NEGAr = sc1.sb([8, 1], F32, "anegar")
    MFB = sc1.sb([8, 2], F32, "amfb")
    DNP = sc1.sb([128, 16], F32, "adnp")
    NEGAc = sc1.sb([128, 8], F32, "anegac")
    RST = sc1.sb([8, 1024], F32, "arst")
    RA = sc1.sb([8, 1024], F32, "ara")
    RB = sc1.sb([8, 1024], F32, "arb")
    RC = sc1.sb([8, 1024], F32, "arc")
    TOT = sc1.sb([8, 8], F32, "atot")
    k.dma(DNR[:], DI["dnr"][l])
    k.dma(MFB[:], DI["mfb"][:])
    k.dma(DNP[:], DI["dnp"][l])
    k.activation(NEGAr[:], DNR[:, 1:2], AF.Exp)
    k.ts(NEGAr[:], NEGAr[:], -1.0, ALU.mult)
    k.activation(NEGAc[:], DNP[:, 8:16], AF.Exp)
    k.ts(NEGAc[:], NEGAc[:], -1.0, ALU.mult)
    k.memset(RST[:], 1.0)
    k.memset(AV(RST.h[:, 0:1024:128], RST), 0.0)
    s = wchunk(DI["win_c"][l, 8])
    ba = M.pj(s, 0, 8)
    bb_ = M.pj(s, 8, 8)
    for tt in range(2):
        k.activation(RA[:, TS[tt]], ba[tt][0:8, :], AF.Exp, bias=DNR[:, 0:1])
        k.activation(RB[:, TS[tt]], bb_[tt][0:8, :], AF.Exp, scale=-1.0)
    k.activation(RA[:], RA[:], AF.Ln, bias=1.0)
    k.ts(RA[:], RA[:], NEGAr[:, 0:1], ALU.mult)
    k.activation(ROWS[:, 1, :], RB[:], AF.Ln, bias=1.0)
    k.scan(RB[:], RST[:], RA[:], 0.0, ALU.mult, ALU.add)
    k.copy(TOT[:], AV(RB.h[:, 127:1024:128], RB))
    r3 = lambda t: AV(t.h[:, :].rearrange("p (c n) -> p c n", c=8), t)
    k.tt(r3(RC), r3(RA), r3(RB), ALU.subtract)
    k.tt(r3(RC), r3(RC), AV(TOT.h[:, :].unsqueeze(2).to_broadcast([8, 8, 128]), TOT), ALU.add)
    k.ts(ROWS[:, 2, :], RB[:], MFB[:, 0:1], ALU.mult)
    k.stt(ROWS[:, 2, :], RC[:], MFB[:, 1:2], ROWS[:, 2, :], ALU.mult, ALU.add)
    k.tt(ROWS[:, 1, :], ROWS[:, 2, :], ROWS[:, 1, :], ALU.subtract)
    k.ts(ROWS[:, 0, :], ROWS[:, 2, :], -1.0, ALU.mult)
    ABc = sc1.sb([128, 8, 16], F32, "aabc")
    TC = sc1.sb([128, 8, 8], F32, "atc")
    for i, b in M.proj_tm([8], 128):
        k.copy(ABc[:, i, :], b[:, 0:16], eng=(k.act if i % 2 else k.dve))
    k.tt(TC[:], ABc[:, :, 0:8], AV(DNP.h[:, 0:8].unsqueeze(1).to_broadcast([128, 8, 8]), DNP), ALU.add)
    k.activation(TC[:], TC[:], AF.Exp)
    k.activation(TC[:], TC[:], AF.Ln, bias=1.0)
    k.tt(COLS[:, cLA, :, :], TC[:], AV(NEGAc.h[:, :].unsqueeze(1).to_broadcast([128, 8, 8]), NEGAc), ALU.mult)
    k.activation(TC[:], ABc[:, :, 8:16], AF.Exp, scale=-1.0)
    k.activation(COLS[:, cL2, :, :], TC[:], AF.Ln, bias=1.0)
    bg = nb()
    its = []
    for i in range(8):
        its.append((bg[:, i * 16:i * 16 + 4], UPIf, COLS[:, cLA, i, 0:4]))
        its.append((bg[:, i * 16 + 4:i * 16 + 8], LOWIf, COLS[:, cLA, i, 4:8]))
        its.append((bg[:, i * 16 + 8:i * 16 + 16], ONESf[:], COLS[:, cLA, i, :]))
    k.mmq(its)
    bgv = AV(bg.h[:, 0:128].rearrange("p (i c) -> p i c", i=8), bg)
    k.copy(COLS[:, cG, :, :], AV(bgv.ap[:, :, 0:8], bg))
    k.tt(COLS[:, cGH, :, :], COLS[:, cG, :, :], COLS[:, cL2, :, :], ALU.subtract)
    k.activation(COLS[:, cBET, :, :], COLS[:, cL2, :, :], AF.Exp, scale=-1.0)
    k.activation(COLS[:, cEGH, :, :], COLS[:, cGH, :, :], AF.Exp)
    k.activation(COLS[:, cEG, :, :], COLS[:, cG, :, :], AF.Exp)
    k.tt(TC[:], AV(bgv.ap[:, :, 8:16], bg), COLS[:, cG, :, :], ALU.subtract)
    k.activation(COLS[:, cED, :, :], TC[:], AF.Exp)
    k.activation(COLS[:, cEGT, :, :], AV(bgv.ap[:, :, 8:16], bg), AF.Exp)
    sc1.close()
    if "A" in M.dbg and g == 0:
        P.dump("A_QT", QT[:], [128, 2, 1024], BF16)
        P.dump("A_KT", KT[:], [128, 2, 1024], BF16)
        P.dump("A_Kt", Kt[:], [128, 8, 256], BF16)
        P.dump("A_Vt", Vt[:], [128, 8, 256], BF16)
        P.dump("A_GS", GS[:], [128, 2, 1024], BF16)
        P.dump("A_ROWS", ROWS[:], [8, 3, 1024])
        P.dump("A_COLS", COLS[:], [128, 9, 8, 8])
    if "A2" in M.dbg:
        k.memset(OT[:], 0.0)
        sc.close()
        return
    sc2 = Scope(k)
    EX = sc2.sb([128, 3, 4, 128], F32, "aex")
    Dm = EX
    NBYs = [[sc2.sb([128, 4, 128], BF16, "anby") for _ in range(3)] for _ in range(2)]
    SVs = [[sc2.sb([128, 4, 128], BF16, "asv") for _ in range(11)] for _ in range(2)]
    QKT = [sc2.sb([128, 4, 128], BF16, "aqkt") for _ in range(2)]
    KDEC = [sc2.sb([128, 4, 64], BF16, "akdec") for _ in range(2)]
    U = [sc2.sb([128, 4, 64], F32, "au") for _ in range(2)]
    WT = [sc2.sb([128, 2, 128], BF16, "awt") for _ in range(2)]
    VNs = [sc2.sb([128, 4, 64], BF16, "avn") for _ in range(2)]
    T1s = [sc2.sb([128, 4, 64], F32, "at1") for _ in range(2)]
    if g == 1:
        k.dma(Zf[:], DI["zdn"][l])
        k.copy(Zb[:], Zf[:])

    def bc4(t, kind, n, r0, w):
        return AV(t.h[:, kind, n, r0:r0 + 4].unsqueeze(2).to_broadcast([128, 4, w]), t)

    def v4(t, n, w):
        return AV(t.h[:, n, :].rearrange("p (h d) -> p h d", h=4), t)

    def prep(d, n, slot):
        tok = slice(n * 128, (n + 1) * 128)
        r0 = d * 4
        ms, msT, miT = (0, 1, 3) if d == 0 else (1, 0, 2)
        bx = [nb(), nb(), nb()]
        for kind in range(3):
            k.mmq([(bx[kind][:, h * 128:(h + 1) * 128], SELt[:, r0 + h, :], ROWS[:, kind, tok]) for h in range(4)])
        for h in range(4):
            r = r0 + h
            hs = slice(h * 128, (h + 1) * 128)
            k.stt(EX[:, 0, h, :], bx[0][:, hs], COLS[:, cGH, n, r:r + 1], NM[:, ms, :], ALU.add, ALU.add)
            k.stt(EX[:, 1, h, :], bx[1][:, hs], COLS[:, cG, n, r:r + 1], NM[:, msT, :], ALU.subtract, ALU.add)
            k.stt(EX[:, 2, h, :], bx[2][:, hs], COLS[:, cG, n, r:r + 1], NM[:, miT, :], ALU.subtract, ALU.add)
        k.activation(Dm[:], EX[:], AF.Exp)
        bG, bQ = nb(), nb()
        HH = [(h, h // 2, (h % 2) * 64) for h in range(4)]
        k.mmq([(bG[:, h * 128:(h + 1) * 128], KT[hb:hb + 64, hp, tok], KT[hb:hb + 64, hp, tok]) for h, hp, hb in HH])
        k.mmq([(bQ[:, h * 128:(h + 1) * 128], KT[hb:hb + 64, hp, tok], QT[hb:hb + 64, hp, tok]) for h, hp, hb in HH])
        b4 = lambda b: AV(b.h[:, :].rearrange("p (h n) -> p h n", h=4), b)
        N0, B0, Y0 = NBYs[slot]
        k.stt(N0[:], b4(bG), -1.0, Dm[:, 0, :, :], ALU.mult, ALU.mult)
        k.stt(B0[:], b4(bG), -1.0, Dm[:, 1, :, :], ALU.mult, ALU.mult)
        k.tt(QKT[slot][:], b4(bQ), Dm[:, 2, :, :], ALU.mult)
        k.tt(Y0[:, :, 0:64], v4(Vt, n, 64), bc4(COLS, cBET, n, r0, 64), ALU.mult)
        k.tt(Y0[:, :, 64:128], v4(Kt, n, 64), bc4(COLS, cEGH, n, r0, 64), ALU.mult)
        k.tt(KDEC[slot][:], v4(Kt, n, 64), bc4(COLS, cED, n, r0, 64), ALU.mult)
        wts = [(WT[slot][(h % 2) * 64:(h % 2) * 64 + 64, h // 2, :], (h % 2) * 64) for h in range(4)]
        if "A" in M.dbg and g == 0 and n == 0 and d == 0:
            P.dump("A_N0", N0[:], [128, 4, 128], BF16)
            P.dump("A_B0", B0[:], [128, 4, 128], BF16)
            P.dump("A_Y0", Y0[:], [128, 4, 128], BF16)
            P.dump("A_QKT", QKT[slot][:], [128, 4, 128], BF16)
        return tri_solve_g(M, N0, B0, Y0, SVs[slot], 4, U[slot], wts)
        if "A" in M.dbg and g == 0 and n == 0 and d == 0:
            P.dump("A_U", U[slot][:], [128, 4, 64])
            P.dump("A_WT", WT[slot][:], [128, 2, 128], BF16)

    def step(d, n, slot):
        tok = slice(n * 128, (n + 1) * 128)
        r0 = d * 4
        pv = nb()
        HH = [(h, h // 2, (h % 2) * 64) for h in range(4)]
        k.mmq([(pv[:, h * 64:(h + 1) * 64], WT[slot][hb:hb + 64, hp, :], Zb[hb:hb + 64, d, hp, :]) for h, hp, hb in HH])
        p4 = lambda b: AV(b.h[:, 0:256].rearrange("p (h n) -> p h n", h=4), b)
        vn = VNs[slot]
        k.tt(vn[:], U[slot][:], p4(pv), ALU.subtract)
        yield
        po1, po2 = nb(), nb()
        k.mmq([(po1[:, h * 64:(h + 1) * 64], QT[hb:hb + 64, hp, tok], Zb[hb:hb + 64, d, hp, :]) for h, hp, hb in HH])
        k.mmq([(po2[:, h * 64:(h + 1) * 64], QKT[slot][:, h, :], vn[:, h, :]) for h, hp, hb in HH])
        t1 = T1s[slot]
        k.tt(t1[:], p4(po1), bc4(COLS, cEG, n, r0, 64), ALU.mult)
        k.tt(t1[:], t1[:], p4(po2), ALU.add)
        k.tt(v4(O, n, 64), v4(O, n, 64), t1[:], ALU.add)
        pz = nb()
        k.mmq([(pz[hb:hb + 64, hp * 64:(hp + 1) * 64], KDEC[slot][:, h, :], vn[:, h, :]) for h, hp, hb in HH])
        yield
        for h in range(4):
            hp, hb = h // 2, (h % 2) * 64
            r = r0 + h
            k.stt(Zf[hb:hb + 64, d, hp, :], Zf[hb:hb + 64, d, hp, :], COLS[hb:hb + 64, cEGT, n, r:r + 1],
                  pz[hb:hb + 64, hp * 64:(hp + 1) * 64], ALU.mult, ALU.add)
        k.copy(Zb[:, d, :, :], Zf[:, d, :, :], eng=k.act)

    cps = 8 // nseq
    for s_ in range(nseq):
        if g == 0:
            k.memset(Zf[:], 0.0)
            k.memset(Zb[:], 0.0)
        for i in range(cps):
            nf = s_ * cps + i
            nbk = s_ * cps + (cps - 1 - i)
            g0 = prep(0, nf, 0)
            g1 = prep(1, nbk, 1)
            interleave([g0, g1])
            interleave([step(0, nf, 0), step(1, nbk, 1)])
        if g == 0:
            for d in range(2):
                k.dma(DO["o_dn"][l, d, s_], Zf[:, d, :, :], is_out=True)
    sc2.close()
    OTF = sc.sb([128, 2, 1024], F32, "aotf")
    for i in range(8):
        b = nb()
        k.trq([(b[:, c * 128:(c + 1) * 128], O[:, i, c * 128:(c + 1) * 128], IDf) for c in range(2)])
        k.copy(OTF[:, :, i * 128:(i + 1) * 128], AV(b.h[:, 0:256].rearrange("p (c n) -> p c n", c=2), b),
               eng=(k.act if i % 2 else k.dve))
    for c in range(2):
        for tt in range(2):
            k.activation(M.SQ[0][:], OTF[:, c, TS[tt]], AF.Square)
            b = nb()
            k.mm(b[:], BDb[:], M.SQ[0][:])
            k.activation(M.RT[:], b[:], AF.Sqrt, bias=1e-6, scale=1.0 / 64)
            k.recip(M.RS[:], M.RT[:])
            k.stt(M.TMP[0][:], OTF[:, c, TS[tt]], VEC[:, 78:79], M.RS[:], ALU.mult, ALU.mult)
            k.tt(OT[:, c, TS[tt]], M.TMP[0][:], GS[:, c, TS[tt]], ALU.mult)
    sc.close()


def mixer_B(M):
    P, l, g, H1, OT = M.P, M.l, M.g, M.H1, M.OT
    k = P.k
    nb, wchunk, DI, DO, TS, bfh = P.nb, P.wchunk, P.DI, P.DO, P.TS, P.bfh
    C = P.consts
    IDb, BDb, IDf, BDf, CF = C["IDb"], C["BDb"], C["IDf"], C["BDf"], C["CF"]
    VEC = P.VEC
    nseq = 4 if g == 0 else 1
    Ls = 1024 // nseq
    cps = 8 // nseq
    sc = Scope(k)
    BON = sc.sb([128, 2, 1024], BF16, "bbon")
    Yacc = sc.sb([128, 8, 256], F32, "byacc")
    TW = sc.sb([128, 1024], BF16, "btw")
    AL = sc.sb([64, 1024], BF16, "bal")
    SGL = sc.sb([128, 1024], BF16, "bsgl")
    RWM = sc.sb([128, 3, 256], BF16, "brwm")
    Zf = sc.sb([128, 2, 2, 64], F32, "bzf")
    C0 = sc.sb([128, 9], F32, "bc0")
    OMK = sc.sb([128, 2], F32, "bomk")
    RST = sc.sb([128, 1024], BF16, "brst")
    RAWh = [None]
    k.dma(RWM[:], DI["rwm"][l], eng=k.pool)
    k.memset(Yacc[:], 0.0)
    k.memset(RST[:], 1.0)
    k.memset(AV(RST.h[:, 0:1024:128], RST), 0.0)
    mu = lambda i, q: VEC[:, 79 + 2 * i + q:79 + 2 * i + q + 1]
    k.ts(C0[:], AV(VEC.h[:, 79:97:2], VEC), -1.0, ALU.mult, 1.0, ALU.add)
    k.tt(C0[:], C0[:], AV(VEC.h[:, 80:98:2], VEC), ALU.subtract)
    k.ts(OMK[:], VEC[:, 105:107], -1.0, ALU.mult, 1.0, ALU.add)
    if g == 1:
        k.dma(Zf[:], DI["zrw"][l])

    def v3(t, a, b, rows=128):
        return AV(t.h[0:rows, :].rearrange("p (s t) -> p s t", s=nseq)[:, :, a:b], t)

    def proj_shift(ci, rows, dst):
        i = ci - 9
        s = wchunk(DI["win_c"][l, ci])
        bs = M.pj(s, 0, rows)
        RAW = RAWh[0]
        for tt in range(2):
            k.copy(RAW[0:rows, TS[tt]], bs[tt][0:rows, :], eng=(k.act if tt else k.dve))
        k.ts(dst, RAW[0:rows, :], C0[0:rows, i:i + 1], ALU.mult)
        dt = dst.t
        k.stt(v3(dt, 1, Ls, rows), v3(RAW, 0, Ls - 1, rows), AV(mu(i, 0).ap[0:rows], VEC), v3(dt, 1, Ls, rows), ALU.mult, ALU.add)
        k.stt(v3(dt, 0, Ls - 1, rows), v3(RAW, 1, Ls, rows), AV(mu(i, 1).ap[0:rows], VEC), v3(dt, 0, Ls - 1, rows), ALU.mult, ALU.add)
    scs = Scope(k)
    RAWh[0] = scs.sb([128, 1024], F32, "braw")
    SH = scs.sb([128, 1024], F32, "bsh")
    proj_shift(15, 128, SH[:])
    k.activation(TW[:], SH[:], AF.Tanh)
    proj_shift(16, 64, SH[0:64, :])
    k.copy(AL[:], SH[0:64, :])
    proj_shift(17, 128, SH[:])
    k.activation(SGL[:], SH[:], AF.Sigmoid)
    scs.close()
    for hp in range(2):
        sch = Scope(k)
        R = sch.sb([128, 1024], F32, "br")
        Kx = sch.sb([128, 1024], F32, "bk")
        A_ = sch.sb([128, 1024], F32, "ba")
        KK = sch.sb([128, 1024], F32, "bkk")
        VTt = sch.sb([128, 8, 128], BF16, "bvtt")
        scv = Scope(k)
        RAWh[0] = scv.sb([128, 1024], F32, "braw")
        V = scv.sb([128, 1024], F32, "bv")
        proj_shift(9 + hp, 128, R[:])
        proj_shift(11 + hp, 128, Kx[:])
        proj_shift(13 + hp, 128, V[:])
        for tt in range(2):
            b = nb()
            k.mm(b[:], RWM[0:64, 1, hp * 128:(hp + 1) * 128], AL[0:64, TS[tt]])
            k.activation(A_[:, TS[tt]], b[:], AF.Sigmoid, bias=VEC[:, 101 + hp:102 + hp])
        k.ts(KK[:], Kx[:], VEC[:, 103 + hp:104 + hp], ALU.mult)
        for tt in range(2):
            k.activation(M.SQ[0][:], KK[:, TS[tt]], AF.Square)
            b = nb()
            k.mm(b[:], BDb[:], M.SQ[0][:])
            k.activation(M.RT[:], b[:], AF.Sqrt, bias=1e-6, scale=1.0)
            k.recip(M.RS[:], M.RT[:])
            k.tt(KK[:, TS[tt]], KK[:, TS[tt]], M.RS[:], ALU.mult)
        for tt in range(2):
            k.ts(M.TMP[0][:], A_[:, TS[tt]], VEC[:, 105 + hp:106 + hp], ALU.mult, OMK[:, hp:hp + 1], ALU.add)
            k.tt(Kx[:, TS[tt]], Kx[:, TS[tt]], M.TMP[0][:], ALU.mult)
            k.tt(A_[:, TS[tt]], KK[:, TS[tt]], A_[:, TS[tt]], ALU.mult)
            k.stt(M.SQ[1][:], R[:, TS[tt]], VEC[:, 107 + hp:108 + hp], Kx[:, TS[tt]], ALU.mult, ALU.mult)
            b = nb()
            k.mm(b[:], BDb[:], M.SQ[1][:])
            k.tt(BON[:, hp, TS[tt]], b[:], V[:, TS[tt]], ALU.mult)
        for i0 in range(0, 8, 4):
            b = nb()
            k.trq([(b[:, ii * 128:(ii + 1) * 128], V[:, (i0 + ii) * 128:(i0 + ii + 1) * 128], IDf) for ii in range(4)])
            k.copy(VTt[:, i0:i0 + 4, :], AV(b.h[:, :].rearrange("p (i n) -> p i n", i=4), b), eng=(k.act if i0 else k.dve))
        scv.close()
        for d in range(2):
            scd = Scope(k)
            RT_ = scd.sb([128, 1024], BF16, "brt")
            KT_ = scd.sb([128, 1024], BF16, "bkt")
            BT_ = scd.sb([128, 1024], BF16, "bbt")
            CT_ = scd.sb([128, 1024], BF16, "bct")
            KTt = scd.sb([128, 8, 128], BF16, "bktt")
            BTt = scd.sb([128, 8, 128], BF16, "bbtt")
            REF = scd.sb([128, 8], F32, "bref")
            NB1 = scd.sb([128, 8], F32, "bnb1")
            NB2 = scd.sb([128, 8], F32, "bnb2")
            ER = scd.sb([128, 8], F32, "ber")
            EEND = scd.sb([128, 8], F32, "beend")
            TOT = scd.sb([128, 8], F32, "btot")
            sce = Scope(k)
            SIG = sce.sb([128, 1024], F32, "bsig")
            CUM = sce.sb([128, 1024], F32, "bcum")
            EB = sce.sb([128, 1024], F32, "beb")
            for tt in range(2):
                b = nb()
                k.mm(b[:], RWM[d * 64:(d + 1) * 64, 0, hp * 128:(hp + 1) * 128], TW[d * 64:(d + 1) * 64, TS[tt]])
                k.activation(SIG[:, TS[tt]], b[:], AF.Sigmoid, bias=VEC[:, 97 + d * 2 + hp:98 + d * 2 + hp])
            k.scan(CUM[:], RST[:], SIG[:], 0.0, ALU.mult, ALU.add)
            c3 = lambda t: AV(t.h[:, :].rearrange("p (c n) -> p c n", c=8), t)
            if d == 1:
                k.copy(TOT[:], AV(CUM.h[:, 127:1024:128], CUM))
                k.tt(c3(CUM), c3(SIG), c3(CUM), ALU.subtract)
                k.tt(c3(CUM), c3(CUM), AV(TOT.h[:, :].unsqueeze(2).to_broadcast([128, 8, 128]), TOT), ALU.add)
            k.tt(SIG[:], CUM[:], SIG[:], ALU.subtract)
            k.copy(REF[:], AV(CUM.h[:, 64:1024:128], CUM))
            k.ts(NB1[:], REF[:], WC, ALU.mult)
            k.ts(NB2[:], REF[:], -WC, ALU.mult)
            k.activation(ER[:], REF[:], AF.Exp, scale=-WC)
            cs = lambda n: slice(n * 128, (n + 1) * 128)
            REFb = AV(REF.h[:, :].unsqueeze(2).to_broadcast([128, 8, 128]), REF)
            k.tt(c3(CUM), c3(CUM), REFb, ALU.subtract)
            k.tt(c3(SIG), c3(SIG), REFb, ALU.subtract)
            k.activation(EB[:], CUM[:], AF.Exp, scale=-WC)
            k.tt(RT_[:], R[:], EB[:], ALU.mult)
            endc = 127 if d == 0 else 0
            k.copy(EEND[:], AV(EB.h[:, endc:1024:128], EB))
            k.activation(EB[:], CUM[:], AF.Exp, scale=WC)
            k.tt(KT_[:], Kx[:], EB[:], ALU.mult)
            k.tt(BT_[:], A_[:], EB[:], ALU.mult)
            k.activation(EB[:], SIG[:], AF.Exp, scale=-WC)
            k.tt(CT_[:], KK[:], EB[:], ALU.mult)
            for (src_, dst_, neg) in ((KT_, KTt, False), (BT_, BTt, True)):
                for i0 in range(0, 8, 4):
                    b = nb()
                    bh_ = bfh(b)
                    k.trq([(AV(bh_[:, ii * 128:(ii + 1) * 128], b), src_[:, (i0 + ii) * 128:(i0 + ii + 1) * 128], IDb[:]) for ii in range(4)])
                    srcv = AV(bh_[:, 0:512].rearrange("p (i n) -> p i n", i=4), b)
                    if neg:
                        k.ts(dst_[:, i0:i0 + 4, :], srcv, -1.0, ALU.mult)
                    else:
                        k.copy(dst_[:, i0:i0 + 4, :], srcv, eng=k.act)
            sce.close()
            scc = Scope(k)
            NBY = [scc.sb([128, 4, 128], BF16, "bnby") for _ in range(3)]
            SV = [scc.sb([128, 4, 128], BF16, "bsv") for _ in range(11)]
            LkT = scc.sb([128, 4, 128], BF16, "blkt")
            G1Ts = [scc.sb([128, 4, 128], BF16, "bg1t") for _ in range(2)]
            G2Ts = [scc.sb([128, 4, 128], BF16, "bg2t") for _ in range(2)]
            Us = [scc.sb([128, 4, 64], F32, "bu") for _ in range(2)]
            WTs = [scc.sb([128, 2, 128], BF16, "bwt") for _ in range(2)]
            Pm = scc.sb([128, 2, 64], BF16, "bpm")
            Z0f = scc.sb([128, 64], F32, "bz0f")
            Z0b = scc.sb([128, 64], BF16, "bz0b")
            ms, msT, miT = (1, 2, 4) if d == 0 else (2, 1, 3)
            mkk = lambda i: AV(CF.h[:, i, :].unsqueeze(1).to_broadcast([128, 4, 128]), CF)
            b4 = lambda b: AV(b.h[:, 0:512].rearrange("p (q n) -> p q n", q=4), b)
            work = []
            for s_ in range(nseq):
                for i in range(0, cps, 2):
                    pair = [s_ * cps + (ii if d == 0 else cps - 1 - ii) for ii in (i, i + 1)]
                    work.append((s_, pair, i == 0, i + 2 >= cps))

            def prepsolve(wi):
                s_, pair, first, last_ = work[wi]
                par = wi % 2
                G1T, G2T, U, WT = G1Ts[par], G2Ts[par], Us[par], WTs[par]
                N0, B0, Y0 = NBY
                bN, bB, bL, bG1, bG2 = nb(), nb(), nb(), nb(), nb()
                PQ = [(ci * 2 + q, cs(n), q * 64) for ci, n in enumerate(pair) for q in range(2)]
                sl_ = lambda t, tok, hb: t[hb:hb + 64, tok]
                for bk_, ta, tb in ((bN, CT_, BT_), (bB, BT_, CT_), (bL, KT_, CT_), (bG1, KT_, RT_), (bG2, BT_, RT_)):
                    k.mmq([(bk_[:, p_ * 128:(p_ + 1) * 128], sl_(ta, tok, hb), sl_(tb, tok, hb)) for p_, tok, hb in PQ])
                k.stt(N0[:], b4(bN), -1.0, mkk(ms), ALU.mult, ALU.mult)
                k.stt(B0[:], b4(bB), -1.0, mkk(msT), ALU.mult, ALU.mult)
                k.tt(LkT[:], b4(bL), mkk(msT), ALU.mult)
                yield
                k.tt(G1T[:], b4(bG1), mkk(miT), ALU.mult)
                k.stt(G2T[:], b4(bG2), -1.0, mkk(miT), ALU.mult, ALU.mult)
                bt = nb()
                bth = bfh(bt)
                k.trq([(AV(bth[:, ci * 128:(ci + 1) * 128], bt), CT_[:, cs(n)], IDb[:]) for ci, n in enumerate(pair)])
                k.copy(Y0[:, :, 64:128], AV(bth[:, 0:256].rearrange("p (q n) -> p q n", q=4), bt), eng=k.act)
                by = nb()
                k.mmq([(by[:, (ci * 2 + q) * 64:(ci * 2 + q + 1) * 64], LkT[:, ci * 2 + q, :], VTt[:, n, q * 64:(q + 1) * 64])
                       for ci, n in enumerate(pair) for q in range(2)])
                k.copy(Y0[:, :, 0:64], AV(by.h[:, 0:256].rearrange("p (q n) -> p q n", q=4), by))
                yield
                wts = [(WT[(p_ % 2) * 64:(p_ % 2) * 64 + 64, p_ // 2, :], (p_ % 2) * 64) for p_ in range(4)]
                yield from tri_solve_g(M, N0, B0, Y0, SV, 4, U, wts)

            def seqpart(wi):
                s_, pair, first, last_ = work[wi]
                par = wi % 2
                G1T, G2T, U, WT = G1Ts[par], G2Ts[par], Us[par], WTs[par]
                if g == 0 and first:
                    k.memset(Zf[:, d, hp, :], 0.0)
                for ci, n in enumerate(pair):
                    tok = cs(n)
                    k.ts(Z0f[:], Zf[:, d, hp, :], ER[:, n:n + 1], ALU.mult)
                    k.copy(Z0b[:], Z0f[:], eng=k.act)
                    bp = nb()
                    k.mmq([(bp[:, q * 64:(q + 1) * 64], WT[q * 64:q * 64 + 64, ci, :], Z0b[q * 64:q * 64 + 64, :]) for q in range(2)])
                    yield
                    k.tt(Pm[:], U[:, ci * 2:ci * 2 + 2, :], AV(bp.h[:, 0:128].rearrange("p (q n) -> p q n", q=2), bp), ALU.add)
                    bo = nb()
                    for q in range(2):
                        hb = q * 64
                        p_ = ci * 2 + q
                        o = bo[:, q * 64:(q + 1) * 64]
                        k.mm1(o, RT_[hb:hb + 64, tok], Z0b[hb:hb + 64, :], True, False)
                        k.mm1(o, G1T[:, p_, :], VTt[:, n, q * 64:(q + 1) * 64], False, False)
                        k.mm1(o, G2T[:, p_, :], Pm[:, q, :], False, True)
                    k.tt(Yacc[:, n, hp * 128:(hp + 1) * 128], Yacc[:, n, hp * 128:(hp + 1) * 128], bo[:, 0:128], ALU.add)
                    bz = nb()
                    for q in range(2):
                        hb = q * 64
                        o = bz[hb:hb + 64, 0:64]
                        k.mm1(o, KTt[:, n, q * 64:(q + 1) * 64], VTt[:, n, q * 64:(q + 1) * 64], True, False)
                        k.mm1(o, BTt[:, n, q * 64:(q + 1) * 64], Pm[:, q, :], False, True)
                    yield
                    k.tt(Z0f[:], Z0f[:], bz[:, 0:64], ALU.add)
                    k.ts(Zf[:, d, hp, :], Z0f[:], EEND[:, n:n + 1], ALU.mult)
                    yield
                if g == 0 and last_:
                    k.dma(DO["o_rw"][l, d, s_, :, hp, :], Zf[:, d, hp, :], is_out=True)

            interleave([prepsolve(0)])
            for wi in range(len(work)):
                gl_ = [seqpart(wi)]
                if wi + 1 < len(work):
                    gl_.append(prepsolve(wi + 1))
                interleave(gl_)
            scc.close()
            scd.close()
        sch.close()
    OTF = sc.sb([128, 2, 1024], F32, "botf")
    for i in range(8):
        b = nb()
        k.trq([(b[:, c * 128:(c + 1) * 128], Yacc[:, i, c * 128:(c + 1) * 128], IDf) for c in range(2)])
        k.copy(OTF[:, :, i * 128:(i + 1) * 128], AV(b.h[:, 0:256].rearrange("p (c n) -> p c n", c=2), b),
               eng=(k.act if i % 2 else k.dve))
    MEAN = sc.sb([128, 512], F32, "bmean")
    for hp in range(2):
        for tt in range(2):
            y = OTF[:, hp, TS[tt]]
            b = nb()
            k.mm(b[:], BDf, y)
            k.ts(MEAN[:], b[:], 1.0 / 64, ALU.mult)
            k.tt(y, y, MEAN[:], ALU.subtract)
            k.activation(M.TMP[0][:], y, AF.Square)
            b = nb()
            k.mm(b[:], BDf, M.TMP[0][:])
            k.activation(M.RT[:], b[:], AF.Sqrt, bias=64e-5, scale=1.0 / 64)
            k.recip(M.RS[:], M.RT[:])
            k.stt(M.TMP[0][:], y, VEC[:, 109 + hp:110 + hp], M.RS[:], ALU.mult, ALU.mult)
            k.ts(M.TMP[0][:], M.TMP[0][:], VEC[:, 111 + hp:112 + hp], ALU.add)
            k.tt(M.TMP[0][:], M.TMP[0][:], BON[:, hp, TS[tt]], ALU.add)
            b = nb()
            k.mm(b[:], RWM[:, 2, hp * 128:(hp + 1) * 128], SGL[:, TS[tt]])
            k.tt(OT[:, hp, TS[tt]], M.TMP[0][:], b[:], ALU.mult)
    sc.close()


def mixer_C(M):
    P, l, g, H1, OT = M.P, M.l, M.g, M.H1, M.OT
    k = P.k
    nb, wchunk, DI, DO, TS, bfh = P.nb, P.wchunk, P.DI, P.DO, P.TS, P.bfh
    IDb, BDb, IDf = P.consts["IDb"], P.consts["BDb"], P.consts["IDf"]
    VEC = P.VEC
    lam_init = 0.8 - 0.6 * math.exp(-0.3 * l)
    SCL = 32 ** -0.5
    sc = Scope(k)
    QT = sc.sb([64, 4, 1024], BF16, "cq")
    KT = sc.sb([64, 4, 1024], BF16, "ck")
    NLA = sc.sb([128, 1], F32, "nla")
    DNC = sc.sb([128, 1], F32, "dnc")
    V = sc.sb([128, 8, 256], BF16, "cv")
    O = sc.sb([128, 8, 256], F32, "co")
    sc_outer = sc
    sc = Scope(k)
    if g == 0:
        KF = [sc.sb([64, 512], F32, "ckf") for _ in range(2)]
        VF = [sc.sb([128, 256], F32, "cvf") for _ in range(2)]
    LAM = sc.sb([128, 128], F32, "lam")
    PR = sc.sb([128, 2, 32], F32, "lpr")
    S2 = sc.sb([128, 2], F32, "ls2")
    E2 = sc.sb([128, 2], F32, "le2")
    k.dma(LAM[:], DI["lamv"][l])
    k.tt(PR[:, 0, :], LAM[:, 0:32], LAM[:, 32:64], ALU.mult)
    k.tt(PR[:, 1, :], LAM[:, 64:96], LAM[:, 96:128], ALU.mult)
    k.reduce(S2[:], PR[:], ALU.add)
    k.activation(E2[:], S2[:], AF.Exp)
    k.tt(NLA[:], E2[:, 1:2], E2[:, 0:1], ALU.subtract)
    k.ts(NLA[:], NLA[:], -lam_init, ALU.add)
    k.ts(DNC[:], VEC[:, 113:114], 1.0 - lam_init, ALU.mult)
    if g == 1:
        ROPE = sc.sb([64, 2, 1024], F32, "rope")
        k.dma(ROPE[:], DI["rope"][:])
        R1 = [sc.sb([64, 512], F32, "r1") for _ in range(2)]
        R2 = [sc.sb([64, 512], F32, "r2") for _ in range(2)]
    for qk in range(2):
        dst = QT if qk == 0 else KT
        for hp in range(2):
            sx = wchunk(DI["win_c"][l, 18 + 2 * qk + hp])
            if g == 1:
                sy = wchunk(DI["win_c"][l, 24 + 2 * qk + hp])
            for hh in range(2):
                h = 2 * hp + hh
                bx = M.pj(sx, hh * 64, 64)
                if g == 1:
                    by = M.pj(sy, hh * 64, 64)
                for tt in range(2):
                    if g == 0:
                        if qk == 0:
                            k.copy(dst[:, h, TS[tt]], bx[tt][0:64, :], eng=k.act)
                        else:
                            k.copy(KF[tt][:], bx[tt][0:64, :])
                            k.dma(DO["o_dk"][l, h, :, TS[tt]], KF[tt][:], is_out=True)
                            k.copy(dst[:, h, TS[tt]], KF[tt][:], eng=k.act)
                    else:
                        k.tt(R1[tt][:], bx[tt][0:64, :], ROPE[:, 0, TS[tt]], ALU.mult)
                        k.tt(R2[tt][:], by[tt][0:64, :], ROPE[:, 1, TS[tt]], ALU.mult)
                        k.tt(dst[:, h, TS[tt]], R1[tt][:], R2[tt][:], ALU.add)
    for i, b in M.proj_tm([22, 23], 128):
        if g == 0:
            k.copy(VF[i % 2][:], b[:, 0:256])
            k.dma(DO["o_dv"][l, :, i, :], VF[i % 2][:], is_out=True)
            k.copy(V[:, i, :], VF[i % 2][:], eng=k.act)
        else:
            k.copy(V[:, i, :], b[:, 0:256], eng=(k.act if i % 2 else k.dve))
    sc.close()
    sc = Scope(k)
    MX = sc.sb([128, 4], F32, "cmx")
    MX1 = sc.sb([128, 2], F32, "cmx1")
    NMX = sc.sb([128, 2], F32, "cnmx")
    SUM = sc.sb([128, 2, 4], F32, "csum")
    SUM1 = sc.sb([128, 2], F32, "csum1")
    RS2 = sc.sb([128, 2], F32, "crs")
    C1 = sc.sb([128, 1], F32, "cc1")
    if g == 0:
        NKEY = 256
    else:
        NKEY = 1280
        KcT = sc.sb([64, 4, 256], BF16, "ckc")
        Vc = sc.sb([128, 2, 256], BF16, "cvc")
        k.dma(KcT[:], DI["cdk"][l], eng=k.pool)
        k.dma(Vc[:], DI["cdv"][l], eng=k.pool)
    NCH = NKEY // 128
    Pm = [sc.sb([128, 2, NKEY], BF16, "cp") for _ in range(2)]
    W = [sc.sb([128, NKEY], BF16, "cw")]
    PT = [sc.sb([128, NCH, 128], BF16, "cpt")]
    SUM1s = [sc.sb([128, 2], F32, "csum1b") for _ in range(2)]
    units = []
    u = 0
    for ti in range(8):
        for h in range(4):
            def pa(u=u, ti=ti, h=h):
                qsl = slice(ti * 128, (ti + 1) * 128)
                s_ = ti // 2
                pm = Pm[u % 2]
                for m in range(2):
                    q = QT[m * 32:(m + 1) * 32, h, qsl]
                    if g == 0:
                        b = nb()
                        segs = [(b, 0, 256, KT[m * 32:(m + 1) * 32, h, s_ * 256:(s_ + 1) * 256])]
                    else:
                        ba, bb, bc = nb(), nb(), nb()
                        segs = [(ba, 0, 512, KT[m * 32:(m + 1) * 32, h, 0:512]),
                                (bb, 512, 512, KT[m * 32:(m + 1) * 32, h, 512:1024]),
                                (bc, 1024, 256, KcT[m * 32:(m + 1) * 32, h, :])]
                    for si, (bk_, off, n, kop) in enumerate(segs):
                        k.mm(bk_[:, 0:n], q, kop)
                        k.reduce(MX[:, si:si + 1], bk_[:, 0:n], ALU.max)
                    if len(segs) > 1:
                        k.reduce(MX1[:, m:m + 1], MX[:, 0:len(segs)], ALU.max)
                        k.ts(NMX[:, m:m + 1], MX1[:, m:m + 1], -SCL, ALU.mult)
                    else:
                        k.ts(NMX[:, m:m + 1], MX[:, 0:1], -SCL, ALU.mult)
                    for si, (bk_, off, n, kop) in enumerate(segs):
                        k.activation(pm[:, m, off:off + n], bk_[:, 0:n], AF.Exp, bias=NMX[:, m:m + 1], scale=SCL,
                                     accum_out=SUM[:, m, si:si + 1])
                    if len(segs) > 1:
                        k.reduce(SUM1s[u % 2][:, m:m + 1], SUM[:, m, 0:len(segs)], ALU.add)
                    else:
                        k.copy(SUM1s[u % 2][:, m:m + 1], SUM[:, m, 0:1])

            def pb_(u=u, ti=ti, h=h):
                s_ = ti // 2
                pm = Pm[u % 2]
                k.recip(RS2[:], SUM1s[u % 2][:])
                k.tt(C1[:], RS2[:, 1:2], NLA[:], ALU.mult)
                w = W[0]
                k.ts(w[:], pm[:, 1, :], C1[:, 0:1], ALU.mult)
                k.stt(w[:], pm[:, 0, :], RS2[:, 0:1], w[:], ALU.mult, ALU.add)
                pt = PT[0]
                for c0 in range(0, NCH, 8):
                    nchunk = min(8, NCH - c0)
                    bt = nb()
                    bth = bfh(bt)
                    k.trq([(AV(bth[:, c * 128:(c + 1) * 128], bt), w[:, (c0 + c) * 128:(c0 + c + 1) * 128], IDb[:]) for c in range(nchunk)])
                    k.copy(pt[:, c0:c0 + nchunk, :], AV(bth[:, 0:nchunk * 128].rearrange("p (j n) -> p j n", j=nchunk), bt),
                           eng=(k.act if (c0 // 8) % 2 == 0 else k.dve))
                po = nb()
                if g == 0:
                    prs = [(pt[:, c, :], V[:, 2 * s_ + c, h * 64:(h + 1) * 64]) for c in range(2)]
                else:
                    prs = [(pt[:, c, :], V[:, c, h * 64:(h + 1) * 64]) for c in range(8)] + \
                          [(pt[:, 8 + c, :], Vc[:, c, h * 64:(h + 1) * 64]) for c in range(2)]
                k.mmg(po[:, 0:64], prs)
                k.copy(O[:, ti, h * 64:(h + 1) * 64], po[:, 0:64], eng=k.act)
            units.append((pa, pb_))
            u += 1
    skewed(units)
    sc.close()
    sc = sc_outer
    OTF = sc.sb([128, 2, 1024], F32, "cotf")
    for i in range(8):
        b = nb()
        k.trq([(b[:, c * 128:(c + 1) * 128], O[:, i, c * 128:(c + 1) * 128], IDf) for c in range(2)])
        k.copy(OTF[:, :, i * 128:(i + 1) * 128], AV(b.h[:, 0:256].rearrange("p (c n) -> p c n", c=2), b),
               eng=(k.act if i % 2 else k.dve))
    for c in range(2):
        for tt in range(2):
            k.activation(M.SQ[0][:], OTF[:, c, TS[tt]], AF.Square)
            b = nb()
            k.mm(b[:], BDb[:], M.SQ[0][:])
            k.activation(M.RT[:], b[:], AF.Sqrt, bias=1e-6, scale=1.0 / 64)
            k.recip(M.RS[:], M.RT[:])
            k.stt(OT[:, c, TS[tt]], OTF[:, c, TS[tt]], DNC[:, 0:1], M.RS[:], ALU.mult, ALU.mult)
    sc.close()


def assemble(outs):
    f32 = np.float32
    y_prompt = np.zeros((32, 256, 1024), f32)
    y_sample = np.zeros((4, 1024, 1024), f32)
    ndk = np.zeros((32, L_, 4, 256, 2, 32), f32)
    ndv = np.zeros((32, L_, 4, 256, 64), f32)
    nnk = np.zeros((32, L_, 4, 256, 64), f32)
    nnv = np.zeros((32, L_, 4, 256, 64), f32)
    dnf = np.zeros((32, L_, 4, 64, 64), f32)
    dnb = np.zeros((32, L_, 4, 64, 64), f32)
    rwf = np.zeros((32, L_, 4, 64, 64), f32)
    rwb = np.zeros((32, L_, 4, 64, 64), f32)
    for c, o in enumerate(outs):
        y = np.asarray(o["y_out"])
        sl = slice(4 * c, 4 * c + 4)
        y_prompt[sl] = y[:, :, :1024].transpose(2, 1, 0).reshape(4, 256, 1024)
        if c < 4:
            y_sample[c] = y[:, :, 1024:].transpose(2, 1, 0).reshape(1024, 1024)
        dk = np.asarray(o["o_dk"]).reshape(L_, 4, 2, 32, 4, 256)
        ndk[sl] = dk.transpose(4, 0, 1, 5, 2, 3)
        for name, dst in (("o_dv", ndv), ("o_nv", nnv)):
            v = np.asarray(o[name]).transpose(0, 2, 1, 3).reshape(L_, 4, 256, 4, 64)
            dst[sl] = v.transpose(1, 0, 3, 2, 4)
        nk = np.asarray(o["o_nk"]).reshape(L_, 2, 64, 2, 4, 256)
        nnk[sl] = nk.transpose(4, 0, 3, 1, 5, 2).reshape(4, L_, 4, 256, 64)
        dn = np.asarray(o["o_dn"]).reshape(L_, 2, 4, 2, 64, 2, 64)
        dn = dn.transpose(1, 2, 0, 5, 3, 4, 6).reshape(2, 4, L_, 4, 64, 64)
        dnf[sl], dnb[sl] = dn[0], dn[1]
        rw = np.asarray(o["o_rw"]).reshape(L_, 2, 4, 2, 64, 2, 64)
        rw = rw.transpose(1, 2, 0, 5, 3, 6, 4).reshape(2, 4, L_, 4, 64, 64)
        rwf[sl], rwb[sl] = rw[0], rw[1]
    return (y_prompt, y_sample, ndk, ndv, nnk, nnv, dnf, dnb, rwf, rwb)


def kernel(**inputs):
    inp = {k_: np.asarray(v, dtype=np.float32) for k_, v in inputs.items()}
    sh = prep_shared(inp)
    in_maps = []
    for c in range(8):
        d = dict(sh)
        d.update(prep_core(inp, c))
        in_maps.append(d)
    P = build({})
    res = run_bass_kernel_spmd(P.k.nc, in_maps, core_ids=list(range(8)))
    return assemble(res.results)
```
